# Optimizing a Trainium2 kernel written in Bass

```python
import jax, jax.numpy as jnp
from jax import lax
import numpy as np

D_MODEL = 2048
BATCH = 1
SEQ = 8192
DEPTH = 4
DEC_BATCH = 32
DEC_SEQ = 64
PAST_LEN = 2048

CHUNK = 64
N_MIXERS = 2
N_CONV_LAYERS = (DEPTH + 1) // 2
N_RWKV_LAYERS = DEPTH // 2
N_VRES_LAYERS = max(N_RWKV_LAYERS - 1, 0)
CONV_WIDTH = 31
FFN_CONV_WIDTH = 3
D_FF = 5632
HEAD_SIZE = 64
N_HEADS = D_MODEL // HEAD_SIZE
D_DECAY_LORA = 96
D_AAA_LORA = 96
D_MV_LORA = 64
D_GATE_LORA = 256
RMS_EPS = 1e-6
LN_EPS = 1e-5
GN_EPS = 64e-5

kernel_name = 'conformer_rwkv7_convglu_stream_step'


def rmsnorm(x, g):
    xf = x.astype(jnp.float32)
    y = xf * lax.rsqrt(jnp.mean(xf * xf, axis=-1, keepdims=True) + RMS_EPS)
    return (y * g.astype(jnp.float32)).astype(x.dtype)


def layernorm(x, g, b):
    xf = x.astype(jnp.float32)
    mu = jnp.mean(xf, axis=-1, keepdims=True)
    var = jnp.mean(jnp.square(xf - mu), axis=-1, keepdims=True)
    y = (xf - mu) * lax.rsqrt(var + LN_EPS)
    return (y * g.astype(jnp.float32) + b.astype(jnp.float32)).astype(x.dtype)


def causal_dwconv(x, buf, w, b):
    k = w.shape[0]
    xp = jnp.concatenate([buf.astype(x.dtype), x], axis=1)
    y = lax.conv_general_dilated(xp, w[:, None, :].astype(x.dtype), window_strides=(1,), padding='VALID',
                                 dimension_numbers=('NWC', 'WIO', 'NWC'), feature_group_count=x.shape[-1])
    return y + b.astype(x.dtype), xp[:, xp.shape[1] - (k - 1):]


def conformer_conv(h, buf, pw1_w, pw1_b, dw_w, dw_b, ln_g, ln_b, pw2_w, pw2_b):
    a = h @ pw1_w + pw1_b
    u = a[..., :D_MODEL] * jax.nn.sigmoid(a[..., D_MODEL:])
    c, new_buf = causal_dwconv(u, buf, dw_w, dw_b)
    c = layernorm(c, ln_g, ln_b)
    c = c * jax.nn.sigmoid(c)
    return c @ pw2_w + pw2_b, new_buf


def wkv7_scan(r, decay, k, v, kk, a, s0):
    def step(s, inp):
        r_t, w_t, k_t, v_t, kk_t, a_t = inp
        sa = jnp.einsum('bhij,bhj->bhi', s, kk_t)
        s = (s * w_t[:, :, None, :] - sa[..., None] * (kk_t * a_t)[:, :, None, :]
             + v_t[..., None] * k_t[:, :, None, :])
        return s, jnp.einsum('bhij,bhj->bhi', s, r_t)
    xs = tuple(jnp.moveaxis(t, 1, 0) for t in (r, decay, k, v, kk, a))
    s, y = lax.scan(step, s0, xs)
    return jnp.moveaxis(y, 0, 1), s


def rwkv7_time_mix(h, shift, s0, v_first, vres, mix, w_r, w_k, w_v, w_o, w0, w1, w2, a0, a1, a2,
                   g1, g2, k_k, k_a, r_k, ln_g, ln_b):
    b, t, c = h.shape
    prev = jnp.concatenate([shift[:, None, :].astype(h.dtype), h[:, :-1]], axis=1)
    xx = prev - h
    xr = h + xx * mix[0]
    xw = h + xx * mix[1]
    xk = h + xx * mix[2]
    xv = h + xx * mix[3]
    xa = h + xx * mix[4]
    xg = h + xx * mix[5]
    r = xr @ w_r
    k = xk @ w_k
    v = xv @ w_v
    w = -jax.nn.softplus(-(w0 + jnp.tanh(xw @ w1) @ w2)) - 0.5
    a = jax.nn.sigmoid(a0 + (xa @ a1) @ a2)
    g = jax.nn.sigmoid(xg @ g1) @ g2
    if vres is None:
        v_first = v
    else:
        v0, v1, v2 = vres
        v = v + (v_first - v) * jax.nn.sigmoid(v0 + (xv @ v1) @ v2)
    f32 = jnp.float32
    hs = (b, t, N_HEADS, HEAD_SIZE)
    kk = (k * k_k).astype(f32).reshape(hs)
    kk = kk / jnp.maximum(jnp.sqrt(jnp.sum(kk * kk, axis=-1, keepdims=True)), 1e-12)
    k = k * (1 + (a - 1) * k_a)
    rh = r.astype(f32).reshape(hs)
    kh = k.astype(f32).reshape(hs)
    vh = v.astype(f32).reshape(hs)
    ah = a.astype(f32).reshape(hs)
    decay = jnp.exp(-jnp.exp(w.astype(f32))).reshape(hs)
    y, s_new = wkv7_scan(rh, decay, kh, vh, kk, ah, s0.astype(f32))
    mu = jnp.mean(y, axis=-1, keepdims=True)
    var = jnp.mean(jnp.square(y - mu), axis=-1, keepdims=True)
    y = ((y - mu) * lax.rsqrt(var + GN_EPS)).reshape(b, t, c) * ln_g.astype(f32) + ln_b.astype(f32)
    bonus = jnp.sum(rh * kh * r_k.astype(f32), axis=-1, keepdims=True) * vh
    y = (y + bonus.reshape(b, t, c)).astype(h.dtype)
    return (y * g) @ w_o, h[:, -1], s_new.astype(s0.dtype), v_first


def conv_glu_ffn(h, buf, w_in, dw_w, dw_b, w_out):
    a = h @ w_in
    u, gp = a[..., :D_FF], a[..., D_FF:]
    gc, new_buf = causal_dwconv(gp, buf, dw_w, dw_b)
    return (jax.nn.gelu(gc) * u) @ w_out, new_buf


def trunk(x, conv_st, shift_st, wkv_st, ffn_st, p):
    new_conv, new_shift, new_wkv, new_ffn = [], [], [], []
    v_first = None
    for i in range(DEPTH):
        g = p['norm_g'][i]
        h = rmsnorm(x, g[0])
        j = i // N_MIXERS
        if i % N_MIXERS == 0:
            out, cb = conformer_conv(h, conv_st[j], p['conv_pw1_w'][j], p['conv_pw1_b'][j], p['conv_dw_w'][j],
                                     p['conv_dw_b'][j], p['conv_ln_g'][j], p['conv_ln_b'][j],
                                     p['conv_pw2_w'][j], p['conv_pw2_b'][j])
            new_conv.append(cb)
        else:
            vres = None if j == 0 else (p['rwkv_v0'][j - 1], p['rwkv_v1'][j - 1], p['rwkv_v2'][j - 1])
            out, sh, s_new, v_first = rwkv7_time_mix(
                h, shift_st[j], wkv_st[j], v_first, vres, p['rwkv_mix'][j], p['rwkv_w_r'][j], p['rwkv_w_k'][j],
                p['rwkv_w_v'][j], p['rwkv_w_o'][j], p['rwkv_w0'][j], p['rwkv_w1'][j], p['rwkv_w2'][j],
                p['rwkv_a0'][j], p['rwkv_a1'][j], p['rwkv_a2'][j], p['rwkv_g1'][j], p['rwkv_g2'][j],
                p['rwkv_k_k'][j], p['rwkv_k_a'][j], p['rwkv_r_k'][j], p['rwkv_ln_g'][j], p['rwkv_ln_b'][j])
            new_shift.append(sh)
            new_wkv.append(s_new)
        x = x + rmsnorm(out, g[1])
        h = rmsnorm(x, g[2])
        out, fb = conv_glu_ffn(h, ffn_st[i], p['ffn_w_in'][i], p['ffn_dw_w'][i], p['ffn_dw_b'][i], p['ffn_w_out'][i])
        new_ffn.append(fb)
        x = x + rmsnorm(out, g[3])
    return x, jnp.stack(new_conv), jnp.stack(new_shift), jnp.stack(new_wkv), jnp.stack(new_ffn)


def setup_inputs(seed: int = 0) -> dict:
    key = jax.random.key(seed)
    ks = iter(jax.random.split(key, 64))
    f32 = jnp.float32

    def nrm(shape, scale):
        return jax.random.normal(next(ks), shape, f32) * scale

    def unif(shape, lo, hi):
        return jax.random.uniform(next(ks), shape, f32, lo, hi)

    D, NC, NR, NV = D_MODEL, N_CONV_LAYERS, N_RWKV_LAYERS, N_VRES_LAYERS
    sd = D ** -0.5
    return {
        'x_prompt': nrm((BATCH, SEQ, D), 1.0),
        'x_sample': nrm((DEC_BATCH, DEC_SEQ, D), 1.0),
        'state_conv_mix': nrm((NC, DEC_BATCH, CONV_WIDTH - 1, D), 0.5),
        'state_rwkv_shift': nrm((NR, DEC_BATCH, D), 1.0),
        'state_rwkv_wkv': nrm((NR, DEC_BATCH, N_HEADS, HEAD_SIZE, HEAD_SIZE), 0.5),
        'state_ffn_conv': nrm((DEPTH, DEC_BATCH, FFN_CONV_WIDTH - 1, D_FF), 1.0),
        'norm_g': 1.0 + nrm((DEPTH, 4, D), 0.02),
        'conv_pw1_w': nrm((NC, D, 2 * D), sd),
        'conv_pw1_b': nrm((NC, 2 * D), 0.02),
        'conv_dw_w': nrm((NC, CONV_WIDTH, D), CONV_WIDTH ** -0.5),
        'conv_dw_b': nrm((NC, D), 0.02),
        'conv_ln_g': 1.0 + nrm((NC, D), 0.02),
        'conv_ln_b': nrm((NC, D), 0.02),
        'conv_pw2_w': nrm((NC, D, D), sd),
        'conv_pw2_b': nrm((NC, D), 0.02),
        'rwkv_mix': unif((NR, 6, D), 0.0, 1.0),
        'rwkv_w_r': nrm((NR, D, D), sd),
        'rwkv_w_k': nrm((NR, D, D), sd),
        'rwkv_w_v': nrm((NR, D, D), sd),
        'rwkv_w_o': nrm((NR, D, D), sd),
        'rwkv_w0': unif((NR, D), -6.0, -1.0),
        'rwkv_w1': nrm((NR, D, D_DECAY_LORA), sd),
        'rwkv_w2': nrm((NR, D_DECAY_LORA, D), 0.5 * D_DECAY_LORA ** -0.5),
        'rwkv_a0': nrm((NR, D), 0.1),
        'rwkv_a1': nrm((NR, D, D_AAA_LORA), sd),
        'rwkv_a2': nrm((NR, D_AAA_LORA, D), 0.5 * D_AAA_LORA ** -0.5),
        'rwkv_v0': 1.0 + nrm((NV, D), 0.1),
        'rwkv_v1': nrm((NV, D, D_MV_LORA), sd),
        'rwkv_v2': nrm((NV, D_MV_LORA, D), 0.5 * D_MV_LORA ** -0.5),
        'rwkv_g1': nrm((NR, D, D_GATE_LORA), sd),
        'rwkv_g2': nrm((NR, D_GATE_LORA, D), D_GATE_LORA ** -0.5),
        'rwkv_k_k': 0.85 + nrm((NR, D), 0.05),
        'rwkv_k_a': 1.0 + nrm((NR, D), 0.05),
        'rwkv_r_k': nrm((NR, N_HEADS, HEAD_SIZE), 0.1),
        'rwkv_ln_g': 1.0 + nrm((NR, D), 0.02),
        'rwkv_ln_b': nrm((NR, D), 0.02),
        'ffn_w_in': nrm((DEPTH, D, 2 * D_FF), sd),
        'ffn_dw_w': nrm((DEPTH, FFN_CONV_WIDTH, D_FF), FFN_CONV_WIDTH ** -0.5),
        'ffn_dw_b': nrm((DEPTH, D_FF), 0.02),
        'ffn_w_out': nrm((DEPTH, D_FF, D), D_FF ** -0.5),
    }


def reference(x_prompt, x_sample, state_conv_mix, state_rwkv_shift, state_rwkv_wkv, state_ffn_conv, norm_g,
              conv_pw1_w, conv_pw1_b, conv_dw_w, conv_dw_b, conv_ln_g, conv_ln_b, conv_pw2_w, conv_pw2_b,
              rwkv_mix, rwkv_w_r, rwkv_w_k, rwkv_w_v, rwkv_w_o, rwkv_w0, rwkv_w1, rwkv_w2, rwkv_a0, rwkv_a1,
              rwkv_a2, rwkv_v0, rwkv_v1, rwkv_v2, rwkv_g1, rwkv_g2, rwkv_k_k, rwkv_k_a, rwkv_r_k, rwkv_ln_g,
              rwkv_ln_b, ffn_w_in, ffn_dw_w, ffn_dw_b, ffn_w_out):
    p = dict(norm_g=norm_g, conv_pw1_w=conv_pw1_w, conv_pw1_b=conv_pw1_b, conv_dw_w=conv_dw_w,
             conv_dw_b=conv_dw_b, conv_ln_g=conv_ln_g, conv_ln_b=conv_ln_b, conv_pw2_w=conv_pw2_w,
             conv_pw2_b=conv_pw2_b, rwkv_mix=rwkv_mix, rwkv_w_r=rwkv_w_r, rwkv_w_k=rwkv_w_k, rwkv_w_v=rwkv_w_v,
             rwkv_w_o=rwkv_w_o, rwkv_w0=rwkv_w0, rwkv_w1=rwkv_w1, rwkv_w2=rwkv_w2, rwkv_a0=rwkv_a0,
             rwkv_a1=rwkv_a1, rwkv_a2=rwkv_a2, rwkv_v0=rwkv_v0, rwkv_v1=rwkv_v1, rwkv_v2=rwkv_v2,
             rwkv_g1=rwkv_g1, rwkv_g2=rwkv_g2, rwkv_k_k=rwkv_k_k, rwkv_k_a=rwkv_k_a, rwkv_r_k=rwkv_r_k,
             rwkv_ln_g=rwkv_ln_g, rwkv_ln_b=rwkv_ln_b, ffn_w_in=ffn_w_in, ffn_dw_w=ffn_dw_w,
             ffn_dw_b=ffn_dw_b, ffn_w_out=ffn_w_out)
    b, dt = x_prompt.shape[0], x_prompt.dtype
    zc = jnp.zeros((N_CONV_LAYERS, b, CONV_WIDTH - 1, D_MODEL), dt)
    zs = jnp.zeros((N_RWKV_LAYERS, b, D_MODEL), dt)
    zw = jnp.zeros((N_RWKV_LAYERS, b, N_HEADS, HEAD_SIZE, HEAD_SIZE), dt)
    zf = jnp.zeros((DEPTH, b, FFN_CONV_WIDTH - 1, D_FF), dt)
    y_prompt, conv_p, shift_p, wkv_p, ffn_p = trunk(x_prompt, zc, zs, zw, zf, p)
    y_sample, conv_s, shift_s, wkv_s, ffn_s = trunk(x_sample, state_conv_mix, state_rwkv_shift,
                                                    state_rwkv_wkv, state_ffn_conv, p)
    return (y_prompt, y_sample, conv_p, conv_s, shift_p, shift_s, wkv_p, wkv_s, ffn_p, ffn_s)
```

```python
import contextlib
import numpy as np
import ml_dtypes
import concourse.bass as bass
import concourse.mybir as mybir
from concourse.bass_utils import run_bass_kernel_spmd

F32 = mybir.dt.float32
BF16 = mybir.dt.bfloat16
AF = mybir.ActivationFunctionType
ALU = mybir.AluOpType
AX = mybir.AxisListType

NCORES = 8
DBG = 0
RMS_EPS = 1e-6
LN_EPS = 1e-5
GN_EPS = 64e-5


class Cfg:
    def __init__(self, D=2048, DFF=5632, PT=1024, NS=4, SL=64, depth=4, G=2):
        self.D, self.DFF, self.PT, self.NS, self.SL, self.depth, self.G = D, DFF, PT, NS, SL, depth, G
        self.KC = D // 128
        self.FC = DFF // 128
        self.HP = D // 128
        self.T = PT + NS * SL
        self.NCL = (depth + 1) // 2
        self.NRL = depth // 2
        self.NVL = max(self.NRL - 1, 0)
        self.HALO = 30
        self.HC = self.HALO + PT + NS * (1 + SL)
        self.UC = self.HALO + PT + NS * (self.HALO + SL)
        self.GC = 2 + PT + NS * (2 + SL)
        self.NCH = PT // 64
        self.NU = self.NCH + NS
        assert PT % 128 == 0 and SL == 64 and self.FC % G == 0


class Res:
    __slots__ = ("name", "w", "rs")

    def __init__(self, name):
        self.name = name
        self.w = None
        self.rs = []


EPOCH = 24000
ENGS = ("pe", "act", "dve", "pool", "sp")


class Prog:
    def __init__(self, nc, stack):
        self.nc = nc
        self.stack = stack
        self.ops = {e: [] for e in ENGS}
        self.cnt = {e: 0 for e in ENGS}
        self.epoch = {e: 0 for e in ENGS}
        self.seen = {e: {} for e in ENGS}
        self.sems = {}
        self.dcnt = {}
        self.nops = 0
        self.pending = {e: [] for e in ENGS}

    def sem(self, key):
        s = self.sems.get(key)
        if s is None:
            s = self.stack.enter_context(self.nc.semaphore("s%d" % len(self.sems)))
            self.sems[key] = s
        return s

    def add(self, eng, fn, reads=(), writes=(), dma=None):
        deps = list(self.pending[eng])
        self.pending[eng] = []
        for r in reads:
            if r.w is not None:
                deps.append(r.w)
        for w in writes:
            if w.w is not None:
                deps.append(w.w)
            deps.extend(w.rs)
        if dma is None:
            if self.cnt[eng] >= EPOCH:
                self.epoch[eng] += 1
                self.cnt[eng] = 0
            self.cnt[eng] += 1
            key = ("c", eng, self.epoch[eng])
            tok = (key, self.cnt[eng])
            inc = 1
        else:
            key = ("d", dma)
            self.dcnt[key] = self.dcnt.get(key, 0) + 16
            tok = (key, self.dcnt[key])
            inc = 16
        need = {}
        for (k, v) in deps:
            if eng == "pe" and k[0] == "c" and k[1] == "pe":
                continue
            if self.seen[eng].get(k, 0) >= v:
                continue
            if need.get(k, 0) < v:
                need[k] = v
        for k, v in need.items():
            self.seen[eng][k] = v
        self.ops[eng].append((list(need.items()), fn, key, inc))
        for r in reads:
            r.rs.append(tok)
        for w in writes:
            w.w = tok
            w.rs = []
        self.nops += 1
        return tok

    def emit(self, final_keys=()):
        nc = self.nc
        for e in ENGS:
            for (need, fn, key, inc) in self.ops[e]:
                self.sem(key)
                for k, _ in need:
                    self.sem(k)
        prog = self

        def run(e, engobj):
            for (need, fn, key, inc) in prog.ops[e]:
                for k, v in need:
                    engobj.wait_ge(prog.sems[k], v)
                ins = fn(engobj)
                ins.then_inc(prog.sems[key], inc)
            if e == "sp":
                for k, v in prog.dcnt.items():
                    engobj.wait_ge(prog.sems[k], v)
                for ee in ENGS:
                    if ee == "sp":
                        continue
                    for ep in range(prog.epoch[ee] + 1):
                        kk = ("c", ee, ep)
                        if kk in prog.sems:
                            vv = prog.cnt[ee] if ep == prog.epoch[ee] else EPOCH
                            if vv > 0:
                                engobj.wait_ge(prog.sems[kk], vv)

        with nc.Block() as block:
            @block.tensor
            def _(t):
                run("pe", t)

            @block.scalar
            def _(s):
                run("act", s)

            @block.vector
            def _(v):
                run("dve", v)

            @block.gpsimd
            def _(g):
                run("pool", g)

            @block.sync
            def _(s):
                run("sp", s)


def param_layout(cfg):
    off = {}
    n = 0

    def put(name, cols):
        nonlocal n
        off[name] = (n, cols)
        n += cols
    KC, FC = cfg.KC, cfg.FC
    for i in range(cfg.depth):
        for q in range(4):
            put(("ng", i, q), KC)
        put(("fdw", i), FC * 3)
        put(("fdb", i), FC)
    for j in range(cfg.NCL):
        put(("pw1b", j), 2 * KC)
        put(("dww", j), KC * 31)
        for nm in ("dwb", "clng", "clnb", "pw2b"):
            put((nm, j), KC)
    for j in range(cfg.NRL):
        for q in range(6):
            put(("mix", j, q), KC)
        for nm in ("w0", "a0", "kk", "ka", "rk", "rlng", "rlnb"):
            put((nm, j), KC)
    for j in range(cfg.NVL):
        put(("v0", j), KC)
    return off, n


def fm_vec(v):
    return np.ascontiguousarray(np.asarray(v, np.float32).reshape(-1, 128).T)


def pack_params(cfg, inp):
    off, n = param_layout(cfg)
    P = np.zeros((128, n), np.float32)

    def st(key, arr):
        o, c = off[key]
        assert arr.shape == (128, c), (key, arr.shape, c)
        P[:, o:o + c] = arr
    for i in range(cfg.depth):
        for q in range(4):
            st(("ng", i, q), fm_vec(inp["norm_g"][i, q]))
        w = np.asarray(inp["ffn_dw_w"][i], np.float32)
        st(("fdw", i), np.ascontiguousarray(w.reshape(3, cfg.FC, 128).transpose(2, 1, 0)).reshape(128, cfg.FC * 3))
        st(("fdb", i), fm_vec(inp["ffn_dw_b"][i]))
    for j in range(cfg.NCL):
        st(("pw1b", j), fm_vec(inp["conv_pw1_b"][j]))
        w = np.asarray(inp["conv_dw_w"][j], np.float32)
        st(("dww", j), np.ascontiguousarray(w.reshape(31, cfg.KC, 128).transpose(2, 1, 0)).reshape(128, cfg.KC * 31))
        st(("dwb", j), fm_vec(inp["conv_dw_b"][j]))
        st(("clng", j), fm_vec(inp["conv_ln_g"][j]))
        st(("clnb", j), fm_vec(inp["conv_ln_b"][j]))
        st(("pw2b", j), fm_vec(inp["conv_pw2_b"][j]))
    for j in range(cfg.NRL):
        for q in range(6):
            st(("mix", j, q), fm_vec(inp["rwkv_mix"][j, q]))
        st(("w0", j), fm_vec(inp["rwkv_w0"][j]))
        st(("a0", j), fm_vec(inp["rwkv_a0"][j]))
        st(("kk", j), fm_vec(inp["rwkv_k_k"][j]))
        st(("ka", j), fm_vec(inp["rwkv_k_a"][j]))
        st(("rk", j), fm_vec(np.asarray(inp["rwkv_r_k"][j]).reshape(-1)))
        st(("rlng", j), fm_vec(inp["rwkv_ln_g"][j]))
        st(("rlnb", j), fm_vec(inp["rwkv_ln_b"][j]))
    for j in range(cfg.NVL):
        st(("v0", j), fm_vec(inp["rwkv_v0"][j]))
    return P


def wl(w, mc=128):
    w = np.asarray(w, np.float32)
    K, M = w.shape
    kc = K // 128
    nm = M // mc
    return np.ascontiguousarray(w.reshape(kc, 128, nm, mc).transpose(2, 1, 0, 3)).reshape(nm, 128, kc * mc)


def const_bf16(cfg):
    C = 64
    s = np.arange(C)
    mstrict = (s[:, None] < s[None, :]).astype(np.float32)
    mincl = (s[:, None] <= s[None, :]).astype(np.float32)
    eye2 = np.eye(2, dtype=np.float32)
    Mst = np.kron(eye2, mstrict)
    MstT = np.kron(eye2, mstrict.T)
    maskG = np.concatenate([np.concatenate([mincl, mincl], 0), np.ones((128, 64), np.float32)], 1)
    ident = np.eye(128, dtype=np.float32)
    ones = np.ones((128, 128), np.float32)
    bones = np.kron(eye2, np.ones((64, 64), np.float32))
    I2 = np.concatenate([np.eye(64, dtype=np.float32)] * 2, 0)
    parts = [ident, ones, bones, I2, np.tile(Mst, (1, 4)), np.tile(MstT, (1, 4)), np.tile(maskG, (1, 4))]
    return np.concatenate(parts, 1).astype(ml_dtypes.bfloat16)


CB_ID, CB_ONES, CB_BONES, CB_I2, CB_MST, CB_MSTT, CB_MG, CB_N = 0, 128, 256, 384, 448, 960, 1472, 1984


class B:
    def __init__(self, cfg, sublayers=None):
        self.cfg = cfg
        c = cfg
        self.nc = nc = bass.Bass("TRN2", target_bir_lowering=False)
        self.stack = contextlib.ExitStack()
        self.p = Prog(nc, self.stack)
        self.poff, self.npar = param_layout(cfg)
        KC, FC, T, D = c.KC, c.FC, c.T, c.D
        di = lambda name, shape, dt=F32: nc.dram_tensor(name, list(shape), dt, kind="ExternalInput").ap()
        do = lambda name, shape, dt=F32: nc.dram_tensor(name, list(shape), dt, kind="ExternalOutput").ap()
        self.d_x = di("xT", [128, KC * T])
        self.d_par = di("par", [128, self.npar])
        self.d_cb = di("cb", [128, CB_N], BF16)
        self.d_sel = di("sel", [128, 17])
        self.d_pw1 = di("w_pw1", [c.NCL * 2 * KC, 128, KC * 128])
        self.d_pw2 = di("w_pw2", [c.NCL * KC, 128, KC * 128])
        self.d_win = di("w_in", [c.depth * 2 * FC, 128, KC * 128])
        self.d_wout = di("w_out", [c.depth * (FC // c.G), 128, c.G * D])
        self.d_stconv = di("st_conv", [128, c.NCL * KC * c.NS * 30])
        self.d_stffn = di("st_ffn", [128, c.depth * FC * c.NS * 2])
        self.d_y = do("yT", [128, KC * T])
        self.d_oconv = do("o_conv", [128, c.NCL * KC * (1 + c.NS) * 30])
        self.d_offn = do("o_ffn", [128, c.depth * FC * (1 + c.NS) * 2])
        self.rwkv_decl()
        self.d_xh = nc.dram_tensor("x_home", [128, KC * T], F32).ap()
        self.t_hxin = nc.dram_tensor("hx_in", [128, KC * 30], BF16)
        self.t_hxmid = nc.dram_tensor("hx_mid", [4 * 128, KC * 30], BF16)
        self.t_hxout = nc.dram_tensor("hx_out", [NCORES * 128, KC * 30], BF16)
        sb = lambda name, cols, dt: self.stack.enter_context(nc.sbuf_tensor(name, [128, cols], dt))
        self.BIG = sb("big", KC * T, F32)
        self.HT = sb("ht", KC * c.HC, BF16)
        self.PAR = sb("par_sb", self.npar, F32)
        self.CB = sb("cb_sb", CB_N, BF16)
        self.SEL = sb("sel_sb", 17, F32)
        self.EPS = sb("eps_sb", 4, F32)
        self.STAT = sb("stat", 2 * T, F32)
        self.WS = [sb("ws%d" % i, KC * 128, BF16) for i in range(3)]
        self.AR2 = sb("ar2", max(2 * c.G * D + c.G * T, 2 * c.UC + 31 * 128 + 2048, 10752), BF16)
        self.STG = sb("stg", 3 * T + 96, F32)
        self.OST = sb("ost", 2 * (1 + c.NS) * 32, F32)
        self.DW3 = sb("dw3", 768, BF16)
        self.SQ = [sb("sq%d" % i, T, BF16) for i in range(4)]
        self.banks = [self.stack.enter_context(nc.psum_tensor("bank%d" % i, [128, 512], F32)) for i in range(8)]
        self.R_big = [Res("big%d" % k) for k in range(KC)]
        self.R_ht = [Res("ht%d" % k) for k in range(KC)]
        self.R_hthalo = Res("hthalo")
        self.R_const = Res("const")
        self.R_stat = Res("stat")
        self.R_ws = [Res("ws%d" % i) for i in range(3)]
        self.R_bank = [Res("bank%d" % i) for i in range(8)]
        self.R_sq = [Res("sq%d" % i) for i in range(4)]
        self.R_xh = [Res("xh%d" % k) for k in range(KC)]
        self.R_misc = {}
        self.wsi = 0
        self.bki = 0
        self.sqi = 0
        self.pend_barrier = None
        self.big3 = self.BIG[:, :].rearrange("p (k t) -> p k t", t=T)
        self.ht3 = self.HT[:, :].rearrange("p (k t) -> p k t", t=c.HC)
        self.rstd = self.STAT[:, 0:T]
        self.mean = self.STAT[:, T:2 * T]
        self.ptiles = [(o, min(512, T - o)) for o in range(0, T, 512)]
        self.mt = [("p", o, min(512, c.PT - o)) for o in range(0, c.PT, 512)] + [("s",)]
        self.build(sublayers)

    def R(self, name):
        r = self.R_misc.get(name)
        if r is None:
            r = self.R_misc[name] = Res(name)
        return r

    def par(self, key, col=0, n=1):
        o, c = self.poff[key]
        return self.PAR[:, o + col:o + col + n]

    def cbv(self, off, n):
        return self.CB[:, off:off + n]

    def bank(self, pool=(0, 1, 2, 3, 4, 5, 6, 7)):
        i = pool[self.bki % len(pool)]
        self.bki += 1
        return i

    def add(self, eng, fn, R=(), W=()):
        return self.p.add(eng, fn, R, W)

    def ACT(self, out, in_, func, R, W, bias=0.0, scale=1.0):
        self.p.add("act", lambda e: e.activation(out=out, in_=in_, func=func, bias=bias, scale=scale), R, W)

    def TS(self, eng, out, in0, s1, s2, op0, op1, R, W):
        if s2 is None:
            self.p.add(eng, lambda e: e.tensor_scalar(out=out, in0=in0, scalar1=s1, scalar2=None, op0=op0), R, W)
        else:
            self.p.add(eng, lambda e: e.tensor_scalar(out=out, in0=in0, scalar1=s1, scalar2=s2, op0=op0, op1=op1), R, W)

    def TT(self, eng, out, in0, in1, op, R, W):
        self.p.add(eng, lambda e: e.tensor_tensor(out=out, in0=in0, in1=in1, op=op), R, W)

    def STT(self, eng, out, in0, scalar, in1, op0, op1, R, W):
        self.p.add(eng, lambda e: e.scalar_tensor_tensor(out=out, in0=in0, scalar=scalar, in1=in1, op0=op0, op1=op1), R, W)

    def CP(self, eng, out, in_, R, W):
        if eng == "act":
            self.ACT(out, in_, AF.Identity, R, W)
        else:
            self.p.add(eng, lambda e: e.tensor_copy(out=out, in_=in_), R, W)

    def MM(self, out, lhsT, rhs, start, stop, R, W):
        self.p.add("pe", lambda e: e.matmul(out, lhsT, rhs, start=start, stop=stop), R, W)

    def DMA(self, q, out, in_, R, W, key):
        self.p.add(q, lambda e: e.dma_start(out=out, in_=in_), R, W, dma=key)

    def wload(self, dram, idx, cols=None):
        i = self.wsi % 3
        self.wsi += 1
        cols = cols or dram.shape[-1]
        self.DMA("pool", self.WS[i][:, 0:cols], dram[idx], [], [self.R_ws[i]], "ws%d" % i)
        return self.WS[i], self.R_ws[i]

    def hrhs(self, kc, t):
        c = self.cfg
        if t[0] == "p":
            return self.ht3[:, kc, 30 + t[1]:30 + t[1] + t[2]]
        if t[0] == "s":
            b0 = 30 + c.PT
            return self.ht3[:, kc, b0:b0 + c.NS * 65].rearrange("p (s c) -> p s c", c=65)[:, :, 1:65]
        if t[0] == "h":
            return self.ht3[:, kc, 0:30]
        if t[0] == "h2":
            return self.ht3[:, kc, 28:30]
        raise ValueError(t)

    def tn(self, t):
        c = self.cfg
        return {"p": t[2] if t[0] == "p" else 0, "s": c.NS * 64, "h": 30, "h2": 2}[t[0]]

    def bk(self, b, t):
        n = self.tn(t)
        ap = self.banks[b][:, 0:n]
        if t[0] == "s":
            return ap.rearrange("p (s c) -> p s c", c=64)
        return ap

    def pl(self, row, t, three=True):
        c = self.cfg
        if t[0] == "p":
            return row[:, t[1]:t[1] + t[2]]
        ap = row[:, c.PT:c.PT + c.NS * 64]
        return ap.rearrange("p (s c) -> p s c", c=64) if three else ap

    def colstats(self, want_sum, eps):
        c = self.cfg
        KC, T = c.KC, c.T
        ones = self.cbv(CB_ONES, 128)
        sqb = (5, 6, 7)
        smb = (2, 3, 4)
        for kc in range(KC):
            qi = self.sqi % 2
            self.sqi += 1
            self.ACT(self.SQ[qi][:, :], self.big3[:, kc, :], AF.Square, [self.R_big[kc]], [self.R_sq[qi]])
            for ti, (o, n) in enumerate(self.ptiles):
                self.MM(self.banks[sqb[ti]][:, 0:n], ones, self.SQ[qi][:, o:o + n], kc == 0, kc == KC - 1,
                        [self.R_sq[qi], self.R_const], [self.R_bank[sqb[ti]]])
            if want_sum:
                self.CP("act", self.SQ[2 + qi][:, :], self.big3[:, kc, :], [self.R_big[kc]], [self.R_sq[2 + qi]])
                for ti, (o, n) in enumerate(self.ptiles):
                    self.MM(self.banks[smb[ti]][:, 0:n], ones, self.SQ[2 + qi][:, o:o + n], kc == 0, kc == KC - 1,
                            [self.R_sq[2 + qi], self.R_const], [self.R_bank[smb[ti]]])
        inv = 1.0 / c.D
        for ti, (o, n) in enumerate(self.ptiles):
            rs = self.rstd[:, o:o + n]
            if not want_sum:
                self.ACT(rs, self.banks[sqb[ti]][:, 0:n], AF.Sqrt, [self.R_bank[sqb[ti]], self.R_const], [self.R_stat],
                         bias=self.epsc(eps), scale=inv)
            else:
                mn = self.mean[:, o:o + n]
                self.TS("dve", mn, self.banks[smb[ti]][:, 0:n], inv, None, ALU.mult, None,
                        [self.R_bank[smb[ti]]], [self.R_stat])
                self.TT("dve", rs, mn, mn, ALU.mult, [self.R_stat], [self.R_stat])
                self.STT("dve", rs, self.banks[sqb[ti]][:, 0:n], inv, rs, ALU.mult, ALU.subtract,
                         [self.R_bank[sqb[ti]], self.R_stat], [self.R_stat])
                self.ACT(rs, rs, AF.Sqrt, [self.R_stat, self.R_const], [self.R_stat], bias=self.epsc(eps))
            self.add("dve", lambda e, rs=rs: e.reciprocal(out=rs, in_=rs), [self.R_stat], [self.R_stat])

    def epsc(self, eps):
        i = {RMS_EPS: 0, LN_EPS: 1, GN_EPS: 2}[eps]
        return self.EPS[:, i:i + 1]

    def norm_in(self, gkey, hlast=None, pre_exchange=None):
        c = self.cfg
        KC, T, PT, NS = c.KC, c.T, c.PT, c.NS
        self.colstats(False, RMS_EPS)
        for kc in range(KC):
            g = self.par(gkey, kc)
            src = self.big3[:, kc, :]
            self.STT("dve", self.ht3[:, kc, PT:30 + PT], src[:, PT - 30:PT], g, self.rstd[:, PT - 30:PT], ALU.mult, ALU.mult,
                     [self.R_big[kc], self.R_stat], [self.R_ht[kc]])
        if pre_exchange is not None:
            pre_exchange()
        self.halo_exchange()
        for kc in range(KC):
            g = self.par(gkey, kc)
            src = self.big3[:, kc, :]
            self.STT("dve", self.ht3[:, kc, 30:PT], src[:, 0:PT - 30], g, self.rstd[:, 0:PT - 30], ALU.mult, ALU.mult,
                     [self.R_big[kc], self.R_stat], [self.R_ht[kc]])
            b0 = 30 + PT
            dst = self.ht3[:, kc, b0:b0 + NS * 65].rearrange("p (s c) -> p s c", c=65)[:, :, 1:65]
            self.STT("dve", dst, self.pl(src, ("s",)), g, self.pl(self.rstd, ("s",)), ALU.mult, ALU.mult,
                     [self.R_big[kc], self.R_stat], [self.R_ht[kc]])
            if hlast is not None:
                hlast(kc, src, g)
            self.DMA("sp", self.d_xh[:, kc * T:(kc + 1) * T], src, [self.R_big[kc]], [self.R_xh[kc]], "xst%d" % kc)

    def halo_exchange(self):
        c = self.cfg
        KC, PT = c.KC, c.PT
        n = KC * 30
        Rd1, Rd2 = self.R("hx_in"), self.R("hx_out")
        self.DMA("sp", self.t_hxin.ap().rearrange("p (k c) -> p k c", c=30), self.ht3[:, :, PT:PT + 30],
                 self.R_ht, [Rd1], "hx1")
        self.allgather8(self.t_hxin, self.t_hxmid, self.t_hxout, Rd1, Rd2)
        src = self.t_hxout.ap().rearrange("(r p) f -> p r f", p=128)
        for i in range(4):
            self.DMA("sp", self.SQ[i][:, 0:2 * n].rearrange("p (r f) -> p r f", f=n), src[:, 2 * i:2 * i + 2, :],
                     [Rd2], [self.R_sq[i]], "hx2_%d" % i)
        self.halo_pending = True

    def halo_recv(self):
        if not getattr(self, "halo_pending", False):
            return
        self.halo_pending = False
        c = self.cfg
        KC = c.KC
        n = KC * 30
        halo = self.ht3[:, :, 0:30]
        for r in range(NCORES):
            s = self.SEL[:, r:r + 1]
            piece = self.SQ[r // 2][:, (r % 2) * n:(r % 2 + 1) * n].rearrange("p (k c) -> p k c", c=30)
            if r == 0:
                self.TS("dve", halo, piece, s, None, ALU.mult, None, [self.R_sq[r // 2], self.R_const], [self.R_hthalo])
            else:
                self.STT("dve", halo, piece, s, halo, ALU.mult, ALU.add, [self.R_sq[r // 2], self.R_const], [self.R_hthalo])

    def resid(self, gkey):
        c = self.cfg
        KC, T = c.KC, c.T
        self.colstats(False, RMS_EPS)
        xs = [self.STG[:, 0:T], self.STG[:, T:2 * T]]
        Rx = [self.R("xs0"), self.R("xs1")]
        for kc in range(KC):
            i = kc % 2
            self.DMA("sp", xs[i], self.d_xh[:, kc * T:(kc + 1) * T], [self.R_xh[kc]], [Rx[i]], "xld%d" % i)
            b = self.big3[:, kc, :]
            self.TT("dve", b, b, self.rstd, ALU.mult, [self.R_stat], [self.R_big[kc]])
            self.STT("dve", b, b, self.par(gkey, kc), xs[i], ALU.mult, ALU.add, [Rx[i], self.R_const], [self.R_big[kc]])

    def allgather8(self, tin, tmid, tout, R_in, R_out):
        Rm = self.R("agmid_" + tmid.name)
        self.add("pool", lambda e: e.collective_compute("AllGather", ALU.bypass, replica_groups=[[0, 1, 2, 3], [4, 5, 6, 7]],
                                                        ins=[tin.ap().opt()], outs=[tmid.ap().opt()]), [R_in], [Rm])
        self.add("pool", lambda e: e.collective_compute("AllGather", ALU.bypass, replica_groups=[[0, 4], [1, 5], [2, 6], [3, 7]],
                                                        ins=[tmid.ap().opt()], outs=[tout.ap().opt()]), [Rm], [R_out])

    def barrier(self):
        p = self.p
        toks = []
        for e in ENGS:
            if p.cnt[e] > 0:
                toks.append((("c", e, p.epoch[e]), p.cnt[e]))
        for k, v in p.dcnt.items():
            toks.append((k, v))
        for e in ENGS:
            p.pending[e] = list(toks)

    def conformer(self, j):
        c = self.cfg
        KC, T, PT, NS, UC = c.KC, c.T, c.PT, c.NS, c.UC
        ub = [self.AR2[:, 0:UC], self.AR2[:, UC:2 * UC]]
        Rub = [self.R("ub0"), self.R("ub1")]
        dwd = self.AR2[:, 2 * UC:2 * UC + 31 * 128]
        Rdwd = self.R("dwd")
        sgs = [self.AR2[:, 2 * UC + 31 * 128 + i * 1024:2 * UC + 31 * 128 + (i + 1) * 1024].bitcast(F32) for i in range(2)]
        Rsg = [self.R("sg0"), self.R("sg1")]
        sth = self.STG[:, 2 * T:2 * T + 64]
        Rsth = self.R("sth")
        ident = self.cbv(CB_ID, 128)
        sgi = 0
        for m in range(KC):
            ws_l, R_l = self.wload(self.d_pw1, j * 2 * KC + m)
            ws_g, R_g = self.wload(self.d_pw1, j * 2 * KC + KC + m)
            u, Ru = ub[m % 2], Rub[m % 2]
            ost = self.OST[:, (m % 2) * (1 + NS) * 32:(m % 2) * (1 + NS) * 32 + (1 + NS) * 30]
            Rost = self.R("ost%d" % (m % 2))
            for s in range(NS):
                o = ((j * KC + m) * NS + s) * 30
                stg = self.STG[:, 2 * T + 32 * (s % 2):2 * T + 32 * (s % 2) + 30]
                Rs = self.R("sth%d" % (s % 2))
                self.DMA("sp", stg, self.d_stconv[:, o:o + 30], [], [Rs], "sth%d" % (s % 2))
                ubase = 30 + PT + s * 94
                self.CP("act", u[:, ubase:ubase + 30], stg, [Rs], [Ru])
            for t in self.mt + [("h",)]:
                bA, bB = self.bank(), self.bank()
                if t[0] == "h":
                    self.halo_recv()
                for kc in range(KC):
                    rr = [self.R_ht[kc], R_l] + ([self.R_hthalo] if t[0] == "h" else [])
                    self.MM(self.bk(bA, t), ws_l[:, kc * 128:(kc + 1) * 128], self.hrhs(kc, t), kc == 0, kc == KC - 1,
                            rr, [self.R_bank[bA]])
                for kc in range(KC):
                    rr = [self.R_ht[kc], R_g] + ([self.R_hthalo] if t[0] == "h" else [])
                    self.MM(self.bk(bB, t), ws_g[:, kc * 128:(kc + 1) * 128], self.hrhs(kc, t), kc == 0, kc == KC - 1,
                            rr, [self.R_bank[bB]])
                n = self.tn(t)
                sg, Rs_ = sgs[sgi % 2], Rsg[sgi % 2]
                sgi += 1
                self.ACT(sg[:, 0:n], self.banks[bB][:, 0:n], AF.Sigmoid, [self.R_bank[bB], self.R_const], [Rs_],
                         bias=self.par(("pw1b", j), KC + m))
                bl = self.par(("pw1b", j), m)
                if t[0] == "h":
                    self.STT("dve", u[:, 0:30], self.banks[bA][:, 0:30], bl, sg[:, 0:30], ALU.add, ALU.mult,
                             [self.R_bank[bA], Rs_, self.R_const], [Ru])
                    self.TS("dve", u[:, 0:30], u[:, 0:30], self.SEL[:, 16:17], None, ALU.mult, None, [self.R_const], [Ru])
                elif t[0] == "p":
                    self.STT("dve", u[:, 30 + t[1]:30 + t[1] + n], self.banks[bA][:, 0:n], bl, sg[:, 0:n], ALU.add, ALU.mult,
                             [self.R_bank[bA], Rs_, self.R_const], [Ru])
                    if t[1] + n == PT:
                        self.STT("dve", ost[:, 0:30], self.banks[bA][:, n - 30:n], bl, sg[:, n - 30:n], ALU.add, ALU.mult,
                                 [self.R_bank[bA], Rs_, self.R_const], [Rost])
                else:
                    b0 = 30 + PT
                    dst = u[:, b0:b0 + NS * 94].rearrange("p (s c) -> p s c", c=94)[:, :, 30:94]
                    sg3 = sg[:, 0:n].rearrange("p (s c) -> p s c", c=64)
                    self.STT("dve", dst, self.bk(bA, t), bl, sg3, ALU.add, ALU.mult,
                             [self.R_bank[bA], Rs_, self.R_const], [Ru])
                    self.STT("dve", ost[:, 30:(1 + NS) * 30].rearrange("p (s c) -> p s c", c=30), self.bk(bA, t)[:, :, 34:64], bl,
                             sg3[:, :, 34:64], ALU.add, ALU.mult, [self.R_bank[bA], Rs_, self.R_const], [Rost])
            oo = (j * KC + m) * (1 + NS) * 30
            self.DMA("sp", self.d_oconv[:, oo:oo + (1 + NS) * 30], ost, [Rost], [self.R("d_oconv")], "ost%d" % (m % 2))
            wk = self.par(("dww", j), m * 31, 31)
            self.TT("dve", dwd.rearrange("p (k j) -> p k j", j=128), ident.unsqueeze(1).broadcast_to([128, 31, 128]),
                    wk.unsqueeze(2).broadcast_to([128, 31, 128]), ALU.mult, [self.R_const], [Rdwd])
            for t in self.mt:
                b = self.bank()
                n = self.tn(t)
                for k in range(31):
                    if t[0] == "p":
                        rhs = u[:, t[1] + k:t[1] + k + n]
                    else:
                        b0 = 30 + PT
                        rhs = u[:, b0:b0 + NS * 94].rearrange("p (s c) -> p s c", c=94)[:, :, k:k + 64]
                    self.MM(self.bk(b, t), dwd[:, k * 128:(k + 1) * 128], rhs, k == 0, k == 30, [Ru, Rdwd], [self.R_bank[b]])
                self.ACT(self.pl(self.big3[:, m, :], t, three=False), self.banks[b][:, 0:n], AF.Identity,
                         [self.R_bank[b], self.R_const], [self.R_big[m]], bias=self.par(("dwb", j), m))
        self.colstats(True, LN_EPS)
        tmp = self.STG[:, 0:T]
        tmp2 = self.STG[:, T:2 * T]
        Rt, Rt2 = self.R("xs0"), self.R("xs1")
        for kc in range(KC):
            b = self.big3[:, kc, :]
            self.TT("dve", tmp, b, self.mean, ALU.subtract, [self.R_big[kc], self.R_stat], [Rt])
            self.TT("dve", tmp, tmp, self.rstd, ALU.mult, [self.R_stat], [Rt])
            self.TS("dve", tmp, tmp, self.par(("clng", j), kc), self.par(("clnb", j), kc), ALU.mult, ALU.add, [self.R_const], [Rt])
            self.ACT(tmp2, tmp, AF.Sigmoid, [Rt], [Rt2])
            self.TT("dve", self.ht3[:, kc, 0:T], tmp, tmp2, ALU.mult, [Rt, Rt2], [self.R_ht[kc], self.R_hthalo])
        for m in range(KC):
            ws, Rw = self.wload(self.d_pw2, j * KC + m)
            for t in self.mt:
                b = self.bank()
                n = self.tn(t)
                for kc in range(KC):
                    self.MM(self.banks[b][:, 0:n], ws[:, kc * 128:(kc + 1) * 128], self.pl(self.ht3[:, kc, 0:T], t, three=False),
                            kc == 0, kc == KC - 1, [self.R_ht[kc], Rw], [self.R_bank[b]])
                self.ACT(self.pl(self.big3[:, m, :], t, three=False), self.banks[b][:, 0:n], AF.Identity,
                         [self.R_bank[b], self.R_const], [self.R_big[m]], bias=self.par(("pw2b", j), m))

    def ffn(self, i):
        c = self.cfg
        KC, FC, T, PT, NS, G, D, GC = c.KC, c.FC, c.T, c.PT, c.NS, c.G, c.D, c.GC
        wo = [self.AR2[:, 0:G * D], self.AR2[:, G * D:2 * G * D]]
        Rwo = [self.R("wo0"), self.R("wo1")]
        zg = self.AR2[:, 2 * G * D:2 * G * D + G * T].rearrange("p (g t) -> p g t", t=T)
        Rzg = [self.R("zg%d" % gi) for gi in range(G)]
        t2 = self.STG[:, T:2 * T]
        Rt2 = self.R("xs1")
        stgb = self.STG[:, 0:T].bitcast(BF16)
        gps = [stgb[:, 0:GC]]
        dw3s = [self.DW3[:, 0:384], self.DW3[:, 384:768]]
        Rgps = [self.R("xs0"), self.R("xs0")]
        Rdw = self.R("dw3")
        shst = self.STG[:, 2 * T:2 * T + 2 * NS]
        Rsh = self.R("stg2")
        ident = self.cbv(CB_ID, 128)
        for g in range(FC // G):
            wsl, Rw_o = wo[g % 2], Rwo[g % 2]
            self.DMA("pool", wsl, self.d_wout[i * (FC // G) + g], [], [Rw_o], "wo%d" % (g % 2))
            for gi in range(G):
                f = g * G + gi
                gp, Rgp = gps[0], Rgps[0]
                dw3 = dw3s[f % 2]
                g3 = gp[:, 2 + PT:2 + PT + NS * 66].rearrange("p (s c) -> p s c", c=66)
                ws_u, R_u = self.wload(self.d_win, i * 2 * FC + f)
                ws_g, R_g = self.wload(self.d_win, i * 2 * FC + FC + f)
                so = ((i * FC + f) * NS) * 2
                self.DMA("sp", shst, self.d_stffn[:, so:so + NS * 2], [], [Rsh], "gph")
                self.CP("act", g3[:, :, 0:2], shst.rearrange("p (s c) -> p s c", c=2), [Rsh], [Rgp])
                sl = f % 2
                ofs = self.OST[:, sl * (1 + NS) * 32:sl * (1 + NS) * 32 + (1 + NS) * 2]
                Rofs = self.R("ost%d" % sl)
                for t in self.mt + [("h2",)]:
                    b = self.bank()
                    n = self.tn(t)
                    if t[0] == "h2":
                        self.halo_recv()
                    for kc in range(KC):
                        rr = [self.R_ht[kc], R_g] + ([self.R_hthalo] if t[0] == "h2" else [])
                        self.MM(self.bk(b, t), ws_g[:, kc * 128:(kc + 1) * 128], self.hrhs(kc, t), kc == 0, kc == KC - 1,
                                rr, [self.R_bank[b]])
                    if t[0] == "h2":
                        dst = gp[:, 0:2]
                    elif t[0] == "p":
                        dst = gp[:, 2 + t[1]:2 + t[1] + n]
                    else:
                        dst = g3[:, :, 2:66]
                    self.CP("act", dst, self.bk(b, t), [self.R_bank[b]], [Rgp])
                    if t[0] == "p" and t[1] + n == PT:
                        self.CP("act", ofs[:, 0:2], self.banks[b][:, n - 2:n], [self.R_bank[b]], [Rofs])
                    if t[0] == "s":
                        self.CP("act", ofs[:, 2:(1 + NS) * 2].rearrange("p (s c) -> p s c", c=2), self.bk(b, t)[:, :, 62:64],
                                [self.R_bank[b]], [Rofs])
                oo = (i * FC + f) * (1 + NS) * 2
                self.DMA("sp", self.d_offn[:, oo:oo + (1 + NS) * 2], ofs, [Rofs], [self.R("d_offn")], "ost%d" % sl)
                wk = self.par(("fdw", i), f * 3, 3)
                self.TT("dve", dw3.rearrange("p (k j) -> p k j", j=128), ident.unsqueeze(1).broadcast_to([128, 3, 128]),
                        wk.unsqueeze(2).broadcast_to([128, 3, 128]), ALU.mult, [self.R_const], [Rdw])
                bb = self.par(("fdb", i), f)
                for t in self.mt:
                    b = self.bank()
                    n = self.tn(t)
                    for k in range(3):
                        rhs = gp[:, t[1] + k:t[1] + k + n] if t[0] == "p" else g3[:, :, k:k + 64]
                        self.MM(self.bk(b, t), dw3[:, k * 128:(k + 1) * 128], rhs, k == 0, k == 2, [Rgp, Rdw], [self.R_bank[b]])
                    self.ACT(self.pl(t2, t, three=False), self.banks[b][:, 0:n], AF.Gelu_apprx_tanh, [self.R_bank[b], self.R_const],
                             [Rt2], bias=bb)
                for t in self.mt:
                    b = self.bank()
                    n = self.tn(t)
                    for kc in range(KC):
                        self.MM(self.bk(b, t), ws_u[:, kc * 128:(kc + 1) * 128], self.hrhs(kc, t), kc == 0, kc == KC - 1,
                                [self.R_ht[kc], R_u], [self.R_bank[b]])
                    self.TT("dve", self.pl(zg[:, gi, :], t, three=False), self.banks[b][:, 0:n], self.pl(t2, t, three=False),
                            ALU.mult, [self.R_bank[b], Rt2], [Rzg[gi]])
            for m in range(KC):
                for t in self.mt:
                    b = self.bank()
                    n = self.tn(t)
                    for gi in range(G):
                        self.MM(self.banks[b][:, 0:n], wsl[:, gi * D + m * 128:gi * D + (m + 1) * 128],
                                self.pl(zg[:, gi, :], t, three=False), gi == 0, gi == G - 1, [Rzg[gi], Rw_o], [self.R_bank[b]])
                    dst = self.pl(self.big3[:, m, :], t, three=False)
                    if g == 0:
                        self.CP("act", dst, self.banks[b][:, 0:n], [self.R_bank[b]], [self.R_big[m]])
                    else:
                        self.TT("dve", dst, dst, self.banks[b][:, 0:n], ALU.add, [self.R_bank[b]], [self.R_big[m]])

    def rwkv_decl(self):
        c = self.cfg
        nc = self.nc
        if c.NRL == 0:
            return
        KC, HP, T, NS = c.KC, c.HP, c.T, c.NS
        di = lambda name, shape, dt=F32: nc.dram_tensor(name, list(shape), dt, kind="ExternalInput").ap()
        do = lambda name, shape, dt=F32: nc.dram_tensor(name, list(shape), dt, kind="ExternalOutput").ap()
        self.d_rkvo = di("w_rkvo", [c.NRL * 4 * HP, 128, KC * 128])
        self.d_l1 = di("w_l1", [c.NRL * 5, 128, KC * 128])
        self.d_l2 = di("w_l2", [c.NRL * HP, 128, 5 * 128])
        self.d_stshift = di("st_shift", [128, c.NRL * KC * NS])
        self.d_stwkv = di("st_wkv", [128, c.NRL * HP * NS * 64])
        self.d_oshift = do("o_shift", [128, c.NRL * KC * (1 + NS)])
        self.d_owkv = do("o_wkv", [128, c.NRL * HP * (1 + NS) * 64])
        self.d_rkv = nc.dram_tensor("rkv_s", [c.NRL * 3 * HP, 128, T], F32).ap()
        self.d_ycb = nc.dram_tensor("ycb_s", [3 * HP, 128, T], F32).ap()
        self.NQ = 4 if HP % 4 == 0 else 1
        hq = HP // self.NQ
        self.t_sgin = [nc.dram_tensor("seg_in%d" % q, [128, hq * 128], F32) for q in range(self.NQ)]
        self.t_sgmid = [nc.dram_tensor("seg_mid%d" % q, [4 * 128, hq * 128], F32) for q in range(self.NQ)]
        self.t_sgout = [nc.dram_tensor("seg_out%d" % q, [NCORES * 128, hq * 128], F32) for q in range(self.NQ)]

    def rwkv_alloc(self):
        c = self.cfg
        KC, T, NS, HP = c.KC, c.T, c.NS, c.HP
        if not hasattr(self, "LW"):
            sb0 = lambda name, cols, dt: self.stack.enter_context(self.nc.sbuf_tensor(name, [128, cols], dt))
            self.LW = [self.AR2[:, 8448:9088], self.AR2[:, 9088:9728]]
            self.SGL = self.STG[:, 2 * T:3 * T].bitcast(BF16)
            self.OSH = sb0("osh", KC * (1 + NS) + KC, F32)
            self.SEGB = self.STAT[:, 0:HP * 128]
            self.SMALL = self.AR2[:, 5888:8448].bitcast(F32)

    def rwkv(self, j):
        c = self.cfg
        KC, HP, T, PT, NS, NU, NCH, D = c.KC, c.HP, c.T, c.PT, c.NS, c.NU, c.NCH, c.D
        sb = lambda name, cols, dt: self.stack.enter_context(self.nc.sbuf_tensor(name + "_%d" % j, [128, cols], dt))
        i_layer = 2 * j + 1
        bigb = self.BIG[:, :].bitcast(BF16)
        xm3 = bigb[:, 0:KC * T].rearrange("p (k t) -> p k t", t=T)
        dd3 = bigb[:, KC * T:2 * KC * T].rearrange("p (k t) -> p k t", t=T)
        R_xm = [Res("xm%d" % k) for k in range(KC)]
        R_dd = [Res("dd%d" % k) for k in range(KC)]
        ht3 = self.ht3
        b0 = 30 + PT
        hs3 = lambda kc: ht3[:, kc, b0:b0 + NS * 65].rearrange("p (s c) -> p s c", c=65)
        ident = self.cbv(CB_ID, 128)
        bones = self.cbv(CB_BONES, 128)
        I2 = self.cbv(CB_I2, 64)
        omk = self.OSH[:, KC * (1 + NS):KC * (1 + NS) + KC]
        self.TS("dve", omk, self.par(("ka", j), 0, KC), -1.0, 1.0, ALU.mult, ALU.add, [self.R_const], [self.R("omk")])
        for kc in range(KC):
            rr = [self.R_ht[kc], self.R_hthalo]
            eng = "dve"
            self.TT(eng, dd3[:, kc, 0:PT], ht3[:, kc, 29:29 + PT], ht3[:, kc, 30:30 + PT], ALU.subtract, rr, [R_dd[kc]])
            self.TT(eng, dd3[:, kc, PT:T].rearrange("p (s c) -> p s c", c=64), hs3(kc)[:, :, 0:64], hs3(kc)[:, :, 1:65],
                    ALU.subtract, rr, [R_dd[kc]])
        tw, xa1, xv1 = self.SQ[0], self.SQ[1], self.SQ[2]
        R_tw, R_xa1, R_xv1, R_sgl = self.R_sq[0], self.R_sq[1], self.R_sq[2], self.R("sgl")
        stg = [self.STG[:, 0:T], self.STG[:, T:2 * T]]
        Rstg = [self.R("xs0"), self.R("xs1")]
        stgi = 0
        for q in range(6):
            for kc in range(KC):
                mx = self.par(("mix", j, q), kc)
                eng = "dve"
                xs_ = xm3[:, kc, PT:T].rearrange("p (s c) -> p s c", c=64)
                ds_ = dd3[:, kc, PT:T].rearrange("p (s c) -> p s c", c=64)
                rr_ = [R_dd[kc], self.R_ht[kc], self.R_const]
                if eng == "dve":
                    self.STT(eng, xm3[:, kc, 0:PT], dd3[:, kc, 0:PT], mx, ht3[:, kc, 30:30 + PT], ALU.mult, ALU.add, rr_, [R_xm[kc]])
                    self.STT(eng, xs_, ds_, mx, hs3(kc)[:, :, 1:65], ALU.mult, ALU.add, rr_, [R_xm[kc]])
                else:
                    self.TS(eng, xm3[:, kc, 0:PT], dd3[:, kc, 0:PT], mx, None, ALU.mult, None, rr_, [R_xm[kc]])
                    self.TT(eng, xm3[:, kc, 0:PT], xm3[:, kc, 0:PT], ht3[:, kc, 30:30 + PT], ALU.add, rr_, [R_xm[kc]])
                    self.TS(eng, xs_, ds_, mx, None, ALU.mult, None, rr_, [R_xm[kc]])
                    self.TT(eng, xs_, xs_, hs3(kc)[:, :, 1:65], ALU.add, rr_, [R_xm[kc]])
            if q in (0, 2, 3):
                qq = {0: 0, 2: 1, 3: 2}[q]
                for m in range(HP):
                    ws, Rw = self.wload(self.d_rkvo, (j * 4 + qq) * HP + m)
                    sg_, Rs_ = stg[stgi % 2], Rstg[stgi % 2]
                    stgi += 1
                    for (o, n) in self.ptiles:
                        b = self.bank()
                        for kc in range(KC):
                            self.MM(self.banks[b][:, 0:n], ws[:, kc * 128:(kc + 1) * 128], xm3[:, kc, o:o + n], kc == 0, kc == KC - 1,
                                    [R_xm[kc], Rw], [self.R_bank[b]])
                        self.CP("act", sg_[:, o:o + n], self.banks[b][:, 0:n], [self.R_bank[b]], [Rs_])
                    self.DMA("sp", self.d_rkv[(j * 3 + qq) * HP + m], sg_, [Rs_], [self.R("rkv%d_%d_%d" % (j, qq, m))], "rkvst%d" % ((stgi - 1) % 2))
            if q in (1, 4, 5) or (q == 3 and j > 0):
                specs = {1: [(0, 96, tw, R_tw, AF.Tanh, 0)], 4: [(1, 96, xa1, R_xa1, AF.Identity, 0)],
                         5: [(2, 128, self.SGL, R_sgl, AF.Sigmoid, 0), (3, 128, self.SGL, R_sgl, AF.Sigmoid, T)],
                         3: [(4, 64, xv1, R_xv1, AF.Identity, 0)]}[q]
                for (row, mc, dstt, Rd, fn, co) in specs:
                    ws, Rw = self.wload(self.d_l1, j * 5 + row)
                    for (o, n) in self.ptiles:
                        b = self.bank()
                        for kc in range(KC):
                            self.MM(self.banks[b][0:mc, 0:n], ws[:, kc * mc:(kc + 1) * mc], xm3[:, kc, o:o + n], kc == 0, kc == KC - 1,
                                    [R_xm[kc], Rw], [self.R_bank[b]])
                        self.ACT(dstt[0:mc, co + o:co + o + n], self.banks[b][0:mc, 0:n], fn, [self.R_bank[b]], [Rd])
        if DBG == 1:
            return
        self.barrier()
        self.rwkv_scan(j)
        self.barrier()
        if DBG in (2, 3, 4) or DBG >= 20:
            return
        for m in range(HP):
            ws, Rw = self.wload(self.d_rkvo, (j * 4 + 3) * HP + m)
            for (o, n) in self.ptiles:
                b = self.bank()
                for kc in range(KC):
                    self.MM(self.banks[b][:, 0:n], ws[:, kc * 128:(kc + 1) * 128], ht3[:, kc, o:o + n], kc == 0, kc == KC - 1,
                            [self.R_ht[kc], Rw], [self.R_bank[b]])
                self.CP("act", self.big3[:, m, o:o + n], self.banks[b][:, 0:n], [self.R_bank[b]], [self.R_big[m]])

    def rwkv_scan(self, j):
        c = self.cfg
        KC, HP, T, PT, NS, NU, NCH, D = c.KC, c.HP, c.T, c.PT, c.NS, c.NU, c.NCH, c.D
        assert NU % 4 == 0
        NB = NU * 128
        needB = 5 * NB + NB + NU * 64 + 8 * 512
        if not hasattr(self, "arF"):
            sb0 = lambda name, cols, dt: self.stack.enter_context(self.nc.sbuf_tensor(name, [128, cols], dt))
            self.arF = self.BIG if KC >= 15 else sb0("arF", 15 * T, F32)
            self.arB = self.HT if KC * c.HC >= needB else sb0("arB", needB, BF16)
            self.RF = [Res("f%d" % i) for i in range(14)]
            self.RQ = [[Res("bd%d_%d" % (i, g)) for g in range(NU // 4)] for i in range(6)]
            self.RVS = [Res("vs%d" % g) for g in range(NU // 4)]
            self.RT8 = [Res("t8_%d" % i) for i in range(8)]
            self.RT8b = [Res("t8b_%d" % i) for i in range(8)]
            self.RT8c = [Res("t8c_%d" % i) for i in range(8)]
            self.zeroed = False
        arF, arB, RF = self.arF, self.arB, self.RF
        F = lambda i: arF[:, i * T:(i + 1) * T]
        r_, k_, v_, a_, lw_, kk_, b_, cs0, cs1, e_, Y_, tmp, vf_, bon = [F(i) for i in range(14)]
        Rr, Rk, Rv, Ra, Rlw, Rkk, Rb, Rcs0, Rcs1, Re, RY, Rtmp, Rvf, Rbon = RF
        U3 = lambda ap: ap.rearrange("p (u c) -> p u c", c=64)
        Qbd, Kbd, Pbd, Vbd, RD, EE = [arB[:, i * NB:(i + 1) * NB] for i in range(6)]
        RQ, RK, RP, RV, RRD, REE = self.RQ
        VS = arB[:, 6 * NB:6 * NB + NU * 64]
        T8 = [arB[:, 6 * NB + NU * 64 + i * 512:6 * NB + NU * 64 + (i + 1) * 512] for i in range(8)]
        A4, AT4, S0, T0, AkT4, X4, H4, PT4 = T8
        RA4, RAT4, RS0, RT0, RAk, RX, RH, RPT = self.RT8
        bd4 = lambda ap: ap.rearrange("p (u h c) -> p u h c", h=2, c=64)
        ident = self.cbv(CB_ID, 128)
        bones = self.cbv(CB_BONES, 128)
        I2 = self.cbv(CB_I2, 64)
        MST, MSTT, MG = self.cbv(CB_MST, 512), self.cbv(CB_MSTT, 512), self.cbv(CB_MG, 512)
        Rc = self.R_const
        G4 = NU // 4
        SM = self.SMALL
        smb = SM[:, 0:640].bitcast(BF16)
        SS = [smb[:, 0:128], smb[:, 128:256]]
        STbd, TTbd = smb[:, 256:384], smb[:, 384:512]
        S0st = smb[:, 512:512 + NS * 64]
        S0bd = smb[:, 768:768 + NS * 128]
        RSS = [self.R("ss0"), self.R("ss1")]
        RSTbd, RTTbd, RS0st, RS0bd = self.R("stbd"), self.R("ttbd"), self.R("s0st"), self.R("s0bd")
        s0stg = SM[:, 640:640 + NS * 64]
        owk = SM[:, 640 + NS * 64:640 + NS * 64 + (1 + NS) * 64]
        Rs0stg, Rowk = self.R("s0stg"), self.R("owk")
        if True:
            for buf, RR in ((Qbd, RQ), (Kbd, RK), (Pbd, RP), (Vbd, RV)):
                self.add("dve", lambda e, buf=buf: e.memset(buf, 0.0), [], RR)
            self.add("dve", lambda e: e.memset(STbd, 0.0), [], [RSTbd])
            self.add("dve", lambda e: e.memset(TTbd, 0.0), [], [RTTbd])
            self.add("dve", lambda e: e.memset(S0bd, 0.0), [], [RS0bd])
        allg = lambda RR: list(RR)
        bq = [0]

        def nb():
            bq[0] += 1
            return (bq[0] - 1) % 8
        pt_ = self.ptiles
        def prepA(m):
                lw2, Rlw2 = self.LW[m % 2], self.R("lw2_%d" % (m % 2))
                self.DMA("pool", lw2, self.d_l2[j * HP + m], [], [Rlw2], "lw2_%d" % (m % 2))
                for (dst, Rd, qq) in ((r_, Rr, 0), (k_, Rk, 1), (v_, Rv, 2)):
                    self.DMA("sp", dst, self.d_rkv[(j * 3 + qq) * HP + m], [self.R("rkv%d_%d_%d" % (j, qq, m))], [Rd], "ld%d" % qq)
                if j > 0:
                    self.DMA("sp", vf_, self.d_rkv[(0 * 3 + 2) * HP + m], [self.R("rkv%d_%d_%d" % (0, 2, m))], [Rvf], "ldvf")
                tw, xa1, xv1 = self.SQ[0], self.SQ[1], self.SQ[2]
                for (o, n) in pt_:
                    b = nb()
                    self.MM(self.banks[b][:, 0:n], lw2[0:96, 0:128], tw[0:96, o:o + n], True, True, [Rlw2, self.R_sq[0]], [self.R_bank[b]])
                    self.ACT(lw_[:, o:o + n], self.banks[b][:, 0:n], AF.Sigmoid, [self.R_bank[b], Rc], [Rlw], bias=self.par(("w0", j), m))
                    b = nb()
                    self.MM(self.banks[b][:, 0:n], lw2[0:96, 128:256], xa1[0:96, o:o + n], True, True, [Rlw2, self.R_sq[1]], [self.R_bank[b]])
                    self.ACT(a_[:, o:o + n], self.banks[b][:, 0:n], AF.Sigmoid, [self.R_bank[b], Rc], [Ra], bias=self.par(("a0", j), m))
                    if j > 0:
                        b = nb()
                        self.MM(self.banks[b][:, 0:n], lw2[0:64, 256:384], xv1[0:64, o:o + n], True, True, [Rlw2, self.R_sq[2]], [self.R_bank[b]])
                        self.ACT(tmp[:, o:o + n], self.banks[b][:, 0:n], AF.Sigmoid, [self.R_bank[b], Rc], [Rtmp], bias=self.par(("v0", j - 1), m))
                self.ACT(lw_, lw_, AF.Identity, [], [Rlw], scale=-0.6065306597126334)
                if j > 0:
                    self.TT("pool", vf_, vf_, v_, ALU.subtract, [Rv], [Rvf])
                    self.TT("pool", vf_, vf_, tmp, ALU.mult, [Rtmp], [Rvf])
                    self.TT("pool", v_, v_, vf_, ALU.add, [Rvf], [Rv])
                yield
                self.ACT(kk_, k_, AF.Identity, [Rk, Rc], [Rkk], scale=self.par(("kk", j), m))
                sq, Rsq = self.SQ[3], self.R_sq[3]
                self.ACT(sq[:, :], kk_, AF.Square, [Rkk], [Rsq])
                for (o, n) in pt_:
                    b = nb()
                    self.MM(self.banks[b][:, 0:n], bones, sq[:, o:o + n], True, True, [Rsq, Rc], [self.R_bank[b]])
                    self.ACT(tmp[:, o:o + n], self.banks[b][:, 0:n], AF.Sqrt, [self.R_bank[b]], [Rtmp])
                self.TS("dve", tmp, tmp, 1e-12, None, ALU.max, None, [], [Rtmp])
                self.add("dve", lambda e: e.reciprocal(out=tmp, in_=tmp), [], [Rtmp])
                self.TT("dve", kk_, kk_, tmp, ALU.mult, [Rtmp], [Rkk])
                yield
                omk = self.OSH[:, KC * (1 + NS) + m:KC * (1 + NS) + m + 1]
                self.TS("dve", tmp, a_, self.par(("ka", j), m), omk, ALU.mult, ALU.add, [Ra, Rc, self.R("omk")], [Rtmp])
                self.TT("dve", k_, k_, tmp, ALU.mult, [Rtmp], [Rk])
                self.TT("dve", b_, kk_, a_, ALU.mult, [Rkk, Ra], [Rb])
                yield
                self.STT("dve", sq[:, :], r_, self.par(("rk", j), m), k_, ALU.mult, ALU.mult, [Rr, Rk, Rc], [Rsq])
                for (o, n) in pt_:
                    b = nb()
                    self.MM(self.banks[b][:, 0:n], bones, sq[:, o:o + n], True, True, [Rsq, Rc], [self.R_bank[b]])
                    self.TT("dve", bon[:, o:o + n], self.banks[b][:, 0:n], v_[:, o:o + n], ALU.mult, [self.R_bank[b], Rv], [Rbon])
                self.DMA("sp", self.d_ycb[2 * HP + m], bon, [Rbon], [self.R("ycb2_%d" % m)], "stbon")
                src, Rs_, dst, Rd_ = lw_, Rlw, cs0, Rcs0
                for d in (1, 2, 4, 8, 16, 32):
                    self.TT("pool", U3(dst)[:, :, d:64], U3(src)[:, :, d:64], U3(src)[:, :, 0:64 - d], ALU.add, [Rs_], [Rd_])
                    self.CP("act", U3(dst)[:, :, 0:d], U3(src)[:, :, 0:d], [Rs_], [Rd_])
                    yield
                    if src is lw_:
                        src, Rs_, dst, Rd_ = cs0, Rcs0, cs1, Rcs1
                    else:
                        src, Rs_, dst, Rd_ = dst, Rd_, src, Rs_
                cs, Rcs, oth, Roth = src, Rs_, dst, Rd_
                gcb = self.SMALL[:, 1216:1216 + NU]
                Rgcb = self.R("gcb")
                self.ACT(e_, cs, AF.Exp, [Rcs], [Re])
                self.TT("pool", r_, r_, e_, ALU.mult, [Re], [Rr])
                self.CP("act", gcb.unsqueeze(2), U3(e_)[:, :, 63:64], [Re], [Rgcb])
                yield
                self.TT("dve", oth, cs, lw_, ALU.subtract, [Rcs, Rlw], [Roth])
                self.ACT(e_, oth, AF.Exp, [Roth], [Re])
                self.TT("pool", kk_, kk_, e_, ALU.mult, [Re], [Rkk])
                yield
                self.ACT(e_, cs, AF.Exp, [Rcs], [Re], scale=-1.0)
                self.TT("dve", b_, b_, e_, ALU.mult, [Re], [Rb])
                self.TT("pool", k_, k_, e_, ALU.mult, [Re], [Rk])
                yield

        def prepB(m):
            RD3 = RD.rearrange("p (u n) -> p u n", n=128)
            gcb = self.SMALL[:, 1216:1216 + NU]
            Rgcb = self.R("gcb")
            self.CP("act", RD3[:, :, 0:64], U3(r_), [Rr], allg(RRD))
            self.TT("dve", RD3[:, :, 64:128], I2.unsqueeze(1).broadcast_to([128, NU, 64]),
                    gcb.unsqueeze(2).broadcast_to([128, NU, 64]), ALU.mult, [Rgcb, Rc], allg(RRD))
            for h in range(2):
                ps = slice(64 * h, 64 * h + 64)
                self.CP("act" if h else "dve", bd4(Pbd)[ps, :, h, :], U3(kk_)[ps], [Rkk], allg(RP))
                self.CP("act" if h else "dve", bd4(Qbd)[ps, :, h, :], U3(b_)[ps], [Rb], allg(RQ))
                self.CP("act" if h else "dve", bd4(Kbd)[ps, :, h, :], U3(k_)[ps], [Rk], allg(RK))
                self.CP("pool" if h else "act", bd4(Vbd)[ps, :, h, :], U3(v_)[ps], [Rv], allg(RV))

        def groups(m):
                def group_steps(g, T8s, RT8s):
                    A4, AT4, S0, T0, AkT4, X4, H4, PT4 = T8s
                    RA4, RAT4, RS0, RT0, RAk, RX, RH, RPT = RT8s
                    us = list(range(4 * g, 4 * g + 4))

                    def mmu(lbuf, Rl, rfn, Rr_, oc=128, lfn=None):
                        b = nb()
                        for ui, u in enumerate(us):
                            l = lbuf[:, u * 128:(u + 1) * 128] if lfn is None else lfn(ui)
                            self.MM(self.banks[b][:, ui * oc:(ui + 1) * oc], l, rfn(ui, u), True, True, list(Rl) + list(Rr_), [self.R_bank[b]])
                        return b
                    ub = lambda buf: (lambda ui, u: buf[:, u * 128:(u + 1) * 128])
                    t4 = lambda buf: (lambda ui, u=None: buf[:, ui * 128:(ui + 1) * 128])
                    cst = lambda ap: (lambda ui, u: ap)
                    b = mmu(Qbd, [RQ[g]], ub(Pbd), [RP[g]])
                    self.TT("dve", A4, self.banks[b][:, :], MST, ALU.mult, [self.R_bank[b], Rc], [RA4])
                    b = mmu(Pbd, [RP[g]], ub(Qbd), [RQ[g]])
                    self.TT("dve", AT4, self.banks[b][:, :], MSTT, ALU.mult, [self.R_bank[b], Rc], [RAT4])
                    yield
                    b = mmu(Pbd, [RP[g]], ub(Kbd), [RK[g]])
                    self.TT("dve", AkT4, self.banks[b][:, :], MSTT, ALU.mult, [self.R_bank[b], Rc], [RAk])
                    b = mmu(Qbd, [RQ[g]], ub(RD), [RRD[g]])
                    self.TT("dve", X4, self.banks[b][:, :], MG, ALU.mult, [self.R_bank[b], Rc], [RX])
                    yield
                    b = mmu(Kbd, [RK[g]], ub(RD), [RRD[g]])
                    self.TT("dve", H4, self.banks[b][:, :], MG, ALU.mult, [self.R_bank[b], Rc], [RH])
                    b = mmu(Pbd, [RP[g]], cst(ident), [Rc])
                    self.CP("act", PT4, self.banks[b][:, :], [self.R_bank[b]], [RPT])
                    yield
                    b2 = mmu(Vbd, [RV[g]], cst(I2), [Rc], oc=64)
                    b = mmu(Vbd, [RV[g]], cst(ident), [Rc])
                    self.CP("act", VS[:, 4 * g * 64:(4 * g + 4) * 64], self.banks[b2][:, 0:256], [self.R_bank[b2]], [self.RVS[g]])
                    self.CP("act", Vbd[:, 4 * g * 128:(4 * g + 4) * 128], self.banks[b][:, :], [self.R_bank[b]], [RV[g]])
                    yield
                    b = mmu(None, [RAT4], t4(X4), [RX], lfn=t4(AT4))
                    self.TT("dve", X4, X4, self.banks[b][:, :], ALU.subtract, [self.R_bank[b]], [RX])
                    yield
                    Pc, RPc, PTc, RPTc = A4, RA4, AT4, RAT4
                    Pn, RPn, PTn, RPTn = S0, RS0, T0, RT0
                    for lvl in range(5):
                        if lvl < 4:
                            b = mmu(None, [RPTc], t4(Pc), [RPc], lfn=t4(PTc))
                            self.CP("act", Pn, self.banks[b][:, :], [self.R_bank[b]], [RPn])
                        b = mmu(None, [RPc], t4(PTc), [RPTc], lfn=t4(Pc))
                        self.CP("act" if lvl % 2 else "dve", PTn, self.banks[b][:, :], [self.R_bank[b]], [RPTn])
                        yield
                        b = mmu(None, [RPTn], t4(X4), [RX], lfn=t4(PTn))
                        self.TT("dve", X4, X4, self.banks[b][:, :], ALU.add, [self.R_bank[b]], [RX])
                        yield
                        Pc, RPc, PTc, RPTc, Pn, RPn, PTn, RPTn = Pn, RPn, PTn, RPTn, Pc, RPc, PTc, RPTc
                    b = mmu(None, [RAk], t4(X4), [RX], lfn=t4(AkT4))
                    gs = slice(4 * g * 128, (4 * g + 4) * 128)
                    self.TT("dve", EE[:, gs], H4, self.banks[b][:, :], ALU.subtract, [self.R_bank[b], RH], [REE[g]])
                    b = mmu(None, [RPT], t4(X4), [RX], lfn=t4(PT4))
                    self.TT("dve", RD[:, gs], RD[:, gs], self.banks[b][:, :], ALU.subtract, [self.R_bank[b]], [RRD[g]])
                    yield
                    RDg = RD[:, gs].rearrange("p (u n) -> p u n", n=128)
                    EEg = EE[:, gs].rearrange("p (u n) -> p u n", n=128)
                    for h in range(2):
                        ps = slice(64 * h, 64 * h + 64)
                        self.CP("act", bd4(Qbd[:, gs])[ps, :, h, :], RDg[ps, :, 64:128], [RRD[g]], [RQ[g]])
                        self.CP("act" if h else "dve", bd4(Kbd[:, gs])[ps, :, h, :], EEg[ps, :, 64:128], [REE[g]], [RK[g]])

                T8b = [self.AR2[:, i * 512:(i + 1) * 512] for i in range(8)]
                NW = 3 if self.cfg.KC * 128 >= 2048 else 2
                sets = [(T8, self.RT8), (T8b, self.RT8b)]
                if NW == 3:
                    sets.append(([self.WS[i // 4][:, (i % 4) * 512:(i % 4 + 1) * 512] for i in range(8)], self.RT8c))
                for g0 in range(0, G4, NW):
                    gens = [group_steps(g, *sets[(g - g0) % NW]) for g in range(g0, min(g0 + NW, G4))]
                    while gens:
                        for gen in list(gens):
                            try:
                                next(gen)
                            except StopIteration:
                                gens.remove(gen)
                        yield

        def seqpass(m):
                M1u = lambda u: RD[:, u * 128:u * 128 + 64]
                Eu = lambda u: EE[:, u * 128:u * 128 + 64]
                Msb = lambda u: Qbd[:, u * 128:(u + 1) * 128]
                Esb = lambda u: Kbd[:, u * 128:(u + 1) * 128]
                VTb = lambda u: Vbd[:, u * 128:(u + 1) * 128]
                VSu = lambda u: VS[:, u * 64:(u + 1) * 64]
                self.add("dve", lambda e: e.memset(SS[0][:, 0:64], 0.0), [], [RSS[0]])
                self.CP("dve", SS[0][:, 64:128], I2, [Rc], [RSS[0]])
                for h in range(2):
                    ps = slice(64 * h, 64 * h + 64)
                    self.add("dve", lambda e, ps=ps, h=h: e.memset(STbd[ps, 64 * h:64 * h + 64], 0.0), [], [RSTbd])
                    self.CP("dve", TTbd[ps, 64 * h:64 * h + 64], I2[ps, :], [Rc], [RTTbd])
                cf = F(14)
                Rcf_ = self.R("f14")
                for cgrp in range(0, NCH, 4):
                    by, bc = nb(), nb()
                    cl = list(range(cgrp, min(cgrp + 4, NCH)))
                    for ci, ch in enumerate(cl):
                        g = ch // 4
                        cur, nxt = SS[ch % 2], SS[(ch + 1) % 2]
                        Rcur, Rnxt = RSS[ch % 2], RSS[(ch + 1) % 2]
                        cs_ = slice(ci * 64, ci * 64 + 64)
                        bs = nb()
                        self.MM(self.banks[bs][:, 0:64], Msb(ch), cur[:, 0:64], True, False, [RQ[g], Rcur], [self.R_bank[bs]])
                        self.MM(self.banks[bs][:, 0:64], Esb(ch), VSu(ch), False, True, [RK[g], self.RVS[g]], [self.R_bank[bs]])
                        self.MM(self.banks[bs][:, 64:128], Msb(ch), cur[:, 64:128], True, True, [RQ[g], Rcur], [self.R_bank[bs]])
                        self.CP("dve", nxt, self.banks[bs][:, 0:128], [self.R_bank[bs]], [Rnxt])
                        self.MM(self.banks[by][:, cs_], STbd, M1u(ch), True, False, [RSTbd, RRD[g]], [self.R_bank[by]])
                        self.MM(self.banks[by][:, cs_], VTb(ch), Eu(ch), False, True, [RV[g], REE[g]], [self.R_bank[by]])
                        self.MM(self.banks[bc][:, cs_], TTbd, M1u(ch), True, True, [RTTbd, RRD[g]], [self.R_bank[bc]])
                        if ch == NCH - 1:
                            self.CP("dve", self.SEGB[:, m * 128:m * 128 + 64], self.banks[bs][:, 0:64], [self.R_bank[bs]], [self.R("segb")])
                        for h in range(2):
                            ps = slice(64 * h, 64 * h + 64)
                            self.CP("act", STbd[ps, 64 * h:64 * h + 64], nxt[ps, 0:64], [Rnxt], [RSTbd])
                            self.CP("act", TTbd[ps, 64 * h:64 * h + 64], nxt[ps, 64:128], [Rnxt], [RTTbd])
                        yield
                    w = len(cl) * 64
                    self.CP("act", Y_[:, cgrp * 64:cgrp * 64 + w], self.banks[by][:, 0:w], [self.R_bank[by]], [RY])
                    self.CP("dve", cf[:, cgrp * 64:cgrp * 64 + w], self.banks[bc][:, 0:w], [self.R_bank[bc]], [Rcf_])
                b = nb()
                self.MM(self.banks[b][:, 0:64], TTbd, I2, True, True, [RTTbd, Rc], [self.R_bank[b]])
                self.CP("act", self.SEGB[:, m * 128 + 64:m * 128 + 128], self.banks[b][:, 0:64], [self.R_bank[b]], [self.R("segb")])
                self.DMA("sp", self.d_ycb[HP + m, :, 0:PT], cf[:, 0:PT], [Rcf_], [self.R("ycb1_%d" % m)], "stcf")
                so = ((j * HP + m) * NS) * 64
                self.DMA("sp", s0stg, self.d_stwkv[:, so:so + NS * 64], [], [Rs0stg], "s0ld")
                self.CP("act", S0st, s0stg, [Rs0stg], [RS0st])
                S0bd4 = S0bd.rearrange("p (s h c) -> p s h c", h=2, c=64)
                for h in range(2):
                    ps = slice(64 * h, 64 * h + 64)
                    self.CP("dve", S0bd4[ps, :, h, :], S0st[ps, :].rearrange("p (s c) -> p s c", c=64), [RS0st], [RS0bd])
                by, bs = nb(), nb()
                for s_ in range(NS):
                    u = NCH + s_
                    g = u // 4
                    cs_ = slice(s_ * 64, s_ * 64 + 64)
                    self.MM(self.banks[by][:, cs_], S0bd[:, s_ * 128:(s_ + 1) * 128], M1u(u), True, False, [RS0bd, RRD[g]], [self.R_bank[by]])
                    self.MM(self.banks[by][:, cs_], VTb(u), Eu(u), False, True, [RV[g], REE[g]], [self.R_bank[by]])
                    self.MM(self.banks[bs][:, cs_], Msb(u), S0st[:, cs_], True, False, [RQ[g], RS0st], [self.R_bank[bs]])
                    self.MM(self.banks[bs][:, cs_], Esb(u), VSu(u), False, True, [RK[g], self.RVS[g]], [self.R_bank[bs]])
                self.CP("act", Y_[:, PT:T], self.banks[by][:, 0:NS * 64], [self.R_bank[by]], [RY])
                self.CP("dve", owk[:, 64:(1 + NS) * 64], self.banks[bs][:, 0:NS * 64], [self.R_bank[bs]], [Rowk])
                oo = ((j * HP + m) * (1 + NS) + 1) * 64
                self.DMA("sp", self.d_owkv[:, oo:oo + NS * 64], owk[:, 64:(1 + NS) * 64], [Rowk], [self.R("d_owkv")], "stowk")
                self.DMA("sp", self.d_ycb[m], Y_, [RY], [self.R("ycb0_%d" % m)], "sty")

        def drain(gen):
            if gen is not None:
                for _ in gen:
                    pass

        def step(gen):
            if gen is None:
                return None
            try:
                next(gen)
                return gen
            except StopIteration:
                return None
        drain(prepA(0))
        for m in range(HP):
            prepB(m)
            nxt_prep = prepA(m + 1) if m + 1 < HP else None
            for _ in groups(m):
                nxt_prep = step(nxt_prep)
            for _ in seqpass(m):
                nxt_prep = step(nxt_prep)
            drain(nxt_prep)
        self.zeroed = True
        if DBG == 2 or DBG >= 20:
            return
        hq = HP // self.NQ
        for q in range(self.NQ):
            Rs1, Rs2 = self.R("seg_in%d" % q), self.R("seg_out%d" % q)
            self.DMA("sp", self.t_sgin[q].ap(), self.SEGB[:, q * hq * 128:(q + 1) * hq * 128], [self.R("segb")], [Rs1], "segst")
            self.allgather8(self.t_sgin[q], self.t_sgmid[q], self.t_sgout[q], Rs1, Rs2)
        if DBG == 3:
            return
        self.barrier()
        self.rwkv_finish(j)

    def rwkv_finish(self, j):
        c = self.cfg
        KC, HP, T, PT, NS, NU, NCH, D = c.KC, c.HP, c.T, c.PT, c.NS, c.NU, c.NCH, c.D
        arF, RF = self.arF, self.RF
        F = lambda i: arF[:, i * T:(i + 1) * T]
        slotsets = [(10, 13, 11, 7), (0, 1, 2, 3)]
        A2 = self.AR2
        A2f = A2[:, 0:10240].bitcast(F32)
        SG = A2f[:, 0:1024]
        SG3 = SG.rearrange("p (r n) -> p r n", n=128)
        Tbd8 = A2[:, 2048:3072]
        Pl = A2f[:, 1536:2112]
        Pb = [A2[:, 4224:4288], A2[:, 4288:4352]]
        Sst, Sbd = A2[:, 4352:4416], A2[:, 4416:4544]
        Ssel = A2f[:, 2304:2432]
        Ceffb = A2[:, 4864:4864 + PT]
        RSG, RT8, RPl, RSst, RSbd, RSsel, RCb = (self.R(n) for n in ("c_sg", "c_t8", "c_pl", "c_sst", "c_sbd", "c_ssel", "c_cb"))
        RPb = [self.R("c_pb0"), self.R("c_pb1")]
        bones = self.cbv(CB_BONES, 128)
        Rc = self.R_const
        SM = self.SMALL
        owk = SM[:, 640 + NS * 64:640 + NS * 64 + (1 + NS) * 64]
        Rowk = self.R("owk")
        hq = HP // self.NQ
        self.add("dve", lambda e: e.memset(Tbd8, 0.0), [], [RT8])
        self.add("dve", lambda e: e.memset(Sbd, 0.0), [], [RSbd])
        self.add("dve", lambda e: e.memset(Pb[0], 0.0), [], [RPb[0]])
        self.add("dve", lambda e: e.memset(Pl[:, 0:64], 0.0), [], [RPl])
        T84 = Tbd8.rearrange("p (r h c) -> p r h c", h=2, c=64)
        seg3 = [t.ap().rearrange("(r p) f -> p r f", p=128) for t in self.t_sgout]
        yb, ysq = self.SQ[0], self.SQ[1]
        Ryb, Rysq = self.R_sq[0], self.R_sq[1]
        bq = [0]

        def nb():
            bq[0] += 1
            return (bq[0] - 1) % 8
        def stage1(m):
            sY, sB, sC, sT = slotsets[m % 2]
            Y_, bon, cfst, t1 = F(sY), F(sB), F(sC), F(sT)
            RY, Rbon, Rcf, Rt1 = RF[sY], RF[sB], RF[sC], RF[sT]
            lw2, Rlw2 = self.LW[m % 2], self.R("lw2_%d" % (m % 2))
            lw2, Rlw2 = self.LW[m % 2], self.R("lw2_%d" % (m % 2))
            self.DMA("pool", lw2, self.d_l2[j * HP + m], [], [Rlw2], "lw2_%d" % (m % 2))
            self.DMA("sp", SG3, seg3[m // hq][:, :, (m % hq) * 128:(m % hq + 1) * 128], [self.R("seg_out%d" % (m // hq))], [RSG], "c_sg")
            self.DMA("sp", Y_, self.d_ycb[m], [self.R("ycb0_%d" % m)], [RY], "c_y%d" % (m % 2))
            self.DMA("sp", cfst[:, 0:PT], self.d_ycb[HP + m, :, 0:PT], [self.R("ycb1_%d" % m)], [Rcf], "c_cf%d" % (m % 2))
            self.DMA("sp", bon, self.d_ycb[2 * HP + m], [self.R("ycb2_%d" % m)], [Rbon], "c_bon%d" % (m % 2))
            for h in range(2):
                ps = slice(64 * h, 64 * h + 64)
                self.CP("act" if h else "dve", T84[ps, :, h, :], SG3[ps, :, 64:128], [RSG], [RT8])
            for r in range(NCORES):
                b = nb()
                cur, nxt = r % 2, (r + 1) % 2
                self.MM(self.banks[b][:, 0:64], Tbd8[:, r * 128:(r + 1) * 128], Pb[cur], True, True, [RT8, RPb[cur]], [self.R_bank[b]])
                self.TT("dve", Pb[nxt], self.banks[b][:, 0:64], SG3[:, r, 0:64], ALU.add, [self.R_bank[b], RSG], [RPb[nxt]])
                self.TT("dve", Pl[:, (r + 1) * 64:(r + 2) * 64], self.banks[b][:, 0:64], SG3[:, r, 0:64], ALU.add, [self.R_bank[b], RSG], [RPl])
                yield
            for k_, base in ((0, 0), (1, 1)):
                dst = Ssel[:, k_ * 64:(k_ + 1) * 64]
                for r in range(NCORES):
                    src = Pl[:, (r + base) * 64:(r + base + 1) * 64]
                    sc = self.SEL[:, 8 + r:9 + r]
                    if r == 0:
                        self.TS("dve", dst, src, sc, None, ALU.mult, None, [RPl, Rc], [RSsel])
                    else:
                        self.STT("dve", dst, src, sc, dst, ALU.mult, ALU.add, [RPl, Rc], [RSsel])
            self.CP("act", owk[:, 0:64], Ssel[:, 64:128], [RSsel], [Rowk])
            oo = ((j * HP + m) * (1 + NS)) * 64
            self.DMA("sp", self.d_owkv[:, oo:oo + 64], owk[:, 0:64], [Rowk], [self.R("d_owkv")], "stowk")
            self.CP("act", Sst, Ssel[:, 0:64], [RSsel], [RSst])
            for h in range(2):
                ps = slice(64 * h, 64 * h + 64)
                self.CP("dve", Sbd[ps, 64 * h:64 * h + 64], Sst[ps, :], [RSst], [RSbd])
            self.CP("act", Ceffb, cfst[:, 0:PT], [Rcf], [RCb])
            for (o, n) in [(o, min(512, PT - o)) for o in range(0, PT, 512)]:
                b = nb()
                self.MM(self.banks[b][:, 0:n], Sbd, Ceffb[:, o:o + n], True, True, [RSbd, RCb], [self.R_bank[b]])
                self.TT("dve", Y_[:, o:o + n], Y_[:, o:o + n], self.banks[b][:, 0:n], ALU.add, [self.R_bank[b]], [RY])
            yield

        def stage2(m):
            sY, sB, sC, sT = slotsets[m % 2]
            Y_, bon, cfst, t1 = F(sY), F(sB), F(sC), F(sT)
            RY, Rbon, Rcf, Rt1 = RF[sY], RF[sB], RF[sC], RF[sT]
            lw2, Rlw2 = self.LW[m % 2], self.R("lw2_%d" % (m % 2))
            self.CP("act", yb[:, :], Y_, [RY], [Ryb])
            self.ACT(ysq[:, :], Y_, AF.Square, [RY], [Rysq])
            mean, rstd = self.mean, self.rstd
            for (o, n) in self.ptiles:
                b1, b2 = nb(), nb()
                self.MM(self.banks[b1][:, 0:n], bones, yb[:, o:o + n], True, True, [Ryb, Rc], [self.R_bank[b1]])
                self.MM(self.banks[b2][:, 0:n], bones, ysq[:, o:o + n], True, True, [Rysq, Rc], [self.R_bank[b2]])
                mn, rs = mean[:, o:o + n], rstd[:, o:o + n]
                self.TS("dve", mn, self.banks[b1][:, 0:n], 1.0 / 64, None, ALU.mult, None, [self.R_bank[b1]], [self.R_stat])
                self.TT("dve", rs, mn, mn, ALU.mult, [], [self.R_stat])
                self.STT("dve", rs, self.banks[b2][:, 0:n], 1.0 / 64, rs, ALU.mult, ALU.subtract, [self.R_bank[b2]], [self.R_stat])
                self.ACT(rs, rs, AF.Sqrt, [Rc], [self.R_stat], bias=self.epsc(GN_EPS))
                self.add("dve", lambda e, rs=rs: e.reciprocal(out=rs, in_=rs), [], [self.R_stat])
                yield
            self.TT("dve", t1, Y_, mean, ALU.subtract, [RY, self.R_stat], [Rt1])
            self.TT("dve", t1, t1, rstd, ALU.mult, [self.R_stat], [Rt1])
            yield
            self.TS("dve", t1, t1, self.par(("rlng", j), m), self.par(("rlnb", j), m), ALU.mult, ALU.add, [Rc], [Rt1])
            self.TT("dve", t1, t1, bon, ALU.add, [Rbon], [Rt1])
            yield
            for (o, n) in self.ptiles:
                b = nb()
                self.MM(self.banks[b][:, 0:n], lw2[:, 384:512], self.SGL[:, o:o + n], True, False, [Rlw2, self.R("sgl")], [self.R_bank[b]])
                self.MM(self.banks[b][:, 0:n], lw2[:, 512:640], self.SGL[:, T + o:T + o + n], False, True, [Rlw2, self.R("sgl")], [self.R_bank[b]])
                self.TT("dve", self.ht3[:, m, o:o + n], t1[:, o:o + n], self.banks[b][:, 0:n], ALU.mult, [self.R_bank[b], Rt1],
                        [self.R_ht[m], self.R_hthalo])

            yield

        def step(gen):
            if gen is None:
                return None
            try:
                next(gen)
                return gen
            except StopIteration:
                return None
        g1 = stage1(0)
        while g1 is not None:
            g1 = step(g1)
        for m in range(HP):
            g2 = stage2(m)
            g1 = stage1(m + 1) if m + 1 < HP else None
            while g1 is not None or g2 is not None:
                g1 = step(g1)
                g2 = step(g2)

    def rwkv_pre(self, j):
        self.halo_recv()
        c = self.cfg
        KC, PT, NS = c.KC, c.PT, c.NS
        stg = self.SMALL[:, 0:KC * NS]
        Rst = self.R("shst")
        self.DMA("sp", stg, self.d_stshift[:, j * KC * NS:(j + 1) * KC * NS], [], [Rst], "shld")
        b0 = 30 + PT
        for kc in range(KC):
            dst = self.ht3[:, kc, b0:b0 + NS * 65].rearrange("p (s c) -> p s c", c=65)[:, :, 0:1]
            self.CP("act", dst, stg[:, kc * NS:(kc + 1) * NS].rearrange("p (s o) -> p s o", o=1), [Rst], [self.R_ht[kc]])
        n = KC * (1 + NS)
        self.DMA("sp", self.d_oshift[:, j * n:(j + 1) * n], self.OSH[:, 0:n], [self.R("osh")], [self.R("d_oshift")], "stosh")

    def hlast(self, kc, src, g):
        c = self.cfg
        PT, NS, T = c.PT, c.NS, c.T
        o = kc * (1 + NS)
        self.STT("dve", self.OSH[:, o:o + 1], src[:, PT - 1:PT], g, self.rstd[:, PT - 1:PT], ALU.mult, ALU.mult,
                 [self.R_big[kc], self.R_stat], [self.R("osh")])
        s3 = src[:, PT:T].rearrange("p (s c) -> p s c", c=64)[:, :, 63:64]
        r3 = self.rstd[:, PT:T].rearrange("p (s c) -> p s c", c=64)[:, :, 63:64]
        self.STT("dve", self.OSH[:, o + 1:o + 1 + NS].rearrange("p (s o) -> p s o", o=1), s3, g, r3, ALU.mult, ALU.mult,
                 [self.R_big[kc], self.R_stat], [self.R("osh")])

    def build(self, sublayers):
        c = self.cfg
        KC, T = c.KC, c.T
        self.DMA("sp", self.PAR[:, :], self.d_par[:, :], [], [self.R_const], "cst")
        self.DMA("sp", self.CB[:, :], self.d_cb[:, :], [], [self.R_const], "cst")
        self.DMA("sp", self.SEL[:, :], self.d_sel[:, :], [], [self.R_const], "cst")
        for i_, v_ in enumerate((RMS_EPS, LN_EPS, GN_EPS, 1e-24)):
            self.add("dve", lambda e, i_=i_, v_=v_: e.memset(self.EPS[:, i_:i_ + 1], v_), [], [self.R_const])
        for kc in range(KC):
            self.DMA("sp", self.big3[:, kc, :], self.d_x[:, kc * T:(kc + 1) * T], [], [self.R_big[kc]], "xin")
        if sublayers is None:
            sublayers = []
            for i in range(c.depth):
                sublayers.append(("mix", i))
                sublayers.append(("ffn", i))
        for (kind, i) in sublayers:
            self.barrier()
            if kind == "mix":
                if i % 2 == 1:
                    self.rwkv_alloc()
                    self.norm_in(("ng", i, 0), hlast=self.hlast)
                    self.rwkv_pre(i // 2)
                    self.barrier()
                    self.rwkv(i // 2)
                else:
                    self.norm_in(("ng", i, 0))
                    self.conformer(i // 2)
                self.barrier()
                self.resid(("ng", i, 1))
            else:
                self.norm_in(("ng", i, 2))
                self.ffn(i)
                self.barrier()
                self.resid(("ng", i, 3))
        for kc in range(KC):
            self.DMA("sp", self.d_y[:, kc * T:(kc + 1) * T], self.big3[:, kc, :], [self.R_big[kc]], [self.R("d_y")], "yout")
        self.p.emit()
        self.stack.close()


def fm_tokens(x):
    x = np.asarray(x, np.float32)
    t, d = x.shape
    return np.ascontiguousarray(x.reshape(t, d // 128, 128).transpose(2, 1, 0))


def prep_shared(cfg, inp):
    c = cfg
    sh = {}
    sh["par"] = pack_params(cfg, inp)
    sh["cb"] = const_bf16(cfg)
    sh["w_pw1"] = np.concatenate([wl(inp["conv_pw1_w"][j]) for j in range(c.NCL)], 0)
    sh["w_pw2"] = np.concatenate([wl(inp["conv_pw2_w"][j]) for j in range(c.NCL)], 0)
    sh["w_in"] = np.concatenate([wl(inp["ffn_w_in"][i]) for i in range(c.depth)], 0)
    wo = []
    for i in range(c.depth):
        w = np.asarray(inp["ffn_w_out"][i], np.float32)
        wo.append(np.ascontiguousarray(w.reshape(c.FC // c.G, c.G, 128, c.D).transpose(0, 2, 1, 3)).reshape(c.FC // c.G, 128, c.G * c.D))
    sh["w_out"] = np.concatenate(wo, 0)
    return sh


def prep_core(cfg, inp, core):
    c = cfg
    m = {}
    xp = np.asarray(inp["x_prompt"], np.float32)[0, core * c.PT:(core + 1) * c.PT]
    xs = np.asarray(inp["x_sample"], np.float32)[core * c.NS:(core + 1) * c.NS].reshape(c.NS * c.SL, c.D)
    m["xT"] = fm_tokens(np.concatenate([xp, xs], 0)).reshape(128, -1)
    sel = np.zeros((128, 17), np.float32)
    if core > 0:
        sel[:, core - 1] = 1.0
        sel[:, 16] = 1.0
    sel[:, 8 + core] = 1.0
    m["sel"] = sel
    sq = slice(core * c.NS, (core + 1) * c.NS)
    sc = np.asarray(inp["state_conv_mix"], np.float32)[:, sq]
    m["st_conv"] = np.ascontiguousarray(sc.reshape(c.NCL, c.NS, 30, c.KC, 128).transpose(4, 0, 3, 1, 2)).reshape(128, -1)
    sf = np.asarray(inp["state_ffn_conv"], np.float32)[:, sq]
    m["st_ffn"] = np.ascontiguousarray(sf.reshape(c.depth, c.NS, 2, c.FC, 128).transpose(4, 0, 3, 1, 2)).reshape(128, -1)
    return m


def assemble(cfg, results):
    c = cfg
    KC, FC, T, PT, NS, SL, D = c.KC, c.FC, c.T, c.PT, c.NS, c.SL, c.D
    yp, ys = [], []
    for r in results:
        y = np.asarray(r["yT"]).reshape(128, KC, T).transpose(2, 1, 0).reshape(T, D)
        yp.append(y[:PT])
        ys.append(y[PT:].reshape(NS, SL, D))
    y_prompt = np.concatenate(yp, 0)[None]
    y_sample = np.concatenate(ys, 0)

    def st(name, nl, nch, w):
        arrs = [np.asarray(r[name]).reshape(128, nl, nch, 1 + NS, w) for r in results]
        p = arrs[-1][:, :, :, 0, :].transpose(1, 3, 2, 0).reshape(nl, 1, w, nch * 128)
        s = np.concatenate([a[:, :, :, 1:, :].transpose(1, 3, 4, 2, 0).reshape(nl, NS, w, nch * 128) for a in arrs], 1)
        return np.ascontiguousarray(p), np.ascontiguousarray(s)
    conv_p, conv_s = st("o_conv", c.NCL, KC, 30)
    ffn_p, ffn_s = st("o_ffn", c.depth, FC, 2)
    outs = [y_prompt, y_sample, conv_p, conv_s, None, None, None, None, ffn_p, ffn_s]
    if c.NRL > 0 and "o_shift" in results[0]:
        outs[4:8] = assemble_rwkv(cfg, results)
    return tuple(outs)


def prep_shared_rwkv(cfg, inp):
    c = cfg
    if c.NRL == 0:
        return {}
    sh = {}
    rk = []
    for j in range(c.NRL):
        for nm in ("rwkv_w_r", "rwkv_w_k", "rwkv_w_v", "rwkv_w_o"):
            rk.append(wl(inp[nm][j]))
    sh["w_rkvo"] = np.concatenate(rk, 0)
    W = c.KC * 128
    l1 = np.zeros((c.NRL * 5, 128, W), np.float32)
    l2 = np.zeros((c.NRL * c.HP, 128, 5 * 128), np.float32)
    for j in range(c.NRL):
        l1[j * 5 + 0, :, :c.KC * 96] = wl(inp["rwkv_w1"][j], 96)[0]
        l1[j * 5 + 1, :, :c.KC * 96] = wl(inp["rwkv_a1"][j], 96)[0]
        g1 = wl(inp["rwkv_g1"][j], 128)
        l1[j * 5 + 2] = g1[0]
        l1[j * 5 + 3] = g1[1]
        if j > 0:
            l1[j * 5 + 4, :, :c.KC * 64] = wl(inp["rwkv_v1"][j - 1], 64)[0]
        w2 = np.asarray(inp["rwkv_w2"][j], np.float32)
        a2 = np.asarray(inp["rwkv_a2"][j], np.float32)
        g2 = np.asarray(inp["rwkv_g2"][j], np.float32)
        for m in range(c.HP):
            cs = slice(m * 128, (m + 1) * 128)
            l2[j * c.HP + m, 0:96, 0:128] = w2[:, cs]
            l2[j * c.HP + m, 0:96, 128:256] = a2[:, cs]
            if j > 0:
                l2[j * c.HP + m, 0:64, 256:384] = np.asarray(inp["rwkv_v2"][j - 1], np.float32)[:, cs]
            l2[j * c.HP + m, :, 384:512] = g2[0:128, cs]
            l2[j * c.HP + m, :, 512:640] = g2[128:256, cs]
    sh["w_l1"] = l1
    sh["w_l2"] = l2
    return sh


def prep_core_rwkv(cfg, inp, core):
    c = cfg
    if c.NRL == 0:
        return {}
    m = {}
    sq = slice(core * c.NS, (core + 1) * c.NS)
    ss = np.asarray(inp["state_rwkv_shift"], np.float32)[:, sq]
    m["st_shift"] = np.ascontiguousarray(ss.reshape(c.NRL, c.NS, c.KC, 128).transpose(3, 0, 2, 1)).reshape(128, -1)
    sw = np.asarray(inp["state_rwkv_wkv"], np.float32)[:, sq]
    sw = sw.reshape(c.NRL, c.NS, c.HP, 2, 64, 64)
    m["st_wkv"] = np.ascontiguousarray(sw.transpose(3, 5, 0, 2, 1, 4)).reshape(128, -1)
    return m


def assemble_rwkv(cfg, results):
    c = cfg
    KC, HP, NS, D = c.KC, c.HP, c.NS, c.D
    sh = [np.asarray(r["o_shift"]).reshape(128, c.NRL, KC, 1 + NS) for r in results]
    shift_p = sh[-1][:, :, :, 0].transpose(1, 2, 0).reshape(c.NRL, 1, D)
    shift_s = np.concatenate([a[:, :, :, 1:].transpose(1, 3, 2, 0).reshape(c.NRL, NS, D) for a in sh], 1)
    wk = [np.asarray(r["o_wkv"]).reshape(2, 64, c.NRL, HP, 1 + NS, 64) for r in results]
    tr = lambda a: a.transpose(2, 4, 3, 0, 5, 1).reshape(c.NRL, a.shape[4], HP * 2, 64, 64)
    wkv_p = tr(wk[-1][:, :, :, :, 0:1, :])
    wkv_s = np.concatenate([tr(a[:, :, :, :, 1:, :]) for a in wk], 1)
    return [np.ascontiguousarray(x) for x in (shift_p, shift_s, wkv_p, wkv_s)]


_CACHE = {}


def kernel(**inputs):
    cfg = Cfg()
    if "b" not in _CACHE:
        _CACHE["b"] = B(cfg)
    b = _CACHE["b"]
    inp = {k: np.asarray(v) for k, v in inputs.items()}
    sh = prep_shared(cfg, inp)
    sh.update(prep_shared_rwkv(cfg, inp))
    maps = []
    for c in range(NCORES):
        m = dict(sh)
        m.update(prep_core(cfg, inp, c))
        m.update(prep_core_rwkv(cfg, inp, c))
        maps.append(m)
    res = run_bass_kernel_spmd(b.nc, maps, core_ids=list(range(NCORES)))
    outs = assemble(cfg, [r for r in res.results])
    return tuple(np.ascontiguousarray(o, dtype=np.float32) for o in outs)
```

```python
import contextlib
import numpy as np
import ml_dtypes
import concourse.bass as bass
import concourse.mybir as mybir
from concourse.bass_utils import run_bass_kernel_spmd

F32 = mybir.dt.float32
BF16 = mybir.dt.bfloat16
AF = mybir.ActivationFunctionType
ALU = mybir.AluOpType
AX = mybir.AxisListType

NCORES = 8
DBG = 0
RMS_EPS = 1e-6
LN_EPS = 1e-5
GN_EPS = 64e-5


class Cfg:
    def __init__(self, D=2048, DFF=5632, PT=1024, NS=4, SL=64, depth=4, G=2):
        self.D, self.DFF, self.PT, self.NS, self.SL, self.depth, self.G = D, DFF, PT, NS, SL, depth, G
        self.KC = D // 128
        self.FC = DFF // 128
        self.HP = D // 128
        self.T = PT + NS * SL
        self.NCL = (depth + 1) // 2
        self.NRL = depth // 2
        self.NVL = max(self.NRL - 1, 0)
        self.HALO = 30
        self.HC = self.HALO + PT + NS * (1 + SL)
        self.UC = self.HALO + PT + NS * (self.HALO + SL)
        self.GC = 2 + PT + NS * (2 + SL)
        self.NCH = PT // 64
        self.NU = self.NCH + NS
        assert PT % 128 == 0 and SL == 64 and self.FC % G == 0


class Res:
    __slots__ = ("name", "w", "rs")

    def __init__(self, name):
        self.name = name
        self.w = None
        self.rs = []


EPOCH = 24000
ENGS = ("pe", "act", "dve", "pool", "sp")


class Prog:
    def __init__(self, nc, stack):
        self.nc = nc
        self.stack = stack
        self.ops = {e: [] for e in ENGS}
        self.cnt = {e: 0 for e in ENGS}
        self.epoch = {e: 0 for e in ENGS}
        self.seen = {e: {} for e in ENGS}
        self.sems = {}
        self.dcnt = {}
        self.nops = 0
        self.pending = {e: [] for e in ENGS}

    def sem(self, key):
        s = self.sems.get(key)
        if s is None:
            s = self.stack.enter_context(self.nc.semaphore("s%d" % len(self.sems)))
            self.sems[key] = s
        return s

    def add(self, eng, fn, reads=(), writes=(), dma=None):
        deps = list(self.pending[eng])
        self.pending[eng] = []
        for r in reads:
            if r.w is not None:
                deps.append(r.w)
        for w in writes:
            if w.w is not None:
                deps.append(w.w)
            deps.extend(w.rs)
        if dma is None:
            if self.cnt[eng] >= EPOCH:
                self.epoch[eng] += 1
                self.cnt[eng] = 0
            self.cnt[eng] += 1
            key = ("c", eng, self.epoch[eng])
            tok = (key, self.cnt[eng])
            inc = 1
        else:
            key = ("d", dma)
            self.dcnt[key] = self.dcnt.get(key, 0) + 16
            tok = (key, self.dcnt[key])
            inc = 16
        need = {}
        for (k, v) in deps:
            if eng == "pe" and k[0] == "c" and k[1] == "pe":
                continue
            if self.seen[eng].get(k, 0) >= v:
                continue
            if need.get(k, 0) < v:
                need[k] = v
        for k, v in need.items():
            self.seen[eng][k] = v
        self.ops[eng].append((list(need.items()), fn, key, inc))
        for r in reads:
            r.rs.append(tok)
        for w in writes:
            w.w = tok
            w.rs = []
        self.nops += 1
        return tok

    def emit(self, final_keys=()):
        nc = self.nc
        for e in ENGS:
            for (need, fn, key, inc) in self.ops[e]:
                self.sem(key)
                for k, _ in need:
                    self.sem(k)
        prog = self

        def run(e, engobj):
            for (need, fn, key, inc) in prog.ops[e]:
                for k, v in need:
                    engobj.wait_ge(prog.sems[k], v)
                ins = fn(engobj)
                ins.then_inc(prog.sems[key], inc)
            if e == "sp":
                for k, v in prog.dcnt.items():
                    engobj.wait_ge(prog.sems[k], v)
                for ee in ENGS:
                    if ee == "sp":
                        continue
                    for ep in range(prog.epoch[ee] + 1):
                        kk = ("c", ee, ep)
                        if kk in prog.sems:
                            vv = prog.cnt[ee] if ep == prog.epoch[ee] else EPOCH
                            if vv > 0:
                                engobj.wait_ge(prog.sems[kk], vv)

        with nc.Block() as block:
            @block.tensor
            def _(t):
                run("pe", t)

            @block.scalar
            def _(s):
                run("act", s)

            @block.vector
            def _(v):
                run("dve", v)

            @block.gpsimd
            def _(g):
                run("pool", g)

            @block.sync
            def _(s):
                run("sp", s)


def param_layout(cfg):
    off = {}
    n = 0

    def put(name, cols):
        nonlocal n
        off[name] = (n, cols)
        n += cols
    KC, FC = cfg.KC, cfg.FC
    for i in range(cfg.depth):
        for q in range(4):
            put(("ng", i, q), KC)
        put(("fdw", i), FC * 3)
        put(("fdb", i), FC)
    for j in range(cfg.NCL):
        put(("pw1b", j), 2 * KC)
        put(("dww", j), KC * 31)
        for nm in ("dwb", "clng", "clnb", "pw2b"):
            put((nm, j), KC)
    for j in range(cfg.NRL):
        for q in range(6):
            put(("mix", j, q), KC)
        for nm in ("w0", "a0", "kk", "ka", "rk", "rlng", "rlnb"):
            put((nm, j), KC)
    for j in range(cfg.NVL):
        put(("v0", j), KC)
    return off, n


def fm_vec(v):
    return np.ascontiguousarray(np.asarray(v, np.float32).reshape(-1, 128).T)


def pack_params(cfg, inp):
    off, n = param_layout(cfg)
    P = np.zeros((128, n), np.float32)

    def st(key, arr):
        o, c = off[key]
        assert arr.shape == (128, c), (key, arr.shape, c)
        P[:, o:o + c] = arr
    for i in range(cfg.depth):
        for q in range(4):
            st(("ng", i, q), fm_vec(inp["norm_g"][i, q]))
        w = np.asarray(inp["ffn_dw_w"][i], np.float32)
        st(("fdw", i), np.ascontiguousarray(w.reshape(3, cfg.FC, 128).transpose(2, 1, 0)).reshape(128, cfg.FC * 3))
        st(("fdb", i), fm_vec(inp["ffn_dw_b"][i]))
    for j in range(cfg.NCL):
        st(("pw1b", j), fm_vec(inp["conv_pw1_b"][j]))
        w = np.asarray(inp["conv_dw_w"][j], np.float32)
        st(("dww", j), np.ascontiguousarray(w.reshape(31, cfg.KC, 128).transpose(2, 1, 0)).reshape(128, cfg.KC * 31))
        st(("dwb", j), fm_vec(inp["conv_dw_b"][j]))
        st(("clng", j), fm_vec(inp["conv_ln_g"][j]))
        st(("clnb", j), fm_vec(inp["conv_ln_b"][j]))
        st(("pw2b", j), fm_vec(inp["conv_pw2_b"][j]))
    for j in range(cfg.NRL):
        for q in range(6):
            st(("mix", j, q), fm_vec(inp["rwkv_mix"][j, q]))
        st(("w0", j), fm_vec(inp["rwkv_w0"][j]))
        st(("a0", j), fm_vec(inp["rwkv_a0"][j]))
        st(("kk", j), fm_vec(inp["rwkv_k_k"][j]))
        st(("ka", j), fm_vec(inp["rwkv_k_a"][j]))
        st(("rk", j), fm_vec(np.asarray(inp["rwkv_r_k"][j]).reshape(-1)))
        st(("rlng", j), fm_vec(inp["rwkv_ln_g"][j]))
        st(("rlnb", j), fm_vec(inp["rwkv_ln_b"][j]))
    for j in range(cfg.NVL):
        st(("v0", j), fm_vec(inp["rwkv_v0"][j]))
    return P


def wl(w, mc=128):
    w = np.asarray(w, np.float32)
    K, M = w.shape
    kc = K // 128
    nm = M // mc
    return np.ascontiguousarray(w.reshape(kc, 128, nm, mc).transpose(2, 1, 0, 3)).reshape(nm, 128, kc * mc)


def const_bf16(cfg):
    C = 64
    s = np.arange(C)
    mstrict = (s[:, None] < s[None, :]).astype(np.float32)
    mincl = (s[:, None] <= s[None, :]).astype(np.float32)
    eye2 = np.eye(2, dtype=np.float32)
    Mst = np.kron(eye2, mstrict)
    MstT = np.kron(eye2, mstrict.T)
    maskG = np.concatenate([np.concatenate([mincl, mincl], 0), np.ones((128, 64), np.float32)], 1)
    ident = np.eye(128, dtype=np.float32)
    ones = np.ones((128, 128), np.float32)
    bones = np.kron(eye2, np.ones((64, 64), np.float32))
    I2 = np.concatenate([np.eye(64, dtype=np.float32)] * 2, 0)
    parts = [ident, ones, bones, I2, np.tile(Mst, (1, 4)), np.tile(MstT, (1, 4)), np.tile(maskG, (1, 4))]
    return np.concatenate(parts, 1).astype(ml_dtypes.bfloat16)


CB_ID, CB_ONES, CB_BONES, CB_I2, CB_MST, CB_MSTT, CB_MG, CB_N = 0, 128, 256, 384, 448, 960, 1472, 1984


class B:
    def __init__(self, cfg, sublayers=None):
        self.cfg = cfg
        c = cfg
        self.nc = nc = bass.Bass("TRN2", target_bir_lowering=False)
        self.stack = contextlib.ExitStack()
        self.p = Prog(nc, self.stack)
        self.poff, self.npar = param_layout(cfg)
        KC, FC, T, D = c.KC, c.FC, c.T, c.D
        di = lambda name, shape, dt=F32: nc.dram_tensor(name, list(shape), dt, kind="ExternalInput").ap()
        do = lambda name, shape, dt=F32: nc.dram_tensor(name, list(shape), dt, kind="ExternalOutput").ap()
        self.d_x = di("xT", [128, KC * T])
        self.d_par = di("par", [128, self.npar])
        self.d_cb = di("cb", [128, CB_N], BF16)
        self.d_sel = di("sel", [128, 17])
        self.d_pw1 = di("w_pw1", [c.NCL * 2 * KC, 128, KC * 128])
        self.d_pw2 = di("w_pw2", [c.NCL * KC, 128, KC * 128])
        self.d_win = di("w_in", [c.depth * 2 * FC, 128, KC * 128])
        self.d_wout = di("w_out", [c.depth * (FC // c.G), 128, c.G * D])
        self.d_stconv = di("st_conv", [128, c.NCL * KC * c.NS * 30])
        self.d_stffn = di("st_ffn", [128, c.depth * FC * c.NS * 2])
        self.d_y = do("yT", [128, KC * T])
        self.d_oconv = do("o_conv", [128, c.NCL * KC * (1 + c.NS) * 30])
        self.d_offn = do("o_ffn", [128, c.depth * FC * (1 + c.NS) * 2])
        self.rwkv_decl()
        self.d_xh = nc.dram_tensor("x_home", [128, KC * T], F32).ap()
        self.t_hxin = nc.dram_tensor("hx_in", [128, KC * 30], BF16)
        self.t_hxmid = nc.dram_tensor("hx_mid", [4 * 128, KC * 30], BF16)
        self.t_hxout = nc.dram_tensor("hx_out", [NCORES * 128, KC * 30], BF16)
        sb = lambda name, cols, dt: self.stack.enter_context(nc.sbuf_tensor(name, [128, cols], dt))
        self.BIG = sb("big", KC * T, F32)
        self.HT = sb("ht", KC * c.HC, BF16)
        self.PAR = sb("par_sb", self.npar, F32)
        self.CB = sb("cb_sb", CB_N, BF16)
        self.SEL = sb("sel_sb", 17, F32)
        self.EPS = sb("eps_sb", 4, F32)
        self.STAT = sb("stat", 2 * T, F32)
        self.WS = [sb("ws%d" % i, KC * 128, BF16) for i in range(3)]
        self.AR2 = sb("ar2", max(2 * c.G * D + c.G * T, 2 * c.UC + 31 * 128 + 2048, 10752), BF16)
        self.STG = sb("stg", 3 * T + 96, F32)
        self.OST = sb("ost", 2 * (1 + c.NS) * 32, F32)
        self.DW3 = sb("dw3", 768, BF16)
        self.SQ = [sb("sq%d" % i, T, BF16) for i in range(4)]
        self.banks = [self.stack.enter_context(nc.psum_tensor("bank%d" % i, [128, 512], F32)) for i in range(8)]
        self.R_big = [Res("big%d" % k) for k in range(KC)]
        self.R_ht = [Res("ht%d" % k) for k in range(KC)]
        self.R_hthalo = Res("hthalo")
        self.R_const = Res("const")
        self.R_stat = Res("stat")
        self.R_ws = [Res("ws%d" % i) for i in range(3)]
        self.R_bank = [Res("bank%d" % i) for i in range(8)]
        self.R_sq = [Res("sq%d" % i) for i in range(4)]
        self.R_xh = [Res("xh%d" % k) for k in range(KC)]
        self.R_misc = {}
        self.wsi = 0
        self.bki = 0
        self.sqi = 0
        self.pend_barrier = None
        self.big3 = self.BIG[:, :].rearrange("p (k t) -> p k t", t=T)
        self.ht3 = self.HT[:, :].rearrange("p (k t) -> p k t", t=c.HC)
        self.rstd = self.STAT[:, 0:T]
        self.mean = self.STAT[:, T:2 * T]
        self.ptiles = [(o, min(512, T - o)) for o in range(0, T, 512)]
        self.mt = [("p", o, min(512, c.PT - o)) for o in range(0, c.PT, 512)] + [("s",)]
        self.build(sublayers)

    def R(self, name):
        r = self.R_misc.get(name)
        if r is None:
            r = self.R_misc[name] = Res(name)
        return r

    def par(self, key, col=0, n=1):
        o, c = self.poff[key]
        return self.PAR[:, o + col:o + col + n]

    def cbv(self, off, n):
        return self.CB[:, off:off + n]

    def bank(self, pool=(0, 1, 2, 3, 4, 5, 6, 7)):
        i = pool[self.bki % len(pool)]
        self.bki += 1
        return i

    def add(self, eng, fn, R=(), W=()):
        return self.p.add(eng, fn, R, W)

    def ACT(self, out, in_, func, R, W, bias=0.0, scale=1.0):
        self.p.add("act", lambda e: e.activation(out=out, in_=in_, func=func, bias=bias, scale=scale), R, W)

    def TS(self, eng, out, in0, s1, s2, op0, op1, R, W):
        if s2 is None:
            self.p.add(eng, lambda e: e.tensor_scalar(out=out, in0=in0, scalar1=s1, scalar2=None, op0=op0), R, W)
        else:
            self.p.add(eng, lambda e: e.tensor_scalar(out=out, in0=in0, scalar1=s1, scalar2=s2, op0=op0, op1=op1), R, W)

    def TT(self, eng, out, in0, in1, op, R, W):
        self.p.add(eng, lambda e: e.tensor_tensor(out=out, in0=in0, in1=in1, op=op), R, W)

    def STT(self, eng, out, in0, scalar, in1, op0, op1, R, W):
        self.p.add(eng, lambda e: e.scalar_tensor_tensor(out=out, in0=in0, scalar=scalar, in1=in1, op0=op0, op1=op1), R, W)

    def CP(self, eng, out, in_, R, W):
        if eng == "act":
            self.ACT(out, in_, AF.Identity, R, W)
        else:
            self.p.add(eng, lambda e: e.tensor_copy(out=out, in_=in_), R, W)

    def MM(self, out, lhsT, rhs, start, stop, R, W):
        self.p.add("pe", lambda e: e.matmul(out, lhsT, rhs, start=start, stop=stop), R, W)

    def DMA(self, q, out, in_, R, W, key):
        self.p.add(q, lambda e: e.dma_start(out=out, in_=in_), R, W, dma=key)

    def wload(self, dram, idx, cols=None):
        i = self.wsi % 3
        self.wsi += 1
        cols = cols or dram.shape[-1]
        self.DMA("pool", self.WS[i][:, 0:cols], dram[idx], [], [self.R_ws[i]], "ws%d" % i)
        return self.WS[i], self.R_ws[i]

    def hrhs(self, kc, t):
        c = self.cfg
        if t[0] == "p":
            return self.ht3[:, kc, 30 + t[1]:30 + t[1] + t[2]]
        if t[0] == "s":
            b0 = 30 + c.PT
            return self.ht3[:, kc, b0:b0 + c.NS * 65].rearrange("p (s c) -> p s c", c=65)[:, :, 1:65]
        if t[0] == "h":
            return self.ht3[:, kc, 0:30]
        if t[0] == "h2":
            return self.ht3[:, kc, 28:30]
        raise ValueError(t)

    def tn(self, t):
        c = self.cfg
        return {"p": t[2] if t[0] == "p" else 0, "s": c.NS * 64, "h": 30, "h2": 2}[t[0]]

    def bk(self, b, t):
        n = self.tn(t)
        ap = self.banks[b][:, 0:n]
        if t[0] == "s":
            return ap.rearrange("p (s c) -> p s c", c=64)
        return ap

    def pl(self, row, t, three=True):
        c = self.cfg
        if t[0] == "p":
            return row[:, t[1]:t[1] + t[2]]
        ap = row[:, c.PT:c.PT + c.NS * 64]
        return ap.rearrange("p (s c) -> p s c", c=64) if three else ap

    def colstats(self, want_sum, eps):
        c = self.cfg
        KC, T = c.KC, c.T
        ones = self.cbv(CB_ONES, 128)
        sqb = (5, 6, 7)
        smb = (2, 3, 4)
        for kc in range(KC):
            qi = self.sqi % 2
            self.sqi += 1
            self.ACT(self.SQ[qi][:, :], self.big3[:, kc, :], AF.Square, [self.R_big[kc]], [self.R_sq[qi]])
            for ti, (o, n) in enumerate(self.ptiles):
                self.MM(self.banks[sqb[ti]][:, 0:n], ones, self.SQ[qi][:, o:o + n], kc == 0, kc == KC - 1,
                        [self.R_sq[qi], self.R_const], [self.R_bank[sqb[ti]]])
            if want_sum:
                self.CP("act", self.SQ[2 + qi][:, :], self.big3[:, kc, :], [self.R_big[kc]], [self.R_sq[2 + qi]])
                for ti, (o, n) in enumerate(self.ptiles):
                    self.MM(self.banks[smb[ti]][:, 0:n], ones, self.SQ[2 + qi][:, o:o + n], kc == 0, kc == KC - 1,
                            [self.R_sq[2 + qi], self.R_const], [self.R_bank[smb[ti]]])
        inv = 1.0 / c.D
        for ti, (o, n) in enumerate(self.ptiles):
            rs = self.rstd[:, o:o + n]
            if not want_sum:
                self.ACT(rs, self.banks[sqb[ti]][:, 0:n], AF.Sqrt, [self.R_bank[sqb[ti]], self.R_const], [self.R_stat],
                         bias=self.epsc(eps), scale=inv)
            else:
                mn = self.mean[:, o:o + n]
                self.TS("dve", mn, self.banks[smb[ti]][:, 0:n], inv, None, ALU.mult, None,
                        [self.R_bank[smb[ti]]], [self.R_stat])
                self.TT("dve", rs, mn, mn, ALU.mult, [self.R_stat], [self.R_stat])
                self.STT("dve", rs, self.banks[sqb[ti]][:, 0:n], inv, rs, ALU.mult, ALU.subtract,
                         [self.R_bank[sqb[ti]], self.R_stat], [self.R_stat])
                self.ACT(rs, rs, AF.Sqrt, [self.R_stat, self.R_const], [self.R_stat], bias=self.epsc(eps))
            self.add("dve", lambda e, rs=rs: e.reciprocal(out=rs, in_=rs), [self.R_stat], [self.R_stat])

    def epsc(self, eps):
        i = {RMS_EPS: 0, LN_EPS: 1, GN_EPS: 2}[eps]
        return self.EPS[:, i:i + 1]

    def norm_in(self, gkey, hlast=None, pre_exchange=None):
        c = self.cfg
        KC, T, PT, NS = c.KC, c.T, c.PT, c.NS
        self.colstats(False, RMS_EPS)
        for kc in range(KC):
            g = self.par(gkey, kc)
            src = self.big3[:, kc, :]
            self.STT("dve", self.ht3[:, kc, PT:30 + PT], src[:, PT - 30:PT], g, self.rstd[:, PT - 30:PT], ALU.mult, ALU.mult,
                     [self.R_big[kc], self.R_stat], [self.R_ht[kc]])
        if pre_exchange is not None:
            pre_exchange()
        self.halo_exchange()
        for kc in range(KC):
            g = self.par(gkey, kc)
            src = self.big3[:, kc, :]
            self.STT("dve", self.ht3[:, kc, 30:PT], src[:, 0:PT - 30], g, self.rstd[:, 0:PT - 30], ALU.mult, ALU.mult,
                     [self.R_big[kc], self.R_stat], [self.R_ht[kc]])
            b0 = 30 + PT
            dst = self.ht3[:, kc, b0:b0 + NS * 65].rearrange("p (s c) -> p s c", c=65)[:, :, 1:65]
            self.STT("dve", dst, self.pl(src, ("s",)), g, self.pl(self.rstd, ("s",)), ALU.mult, ALU.mult,
                     [self.R_big[kc], self.R_stat], [self.R_ht[kc]])
            if hlast is not None:
                hlast(kc, src, g)
            self.DMA("sp", self.d_xh[:, kc * T:(kc + 1) * T], src, [self.R_big[kc]], [self.R_xh[kc]], "xst%d" % kc)

    def halo_exchange(self):
        c = self.cfg
        KC, PT = c.KC, c.PT
        n = KC * 30
        Rd1, Rd2 = self.R("hx_in"), self.R("hx_out")
        self.DMA("sp", self.t_hxin.ap().rearrange("p (k c) -> p k c", c=30), self.ht3[:, :, PT:PT + 30],
                 self.R_ht, [Rd1], "hx1")
        self.allgather8(self.t_hxin, self.t_hxmid, self.t_hxout, Rd1, Rd2)
        src = self.t_hxout.ap().rearrange("(r p) f -> p r f", p=128)
        for i in range(4):
            self.DMA("sp", self.SQ[i][:, 0:2 * n].rearrange("p (r f) -> p r f", f=n), src[:, 2 * i:2 * i + 2, :],
                     [Rd2], [self.R_sq[i]], "hx2_%d" % i)
        self.halo_pending = True

    def halo_recv(self):
        if not getattr(self, "halo_pending", False):
            return
        self.halo_pending = False
        c = self.cfg
        KC = c.KC
        n = KC * 30
        halo = self.ht3[:, :, 0:30]
        for r in range(NCORES):
            s = self.SEL[:, r:r + 1]
            piece = self.SQ[r // 2][:, (r % 2) * n:(r % 2 + 1) * n].rearrange("p (k c) -> p k c", c=30)
            if r == 0:
                self.TS("dve", halo, piece, s, None, ALU.mult, None, [self.R_sq[r // 2], self.R_const], [self.R_hthalo])
            else:
                self.STT("dve", halo, piece, s, halo, ALU.mult, ALU.add, [self.R_sq[r // 2], self.R_const], [self.R_hthalo])

    def resid(self, gkey):
        c = self.cfg
        KC, T = c.KC, c.T
        self.colstats(False, RMS_EPS)
        xs = [self.STG[:, 0:T], self.STG[:, T:2 * T]]
        Rx = [self.R("xs0"), self.R("xs1")]
        for kc in range(KC):
            i = kc % 2
            self.DMA("sp", xs[i], self.d_xh[:, kc * T:(kc + 1) * T], [self.R_xh[kc]], [Rx[i]], "xld%d" % i)
            b = self.big3[:, kc, :]
            self.TT("dve", b, b, self.rstd, ALU.mult, [self.R_stat], [self.R_big[kc]])
            self.STT("dve", b, b, self.par(gkey, kc), xs[i], ALU.mult, ALU.add, [Rx[i], self.R_const], [self.R_big[kc]])

    def allgather8(self, tin, tmid, tout, R_in, R_out):
        Rm = self.R("agmid_" + tmid.name)
        self.add("pool", lambda e: e.collective_compute("AllGather", ALU.bypass, replica_groups=[[0, 1, 2, 3], [4, 5, 6, 7]],
                                                        ins=[tin.ap().opt()], outs=[tmid.ap().opt()]), [R_in], [Rm])
        self.add("pool", lambda e: e.collective_compute("AllGather", ALU.bypass, replica_groups=[[0, 4], [1, 5], [2, 6], [3, 7]],
                                                        ins=[tmid.ap().opt()], outs=[tout.ap().opt()]), [Rm], [R_out])

    def barrier(self):
        p = self.p
        toks = []
        for e in ENGS:
            if p.cnt[e] > 0:
                toks.append((("c", e, p.epoch[e]), p.cnt[e]))
        for k, v in p.dcnt.items():
            toks.append((k, v))
        for e in ENGS:
            p.pending[e] = list(toks)

    def conformer(self, j):
        c = self.cfg
        KC, T, PT, NS, UC = c.KC, c.T, c.PT, c.NS, c.UC
        ub = [self.AR2[:, 0:UC], self.AR2[:, UC:2 * UC]]
        Rub = [self.R("ub0"), self.R("ub1")]
        dwd = self.AR2[:, 2 * UC:2 * UC + 31 * 128]
        Rdwd = self.R("dwd")
        sgs = [self.AR2[:, 2 * UC + 31 * 128 + i * 1024:2 * UC + 31 * 128 + (i + 1) * 1024].bitcast(F32) for i in range(2)]
        Rsg = [self.R("sg0"), self.R("sg1")]
        sth = self.STG[:, 2 * T:2 * T + 64]
        Rsth = self.R("sth")
        ident = self.cbv(CB_ID, 128)
        sgi = 0
        for m in range(KC):
            ws_l, R_l = self.wload(self.d_pw1, j * 2 * KC + m)
            ws_g, R_g = self.wload(self.d_pw1, j * 2 * KC + KC + m)
            u, Ru = ub[m % 2], Rub[m % 2]
            ost = self.OST[:, (m % 2) * (1 + NS) * 32:(m % 2) * (1 + NS) * 32 + (1 + NS) * 30]
            Rost = self.R("ost%d" % (m % 2))
            for s in range(NS):
                o = ((j * KC + m) * NS + s) * 30
                stg = self.STG[:, 2 * T + 32 * (s % 2):2 * T + 32 * (s % 2) + 30]
                Rs = self.R("sth%d" % (s % 2))
                self.DMA("sp", stg, self.d_stconv[:, o:o + 30], [], [Rs], "sth%d" % (s % 2))
                ubase = 30 + PT + s * 94
                self.CP("act", u[:, ubase:ubase + 30], stg, [Rs], [Ru])
            for t in self.mt + [("h",)]:
                bA, bB = self.bank(), self.bank()
                if t[0] == "h":
                    self.halo_recv()
                for kc in range(KC):
                    rr = [self.R_ht[kc], R_l] + ([self.R_hthalo] if t[0] == "h" else [])
                    self.MM(self.bk(bA, t), ws_l[:, kc * 128:(kc + 1) * 128], self.hrhs(kc, t), kc == 0, kc == KC - 1,
                            rr, [self.R_bank[bA]])
                for kc in range(KC):
                    rr = [self.R_ht[kc], R_g] + ([self.R_hthalo] if t[0] == "h" else [])
                    self.MM(self.bk(bB, t), ws_g[:, kc * 128:(kc + 1) * 128], self.hrhs(kc, t), kc == 0, kc == KC - 1,
                            rr, [self.R_bank[bB]])
                n = self.tn(t)
                sg, Rs_ = sgs[sgi % 2], Rsg[sgi % 2]
                sgi += 1
                self.ACT(sg[:, 0:n], self.banks[bB][:, 0:n], AF.Sigmoid, [self.R_bank[bB], self.R_const], [Rs_],
                         bias=self.par(("pw1b", j), KC + m))
                bl = self.par(("pw1b", j), m)
                if t[0] == "h":
                    self.STT("dve", u[:, 0:30], self.banks[bA][:, 0:30], bl, sg[:, 0:30], ALU.add, ALU.mult,
                             [self.R_bank[bA], Rs_, self.R_const], [Ru])
                    self.TS("dve", u[:, 0:30], u[:, 0:30], self.SEL[:, 16:17], None, ALU.mult, None, [self.R_const], [Ru])
                elif t[0] == "p":
                    self.STT("dve", u[:, 30 + t[1]:30 + t[1] + n], self.banks[bA][:, 0:n], bl, sg[:, 0:n], ALU.add, ALU.mult,
                             [self.R_bank[bA], Rs_, self.R_const], [Ru])
                    if t[1] + n == PT:
                        self.STT("dve", ost[:, 0:30], self.banks[bA][:, n - 30:n], bl, sg[:, n - 30:n], ALU.add, ALU.mult,
                                 [self.R_bank[bA], Rs_, self.R_const], [Rost])
                else:
                    b0 = 30 + PT
                    dst = u[:, b0:b0 + NS * 94].rearrange("p (s c) -> p s c", c=94)[:, :, 30:94]
                    sg3 = sg[:, 0:n].rearrange("p (s c) -> p s c", c=64)
                    self.STT("dve", dst, self.bk(bA, t), bl, sg3, ALU.add, ALU.mult,
                             [self.R_bank[bA], Rs_, self.R_const], [Ru])
                    self.STT("dve", ost[:, 30:(1 + NS) * 30].rearrange("p (s c) -> p s c", c=30), self.bk(bA, t)[:, :, 34:64], bl,
                             sg3[:, :, 34:64], ALU.add, ALU.mult, [self.R_bank[bA], Rs_, self.R_const], [Rost])
            oo = (j * KC + m) * (1 + NS) * 30
            self.DMA("sp", self.d_oconv[:, oo:oo + (1 + NS) * 30], ost, [Rost], [self.R("d_oconv")], "ost%d" % (m % 2))
            wk = self.par(("dww", j), m * 31, 31)
            self.TT("dve", dwd.rearrange("p (k j) -> p k j", j=128), ident.unsqueeze(1).broadcast_to([128, 31, 128]),
                    wk.unsqueeze(2).broadcast_to([128, 31, 128]), ALU.mult, [self.R_const], [Rdwd])
            for t in self.mt:
                b = self.bank()
                n = self.tn(t)
                for k in range(31):
                    if t[0] == "p":
                        rhs = u[:, t[1] + k:t[1] + k + n]
                    else:
                        b0 = 30 + PT
                        rhs = u[:, b0:b0 + NS * 94].rearrange("p (s c) -> p s c", c=94)[:, :, k:k + 64]
                    self.MM(self.bk(b, t), dwd[:, k * 128:(k + 1) * 128], rhs, k == 0, k == 30, [Ru, Rdwd], [self.R_bank[b]])
                self.ACT(self.pl(self.big3[:, m, :], t, three=False), self.banks[b][:, 0:n], AF.Identity,
                         [self.R_bank[b], self.R_const], [self.R_big[m]], bias=self.par(("dwb", j), m))
        self.colstats(True, LN_EPS)
        tmp = self.STG[:, 0:T]
        tmp2 = self.STG[:, T:2 * T]
        Rt, Rt2 = self.R("xs0"), self.R("xs1")
        for kc in range(KC):
            b = self.big3[:, kc, :]
            self.TT("dve", tmp, b, self.mean, ALU.subtract, [self.R_big[kc], self.R_stat], [Rt])
            self.TT("dve", tmp, tmp, self.rstd, ALU.mult, [self.R_stat], [Rt])
            self.TS("dve", tmp, tmp, self.par(("clng", j), kc), self.par(("clnb", j), kc), ALU.mult, ALU.add, [self.R_const], [Rt])
            self.ACT(tmp2, tmp, AF.Sigmoid, [Rt], [Rt2])
            self.TT("dve", self.ht3[:, kc, 0:T], tmp, tmp2, ALU.mult, [Rt, Rt2], [self.R_ht[kc], self.R_hthalo])
        for m in range(KC):
            ws, Rw = self.wload(self.d_pw2, j * KC + m)
            for t in self.mt:
                b = self.bank()
                n = self.tn(t)
                for kc in range(KC):
                    self.MM(self.banks[b][:, 0:n], ws[:, kc * 128:(kc + 1) * 128], self.pl(self.ht3[:, kc, 0:T], t, three=False),
                            kc == 0, kc == KC - 1, [self.R_ht[kc], Rw], [self.R_bank[b]])
                self.ACT(self.pl(self.big3[:, m, :], t, three=False), self.banks[b][:, 0:n], AF.Identity,
                         [self.R_bank[b], self.R_const], [self.R_big[m]], bias=self.par(("pw2b", j), m))

    def ffn(self, i):
        c = self.cfg
        KC, FC, T, PT, NS, G, D, GC = c.KC, c.FC, c.T, c.PT, c.NS, c.G, c.D, c.GC
        wo = [self.AR2[:, 0:G * D], self.AR2[:, G * D:2 * G * D]]
        Rwo = [self.R("wo0"), self.R("wo1")]
        zg = self.AR2[:, 2 * G * D:2 * G * D + G * T].rearrange("p (g t) -> p g t", t=T)
        Rzg = [self.R("zg%d" % gi) for gi in range(G)]
        t2 = self.STG[:, T:2 * T]
        Rt2 = self.R("xs1")
        stgb = self.STG[:, 0:T].bitcast(BF16)
        gps = [stgb[:, 0:GC]]
        dw3s = [self.DW3[:, 0:384], self.DW3[:, 384:768]]
        Rgps = [self.R("xs0"), self.R("xs0")]
        Rdw = self.R("dw3")
        shst = self.STG[:, 2 * T:2 * T + 2 * NS]
        Rsh = self.R("stg2")
        ident = self.cbv(CB_ID, 128)
        for g in range(FC // G):
            wsl, Rw_o = wo[g % 2], Rwo[g % 2]
            self.DMA("pool", wsl, self.d_wout[i * (FC // G) + g], [], [Rw_o], "wo%d" % (g % 2))
            for gi in range(G):
                f = g * G + gi
                gp, Rgp = gps[0], Rgps[0]
                dw3 = dw3s[f % 2]
                g3 = gp[:, 2 + PT:2 + PT + NS * 66].rearrange("p (s c) -> p s c", c=66)
                ws_u, R_u = self.wload(self.d_win, i * 2 * FC + f)
                ws_g, R_g = self.wload(self.d_win, i * 2 * FC + FC + f)
                so = ((i * FC + f) * NS) * 2
                self.DMA("sp", shst, self.d_stffn[:, so:so + NS * 2], [], [Rsh], "gph")
                self.CP("act", g3[:, :, 0:2], shst.rearrange("p (s c) -> p s c", c=2), [Rsh], [Rgp])
                sl = f % 2
                ofs = self.OST[:, sl * (1 + NS) * 32:sl * (1 + NS) * 32 + (1 + NS) * 2]
                Rofs = self.R("ost%d" % sl)
                for t in self.mt + [("h2",)]:
                    b = self.bank()
                    n = self.tn(t)
                    if t[0] == "h2":
                        self.halo_recv()
                    for kc in range(KC):
                        rr = [self.R_ht[kc], R_g] + ([self.R_hthalo] if t[0] == "h2" else [])
                        self.MM(self.bk(b, t), ws_g[:, kc * 128:(kc + 1) * 128], self.hrhs(kc, t), kc == 0, kc == KC - 1,
                                rr, [self.R_bank[b]])
                    if t[0] == "h2":
                        dst = gp[:, 0:2]
                    elif t[0] == "p":
                        dst = gp[:, 2 + t[1]:2 + t[1] + n]
                    else:
                        dst = g3[:, :, 2:66]
                    self.CP("act", dst, self.bk(b, t), [self.R_bank[b]], [Rgp])
                    if t[0] == "p" and t[1] + n == PT:
                        self.CP("act", ofs[:, 0:2], self.banks[b][:, n - 2:n], [self.R_bank[b]], [Rofs])
                    if t[0] == "s":
                        self.CP("act", ofs[:, 2:(1 + NS) * 2].rearrange("p (s c) -> p s c", c=2), self.bk(b, t)[:, :, 62:64],
                                [self.R_bank[b]], [Rofs])
                oo = (i * FC + f) * (1 + NS) * 2
                self.DMA("sp", self.d_offn[:, oo:oo + (1 + NS) * 2], ofs, [Rofs], [self.R("d_offn")], "ost%d" % sl)
                wk = self.par(("fdw", i), f * 3, 3)
                self.TT("dve", dw3.rearrange("p (k j) -> p k j", j=128), ident.unsqueeze(1).broadcast_to([128, 3, 128]),
                        wk.unsqueeze(2).broadcast_to([128, 3, 128]), ALU.mult, [self.R_const], [Rdw])
                bb = self.par(("fdb", i), f)
                for t in self.mt:
                    b = self.bank()
                    n = self.tn(t)
                    for k in range(3):
                        rhs = gp[:, t[1] + k:t[1] + k + n] if t[0] == "p" else g3[:, :, k:k + 64]
                        self.MM(self.bk(b, t), dw3[:, k * 128:(k + 1) * 128], rhs, k == 0, k == 2, [Rgp, Rdw], [self.R_bank[b]])
                    self.ACT(self.pl(t2, t, three=False), self.banks[b][:, 0:n], AF.Gelu_apprx_tanh, [self.R_bank[b], self.R_const],
                             [Rt2], bias=bb)
                for t in self.mt:
                    b = self.bank()
                    n = self.tn(t)
                    for kc in range(KC):
                        self.MM(self.bk(b, t), ws_u[:, kc * 128:(kc + 1) * 128], self.hrhs(kc, t), kc == 0, kc == KC - 1,
                                [self.R_ht[kc], R_u], [self.R_bank[b]])
                    self.TT("dve", self.pl(zg[:, gi, :], t, three=False), self.banks[b][:, 0:n], self.pl(t2, t, three=False),
                            ALU.mult, [self.R_bank[b], Rt2], [Rzg[gi]])
            for m in range(KC):
                for t in self.mt:
                    b = self.bank()
                    n = self.tn(t)
                    for gi in range(G):
                        self.MM(self.banks[b][:, 0:n], wsl[:, gi * D + m * 128:gi * D + (m + 1) * 128],
                                self.pl(zg[:, gi, :], t, three=False), gi == 0, gi == G - 1, [Rzg[gi], Rw_o], [self.R_bank[b]])
                    dst = self.pl(self.big3[:, m, :], t, three=False)
                    if g == 0:
                        self.CP("act", dst, self.banks[b][:, 0:n], [self.R_bank[b]], [self.R_big[m]])
                    else:
                        self.TT("dve", dst, dst, self.banks[b][:, 0:n], ALU.add, [self.R_bank[b]], [self.R_big[m]])

    def rwkv_decl(self):
        c = self.cfg
        nc = self.nc
        if c.NRL == 0:
            return
        KC, HP, T, NS = c.KC, c.HP, c.T, c.NS
        di = lambda name, shape, dt=F32: nc.dram_tensor(name, list(shape), dt, kind="ExternalInput").ap()
        do = lambda name, shape, dt=F32: nc.dram_tensor(name, list(shape), dt, kind="ExternalOutput").ap()
        self.d_rkvo = di("w_rkvo", [c.NRL * 4 * HP, 128, KC * 128])
        self.d_l1 = di("w_l1", [c.NRL * 5, 128, KC * 128])
        self.d_l2 = di("w_l2", [c.NRL * HP, 128, 5 * 128])
        self.d_stshift = di("st_shift", [128, c.NRL * KC * NS])
        self.d_stwkv = di("st_wkv", [128, c.NRL * HP * NS * 64])
        self.d_oshift = do("o_shift", [128, c.NRL * KC * (1 + NS)])
        self.d_owkv = do("o_wkv", [128, c.NRL * HP * (1 + NS) * 64])
        self.d_rkv = nc.dram_tensor("rkv_s", [c.NRL * 3 * HP, 128, T], F32).ap()
        self.d_ycb = nc.dram_tensor("ycb_s", [3 * HP, 128, T], F32).ap()
        self.NQ = 4 if HP % 4 == 0 else 1
        hq = HP // self.NQ
        self.t_sgin = [nc.dram_tensor("seg_in%d" % q, [128, hq * 128], F32) for q in range(self.NQ)]
        self.t_sgmid = [nc.dram_tensor("seg_mid%d" % q, [4 * 128, hq * 128], F32) for q in range(self.NQ)]
        self.t_sgout = [nc.dram_tensor("seg_out%d" % q, [NCORES * 128, hq * 128], F32) for q in range(self.NQ)]

    def rwkv_alloc(self):
        c = self.cfg
        KC, T, NS, HP = c.KC, c.T, c.NS, c.HP
        if not hasattr(self, "LW"):
            sb0 = lambda name, cols, dt: self.stack.enter_context(self.nc.sbuf_tensor(name, [128, cols], dt))
            self.LW = [self.AR2[:, 8448:9088], self.AR2[:, 9088:9728]]
            self.SGL = self.STG[:, 2 * T:3 * T].bitcast(BF16)
            self.OSH = sb0("osh", KC * (1 + NS) + KC, F32)
            self.SEGB = self.STAT[:, 0:HP * 128]
            self.SMALL = self.AR2[:, 5888:8448].bitcast(F32)

    def rwkv(self, j):
        c = self.cfg
        KC, HP, T, PT, NS, NU, NCH, D = c.KC, c.HP, c.T, c.PT, c.NS, c.NU, c.NCH, c.D
        sb = lambda name, cols, dt: self.stack.enter_context(self.nc.sbuf_tensor(name + "_%d" % j, [128, cols], dt))
        i_layer = 2 * j + 1
        bigb = self.BIG[:, :].bitcast(BF16)
        xm3 = bigb[:, 0:KC * T].rearrange("p (k t) -> p k t", t=T)
        dd3 = bigb[:, KC * T:2 * KC * T].rearrange("p (k t) -> p k t", t=T)
        R_xm = [Res("xm%d" % k) for k in range(KC)]
        R_dd = [Res("dd%d" % k) for k in range(KC)]
        ht3 = self.ht3
        b0 = 30 + PT
        hs3 = lambda kc: ht3[:, kc, b0:b0 + NS * 65].rearrange("p (s c) -> p s c", c=65)
        ident = self.cbv(CB_ID, 128)
        bones = self.cbv(CB_BONES, 128)
        I2 = self.cbv(CB_I2, 64)
        omk = self.OSH[:, KC * (1 + NS):KC * (1 + NS) + KC]
        self.TS("dve", omk, self.par(("ka", j), 0, KC), -1.0, 1.0, ALU.mult, ALU.add, [self.R_const], [self.R("omk")])
        for kc in range(KC):
            rr = [self.R_ht[kc], self.R_hthalo]
            eng = "dve"
            self.TT(eng, dd3[:, kc, 0:PT], ht3[:, kc, 29:29 + PT], ht3[:, kc, 30:30 + PT], ALU.subtract, rr, [R_dd[kc]])
            self.TT(eng, dd3[:, kc, PT:T].rearrange("p (s c) -> p s c", c=64), hs3(kc)[:, :, 0:64], hs3(kc)[:, :, 1:65],
                    ALU.subtract, rr, [R_dd[kc]])
        tw, xa1, xv1 = self.SQ[0], self.SQ[1], self.SQ[2]
        R_tw, R_xa1, R_xv1, R_sgl = self.R_sq[0], self.R_sq[1], self.R_sq[2], self.R("sgl")
        stg = [self.STG[:, 0:T], self.STG[:, T:2 * T]]
        Rstg = [self.R("xs0"), self.R("xs1")]
        stgi = 0
        for q in range(6):
            for kc in range(KC):
                mx = self.par(("mix", j, q), kc)
                eng = "dve"
                xs_ = xm3[:, kc, PT:T].rearrange("p (s c) -> p s c", c=64)
                ds_ = dd3[:, kc, PT:T].rearrange("p (s c) -> p s c", c=64)
                rr_ = [R_dd[kc], self.R_ht[kc], self.R_const]
                if eng == "dve":
                    self.STT(eng, xm3[:, kc, 0:PT], dd3[:, kc, 0:PT], mx, ht3[:, kc, 30:30 + PT], ALU.mult, ALU.add, rr_, [R_xm[kc]])
                    self.STT(eng, xs_, ds_, mx, hs3(kc)[:, :, 1:65], ALU.mult, ALU.add, rr_, [R_xm[kc]])
                else:
                    self.TS(eng, xm3[:, kc, 0:PT], dd3[:, kc, 0:PT], mx, None, ALU.mult, None, rr_, [R_xm[kc]])
                    self.TT(eng, xm3[:, kc, 0:PT], xm3[:, kc, 0:PT], ht3[:, kc, 30:30 + PT], ALU.add, rr_, [R_xm[kc]])
                    self.TS(eng, xs_, ds_, mx, None, ALU.mult, None, rr_, [R_xm[kc]])
                    self.TT(eng, xs_, xs_, hs3(kc)[:, :, 1:65], ALU.add, rr_, [R_xm[kc]])
            if q in (0, 2, 3):
                qq = {0: 0, 2: 1, 3: 2}[q]
                for m in range(HP):
                    ws, Rw = self.wload(self.d_rkvo, (j * 4 + qq) * HP + m)
                    sg_, Rs_ = stg[stgi % 2], Rstg[stgi % 2]
                    stgi += 1
                    for (o, n) in self.ptiles:
                        b = self.bank()
                        for kc in range(KC):
                            self.MM(self.banks[b][:, 0:n], ws[:, kc * 128:(kc + 1) * 128], xm3[:, kc, o:o + n], kc == 0, kc == KC - 1,
                                    [R_xm[kc], Rw], [self.R_bank[b]])
                        self.CP("act", sg_[:, o:o + n], self.banks[b][:, 0:n], [self.R_bank[b]], [Rs_])
                    self.DMA("sp", self.d_rkv[(j * 3 + qq) * HP + m], sg_, [Rs_], [self.R("rkv%d_%d_%d" % (j, qq, m))], "rkvst%d" % ((stgi - 1) % 2))
            if q in (1, 4, 5) or (q == 3 and j > 0):
                specs = {1: [(0, 96, tw, R_tw, AF.Tanh, 0)], 4: [(1, 96, xa1, R_xa1, AF.Identity, 0)],
                         5: [(2, 128, self.SGL, R_sgl, AF.Sigmoid, 0), (3, 128, self.SGL, R_sgl, AF.Sigmoid, T)],
                         3: [(4, 64, xv1, R_xv1, AF.Identity, 0)]}[q]
                for (row, mc, dstt, Rd, fn, co) in specs:
                    ws, Rw = self.wload(self.d_l1, j * 5 + row)
                    for (o, n) in self.ptiles:
                        b = self.bank()
                        for kc in range(KC):
                            self.MM(self.banks[b][0:mc, 0:n], ws[:, kc * mc:(kc + 1) * mc], xm3[:, kc, o:o + n], kc == 0, kc == KC - 1,
                                    [R_xm[kc], Rw], [self.R_bank[b]])
                        self.ACT(dstt[0:mc, co + o:co + o + n], self.banks[b][0:mc, 0:n], fn, [self.R_bank[b]], [Rd])
        if DBG == 1:
            return
        self.barrier()
        self.rwkv_scan(j)
        self.barrier()
        if DBG in (2, 3, 4) or DBG >= 20:
            return
        for m in range(HP):
            ws, Rw = self.wload(self.d_rkvo, (j * 4 + 3) * HP + m)
            for (o, n) in self.ptiles:
                b = self.bank()
                for kc in range(KC):
                    self.MM(self.banks[b][:, 0:n], ws[:, kc * 128:(kc + 1) * 128], ht3[:, kc, o:o + n], kc == 0, kc == KC - 1,
                            [self.R_ht[kc], Rw], [self.R_bank[b]])
                self.CP("act", self.big3[:, m, o:o + n], self.banks[b][:, 0:n], [self.R_bank[b]], [self.R_big[m]])

    def rwkv_scan(self, j):
        c = self.cfg
        KC, HP, T, PT, NS, NU, NCH, D = c.KC, c.HP, c.T, c.PT, c.NS, c.NU, c.NCH, c.D
        assert NU % 4 == 0
        NB = NU * 128
        needB = 5 * NB + NB + NU * 64 + 8 * 512
        if not hasattr(self, "arF"):
            sb0 = lambda name, cols, dt: self.stack.enter_context(self.nc.sbuf_tensor(name, [128, cols], dt))
            self.arF = self.BIG if KC >= 15 else sb0("arF", 15 * T, F32)
            self.arB = self.HT if KC * c.HC >= needB else sb0("arB", needB, BF16)
            self.RF = [Res("f%d" % i) for i in range(14)]
            self.RQ = [[Res("bd%d_%d" % (i, g)) for g in range(NU // 4)] for i in range(6)]
            self.RVS = [Res("vs%d" % g) for g in range(NU // 4)]
            self.RT8 = [Res("t8_%d" % i) for i in range(8)]
            self.RT8b = [Res("t8b_%d" % i) for i in range(8)]
            self.RT8c = [Res("t8c_%d" % i) for i in range(8)]
            self.RT8d = [Res("t8d_%d" % i) for i in range(8)]
            self.RT8e = [Res("t8e_%d" % i) for i in range(8)]
            self.zeroed = False
        arF, arB, RF = self.arF, self.arB, self.RF
        F = lambda i: arF[:, i * T:(i + 1) * T]
        r_, k_, v_, a_, lw_, kk_, b_, cs0, cs1, e_, Y_, tmp, vf_, bon = [F(i) for i in range(14)]
        Rr, Rk, Rv, Ra, Rlw, Rkk, Rb, Rcs0, Rcs1, Re, RY, Rtmp, Rvf, Rbon = RF
        U3 = lambda ap: ap.rearrange("p (u c) -> p u c", c=64)
        Qbd, Kbd, Pbd, Vbd, RD, EE = [arB[:, i * NB:(i + 1) * NB] for i in range(6)]
        RQ, RK, RP, RV, RRD, REE = self.RQ
        VS = arB[:, 6 * NB:6 * NB + NU * 64]
        T8 = [arB[:, 6 * NB + NU * 64 + i * 512:6 * NB + NU * 64 + (i + 1) * 512] for i in range(8)]
        A4, AT4, S0, T0, AkT4, X4, H4, PT4 = T8
        RA4, RAT4, RS0, RT0, RAk, RX, RH, RPT = self.RT8
        bd4 = lambda ap: ap.rearrange("p (u h c) -> p u h c", h=2, c=64)
        ident = self.cbv(CB_ID, 128)
        bones = self.cbv(CB_BONES, 128)
        I2 = self.cbv(CB_I2, 64)
        MST, MSTT, MG = self.cbv(CB_MST, 512), self.cbv(CB_MSTT, 512), self.cbv(CB_MG, 512)
        Rc = self.R_const
        G4 = NU // 4
        SM = self.SMALL
        smb = SM[:, 0:640].bitcast(BF16)
        SS = [smb[:, 0:128], smb[:, 128:256]]
        STbd, TTbd = smb[:, 256:384], smb[:, 384:512]
        S0st = smb[:, 512:512 + NS * 64]
        S0bd = smb[:, 768:768 + NS * 128]
        RSS = [self.R("ss0"), self.R("ss1")]
        RSTbd, RTTbd, RS0st, RS0bd = self.R("stbd"), self.R("ttbd"), self.R("s0st"), self.R("s0bd")
        s0stg = SM[:, 640:640 + NS * 64]
        owk = SM[:, 640 + NS * 64:640 + NS * 64 + (1 + NS) * 64]
        Rs0stg, Rowk = self.R("s0stg"), self.R("owk")
        if True:
            for buf, RR in ((Qbd, RQ), (Kbd, RK), (Pbd, RP), (Vbd, RV)):
                self.add("dve", lambda e, buf=buf: e.memset(buf, 0.0), [], RR)
            self.add("dve", lambda e: e.memset(STbd, 0.0), [], [RSTbd])
            self.add("dve", lambda e: e.memset(TTbd, 0.0), [], [RTTbd])
            self.add("dve", lambda e: e.memset(S0bd, 0.0), [], [RS0bd])
        allg = lambda RR: list(RR)
        bq = [0]

        def nb():
            bq[0] += 1
            return (bq[0] - 1) % 8
        pt_ = self.ptiles
        def prepA(m):
                lw2, Rlw2 = self.LW[m % 2], self.R("lw2_%d" % (m % 2))
                self.DMA("pool", lw2, self.d_l2[j * HP + m], [], [Rlw2], "lw2_%d" % (m % 2))
                for (dst, Rd, qq) in ((r_, Rr, 0), (k_, Rk, 1), (v_, Rv, 2)):
                    self.DMA("sp", dst, self.d_rkv[(j * 3 + qq) * HP + m], [self.R("rkv%d_%d_%d" % (j, qq, m))], [Rd], "ld%d" % qq)
                if j > 0:
                    self.DMA("sp", vf_, self.d_rkv[(0 * 3 + 2) * HP + m], [self.R("rkv%d_%d_%d" % (0, 2, m))], [Rvf], "ldvf")
                tw, xa1, xv1 = self.SQ[0], self.SQ[1], self.SQ[2]
                for (o, n) in pt_:
                    b = nb()
                    self.MM(self.banks[b][:, 0:n], lw2[0:96, 0:128], tw[0:96, o:o + n], True, True, [Rlw2, self.R_sq[0]], [self.R_bank[b]])
                    self.ACT(lw_[:, o:o + n], self.banks[b][:, 0:n], AF.Sigmoid, [self.R_bank[b], Rc], [Rlw], bias=self.par(("w0", j), m))
                    b = nb()
                    self.MM(self.banks[b][:, 0:n], lw2[0:96, 128:256], xa1[0:96, o:o + n], True, True, [Rlw2, self.R_sq[1]], [self.R_bank[b]])
                    self.ACT(a_[:, o:o + n], self.banks[b][:, 0:n], AF.Sigmoid, [self.R_bank[b], Rc], [Ra], bias=self.par(("a0", j), m))
                    if j > 0:
                        b = nb()
                        self.MM(self.banks[b][:, 0:n], lw2[0:64, 256:384], xv1[0:64, o:o + n], True, True, [Rlw2, self.R_sq[2]], [self.R_bank[b]])
                        self.ACT(tmp[:, o:o + n], self.banks[b][:, 0:n], AF.Sigmoid, [self.R_bank[b], Rc], [Rtmp], bias=self.par(("v0", j - 1), m))
                self.ACT(lw_, lw_, AF.Identity, [], [Rlw], scale=-0.6065306597126334)
                if j > 0:
                    self.TT("pool", vf_, vf_, v_, ALU.subtract, [Rv], [Rvf])
                    self.TT("pool", vf_, vf_, tmp, ALU.mult, [Rtmp], [Rvf])
                    self.TT("pool", v_, v_, vf_, ALU.add, [Rvf], [Rv])
                yield
                self.ACT(kk_, k_, AF.Identity, [Rk, Rc], [Rkk], scale=self.par(("kk", j), m))
                sq, Rsq = self.SQ[3], self.R_sq[3]
                self.ACT(sq[:, :], kk_, AF.Square, [Rkk], [Rsq])
                for (o, n) in pt_:
                    b = nb()
                    self.MM(self.banks[b][:, 0:n], bones, sq[:, o:o + n], True, True, [Rsq, Rc], [self.R_bank[b]])
                    self.ACT(tmp[:, o:o + n], self.banks[b][:, 0:n], AF.Sqrt, [self.R_bank[b]], [Rtmp])
                self.TS("dve", tmp, tmp, 1e-12, None, ALU.max, None, [], [Rtmp])
                self.add("dve", lambda e: e.reciprocal(out=tmp, in_=tmp), [], [Rtmp])
                self.TT("dve", kk_, kk_, tmp, ALU.mult, [Rtmp], [Rkk])
                yield
                omk = self.OSH[:, KC * (1 + NS) + m:KC * (1 + NS) + m + 1]
                self.TS("dve", tmp, a_, self.par(("ka", j), m), omk, ALU.mult, ALU.add, [Ra, Rc, self.R("omk")], [Rtmp])
                self.TT("dve", k_, k_, tmp, ALU.mult, [Rtmp], [Rk])
                self.TT("dve", b_, kk_, a_, ALU.mult, [Rkk, Ra], [Rb])
                yield
                self.STT("dve", sq[:, :], r_, self.par(("rk", j), m), k_, ALU.mult, ALU.mult, [Rr, Rk, Rc], [Rsq])
                for (o, n) in pt_:
                    b = nb()
                    self.MM(self.banks[b][:, 0:n], bones, sq[:, o:o + n], True, True, [Rsq, Rc], [self.R_bank[b]])
                    self.TT("dve", bon[:, o:o + n], self.banks[b][:, 0:n], v_[:, o:o + n], ALU.mult, [self.R_bank[b], Rv], [Rbon])
                self.DMA("sp", self.d_ycb[2 * HP + m], bon, [Rbon], [self.R("ycb2_%d" % m)], "stbon")
                src, Rs_, dst, Rd_ = lw_, Rlw, cs0, Rcs0
                for d in (1, 2, 4, 8, 16, 32):
                    self.TT("pool", U3(dst)[:, :, d:64], U3(src)[:, :, d:64], U3(src)[:, :, 0:64 - d], ALU.add, [Rs_], [Rd_])
                    self.CP("act", U3(dst)[:, :, 0:d], U3(src)[:, :, 0:d], [Rs_], [Rd_])
                    yield
                    if src is lw_:
                        src, Rs_, dst, Rd_ = cs0, Rcs0, cs1, Rcs1
                    else:
                        src, Rs_, dst, Rd_ = dst, Rd_, src, Rs_
                cs, Rcs, oth, Roth = src, Rs_, dst, Rd_
                gcb = self.SMALL[:, 1216:1216 + NU]
                Rgcb = self.R("gcb")
                self.ACT(e_, cs, AF.Exp, [Rcs], [Re])
                self.TT("pool", r_, r_, e_, ALU.mult, [Re], [Rr])
                self.CP("act", gcb.unsqueeze(2), U3(e_)[:, :, 63:64], [Re], [Rgcb])
                yield
                self.TT("dve", oth, cs, lw_, ALU.subtract, [Rcs, Rlw], [Roth])
                self.ACT(e_, oth, AF.Exp, [Roth], [Re])
                self.TT("pool", kk_, kk_, e_, ALU.mult, [Re], [Rkk])
                yield
                self.ACT(e_, cs, AF.Exp, [Rcs], [Re], scale=-1.0)
                self.TT("dve", b_, b_, e_, ALU.mult, [Re], [Rb])
                self.TT("pool", k_, k_, e_, ALU.mult, [Re], [Rk])
                yield

        def prepB(m):
            RD3 = RD.rearrange("p (u n) -> p u n", n=128)
            gcb = self.SMALL[:, 1216:1216 + NU]
            Rgcb = self.R("gcb")
            self.CP("act", RD3[:, :, 0:64], U3(r_), [Rr], allg(RRD))
            self.TT("dve", RD3[:, :, 64:128], I2.unsqueeze(1).broadcast_to([128, NU, 64]),
                    gcb.unsqueeze(2).broadcast_to([128, NU, 64]), ALU.mult, [Rgcb, Rc], allg(RRD))
            for h in range(2):
                ps = slice(64 * h, 64 * h + 64)
                self.CP("pool", bd4(Pbd)[ps, :, h, :], U3(kk_)[ps], [Rkk], allg(RP))
                self.CP("dve" if h else "pool", bd4(Qbd)[ps, :, h, :], U3(b_)[ps], [Rb], allg(RQ))
                self.CP("act" if h else "dve", bd4(Kbd)[ps, :, h, :], U3(k_)[ps], [Rk], allg(RK))
                self.CP("act", bd4(Vbd)[ps, :, h, :], U3(v_)[ps], [Rv], allg(RV))

        def groups(m):
                def group_steps(g, T8s, RT8s):
                    A4, AT4, S0, T0, AkT4, X4, H4, PT4 = T8s
                    RA4, RAT4, RS0, RT0, RAk, RX, RH, RPT = RT8s
                    us = list(range(4 * g, 4 * g + 4))

                    def mmu(lbuf, Rl, rfn, Rr_, oc=128, lfn=None):
                        b = nb()
                        for ui, u in enumerate(us):
                            l = lbuf[:, u * 128:(u + 1) * 128] if lfn is None else lfn(ui)
                            self.MM(self.banks[b][:, ui * oc:(ui + 1) * oc], l, rfn(ui, u), True, True, list(Rl) + list(Rr_), [self.R_bank[b]])
                        return b
                    ub = lambda buf: (lambda ui, u: buf[:, u * 128:(u + 1) * 128])
                    t4 = lambda buf: (lambda ui, u=None: buf[:, ui * 128:(ui + 1) * 128])
                    cst = lambda ap: (lambda ui, u: ap)
                    b = mmu(Qbd, [RQ[g]], ub(Pbd), [RP[g]])
                    self.TT("dve", A4, self.banks[b][:, :], MST, ALU.mult, [self.R_bank[b], Rc], [RA4])
                    b = mmu(Pbd, [RP[g]], ub(Qbd), [RQ[g]])
                    self.TT("dve", AT4, self.banks[b][:, :], MSTT, ALU.mult, [self.R_bank[b], Rc], [RAT4])
                    yield
                    b = mmu(Pbd, [RP[g]], ub(Kbd), [RK[g]])
                    self.TT("dve", AkT4, self.banks[b][:, :], MSTT, ALU.mult, [self.R_bank[b], Rc], [RAk])
                    b = mmu(Qbd, [RQ[g]], ub(RD), [RRD[g]])
                    self.TT("dve", X4, self.banks[b][:, :], MG, ALU.mult, [self.R_bank[b], Rc], [RX])
                    yield
                    b = mmu(Kbd, [RK[g]], ub(RD), [RRD[g]])
                    self.TT("dve", H4, self.banks[b][:, :], MG, ALU.mult, [self.R_bank[b], Rc], [RH])
                    b = mmu(Pbd, [RP[g]], cst(ident), [Rc])
                    self.CP("act", PT4, self.banks[b][:, :], [self.R_bank[b]], [RPT])
                    yield
                    b2 = mmu(Vbd, [RV[g]], cst(I2), [Rc], oc=64)
                    b = mmu(Vbd, [RV[g]], cst(ident), [Rc])
                    self.CP("act", VS[:, 4 * g * 64:(4 * g + 4) * 64], self.banks[b2][:, 0:256], [self.R_bank[b2]], [self.RVS[g]])
                    self.CP("act", Vbd[:, 4 * g * 128:(4 * g + 4) * 128], self.banks[b][:, :], [self.R_bank[b]], [RV[g]])
                    yield
                    b = mmu(None, [RAT4], t4(X4), [RX], lfn=t4(AT4))
                    self.TT("dve", X4, X4, self.banks[b][:, :], ALU.subtract, [self.R_bank[b]], [RX])
                    yield
                    Pc, RPc, PTc, RPTc = A4, RA4, AT4, RAT4
                    Pn, RPn, PTn, RPTn = S0, RS0, T0, RT0
                    for lvl in range(5):
                        if lvl < 4:
                            b = mmu(None, [RPTc], t4(Pc), [RPc], lfn=t4(PTc))
                            self.CP("act", Pn, self.banks[b][:, :], [self.R_bank[b]], [RPn])
                        b = mmu(None, [RPc], t4(PTc), [RPTc], lfn=t4(Pc))
                        self.CP("act" if lvl % 2 else "dve", PTn, self.banks[b][:, :], [self.R_bank[b]], [RPTn])
                        yield
                        b = mmu(None, [RPTn], t4(X4), [RX], lfn=t4(PTn))
                        self.TT("dve", X4, X4, self.banks[b][:, :], ALU.add, [self.R_bank[b]], [RX])
                        yield
                        Pc, RPc, PTc, RPTc, Pn, RPn, PTn, RPTn = Pn, RPn, PTn, RPTn, Pc, RPc, PTc, RPTc
                    b = mmu(None, [RAk], t4(X4), [RX], lfn=t4(AkT4))
                    gs = slice(4 * g * 128, (4 * g + 4) * 128)
                    self.TT("dve", EE[:, gs], H4, self.banks[b][:, :], ALU.subtract, [self.R_bank[b], RH], [REE[g]])
                    b = mmu(None, [RPT], t4(X4), [RX], lfn=t4(PT4))
                    self.TT("dve", RD[:, gs], RD[:, gs], self.banks[b][:, :], ALU.subtract, [self.R_bank[b]], [RRD[g]])
                    yield
                    RDg = RD[:, gs].rearrange("p (u n) -> p u n", n=128)
                    EEg = EE[:, gs].rearrange("p (u n) -> p u n", n=128)
                    for h in range(2):
                        ps = slice(64 * h, 64 * h + 64)
                        self.CP("act", bd4(Qbd[:, gs])[ps, :, h, :], RDg[ps, :, 64:128], [RRD[g]], [RQ[g]])
                        self.CP("act" if h else "dve", bd4(Kbd[:, gs])[ps, :, h, :], EEg[ps, :, 64:128], [REE[g]], [RK[g]])

                T8b = [self.AR2[:, i * 512:(i + 1) * 512] for i in range(8)]
                NW = 3 if self.cfg.KC * 128 >= 2048 else 2
                sets = [(T8, self.RT8), (T8b, self.RT8b)]
                if NW == 3:
                    sets.append(([self.WS[i // 4][:, (i % 4) * 512:(i % 4 + 1) * 512] for i in range(8)], self.RT8c))
                    if KC >= 16 and G4 >= 5:
                        NW = 5
                        stb = self.STG[:, 0:2 * T].bitcast(BF16)
                        sets.append(([stb[:, i * 512:(i + 1) * 512] for i in range(8)], self.RT8d))
                        f15 = F(15).bitcast(BF16)
                        sets.append(([self.WS[2][:, i * 512:(i + 1) * 512] for i in range(4)] +
                                     [f15[:, i * 512:(i + 1) * 512] for i in range(4)], self.RT8e))
                for g0 in range(0, G4, NW):
                    gens = [group_steps(g, *sets[(g - g0) % NW]) for g in range(g0, min(g0 + NW, G4))]
                    while gens:
                        for gen in list(gens):
                            try:
                                next(gen)
                            except StopIteration:
                                gens.remove(gen)
                        yield

        def seqpass(m):
                M1u = lambda u: RD[:, u * 128:u * 128 + 64]
                Eu = lambda u: EE[:, u * 128:u * 128 + 64]
                Msb = lambda u: Qbd[:, u * 128:(u + 1) * 128]
                Esb = lambda u: Kbd[:, u * 128:(u + 1) * 128]
                VTb = lambda u: Vbd[:, u * 128:(u + 1) * 128]
                VSu = lambda u: VS[:, u * 64:(u + 1) * 64]
                self.add("dve", lambda e: e.memset(SS[0][:, 0:64], 0.0), [], [RSS[0]])
                self.CP("dve", SS[0][:, 64:128], I2, [Rc], [RSS[0]])
                for h in range(2):
                    ps = slice(64 * h, 64 * h + 64)
                    self.add("dve", lambda e, ps=ps, h=h: e.memset(STbd[ps, 64 * h:64 * h + 64], 0.0), [], [RSTbd])
                    self.CP("dve", TTbd[ps, 64 * h:64 * h + 64], I2[ps, :], [Rc], [RTTbd])
                cf = F(14)
                Rcf_ = self.R("f14")
                for cgrp in range(0, NCH, 4):
                    by, bc = nb(), nb()
                    cl = list(range(cgrp, min(cgrp + 4, NCH)))
                    for ci, ch in enumerate(cl):
                        g = ch // 4
                        cur, nxt = SS[ch % 2], SS[(ch + 1) % 2]
                        Rcur, Rnxt = RSS[ch % 2], RSS[(ch + 1) % 2]
                        cs_ = slice(ci * 64, ci * 64 + 64)
                        bs = nb()
                        self.MM(self.banks[bs][:, 0:64], Msb(ch), cur[:, 0:64], True, False, [RQ[g], Rcur], [self.R_bank[bs]])
                        self.MM(self.banks[bs][:, 0:64], Esb(ch), VSu(ch), False, True, [RK[g], self.RVS[g]], [self.R_bank[bs]])
                        self.MM(self.banks[bs][:, 64:128], Msb(ch), cur[:, 64:128], True, True, [RQ[g], Rcur], [self.R_bank[bs]])
                        self.CP("dve", nxt, self.banks[bs][:, 0:128], [self.R_bank[bs]], [Rnxt])
                        self.MM(self.banks[by][:, cs_], STbd, M1u(ch), True, False, [RSTbd, RRD[g]], [self.R_bank[by]])
                        self.MM(self.banks[by][:, cs_], VTb(ch), Eu(ch), False, True, [RV[g], REE[g]], [self.R_bank[by]])
                        self.MM(self.banks[bc][:, cs_], TTbd, M1u(ch), True, True, [RTTbd, RRD[g]], [self.R_bank[bc]])
                        if ch == NCH - 1:
                            self.CP("dve", self.SEGB[:, m * 128:m * 128 + 64], self.banks[bs][:, 0:64], [self.R_bank[bs]], [self.R("segb")])
                        for h in range(2):
                            ps = slice(64 * h, 64 * h + 64)
                            self.CP("act", STbd[ps, 64 * h:64 * h + 64], nxt[ps, 0:64], [Rnxt], [RSTbd])
                            self.CP("dve", TTbd[ps, 64 * h:64 * h + 64], nxt[ps, 64:128], [Rnxt], [RTTbd])
                        yield
                    w = len(cl) * 64
                    self.CP("act", Y_[:, cgrp * 64:cgrp * 64 + w], self.banks[by][:, 0:w], [self.R_bank[by]], [RY])
                    self.CP("dve", cf[:, cgrp * 64:cgrp * 64 + w], self.banks[bc][:, 0:w], [self.R_bank[bc]], [Rcf_])
                b = nb()
                self.MM(self.banks[b][:, 0:64], TTbd, I2, True, True, [RTTbd, Rc], [self.R_bank[b]])
                self.CP("act", self.SEGB[:, m * 128 + 64:m * 128 + 128], self.banks[b][:, 0:64], [self.R_bank[b]], [self.R("segb")])
                self.DMA("sp", self.d_ycb[HP + m, :, 0:PT], cf[:, 0:PT], [Rcf_], [self.R("ycb1_%d" % m)], "stcf")
                so = ((j * HP + m) * NS) * 64
                self.DMA("sp", s0stg, self.d_stwkv[:, so:so + NS * 64], [], [Rs0stg], "s0ld")
                self.CP("act", S0st, s0stg, [Rs0stg], [RS0st])
                S0bd4 = S0bd.rearrange("p (s h c) -> p s h c", h=2, c=64)
                for h in range(2):
                    ps = slice(64 * h, 64 * h + 64)
                    self.CP("dve", S0bd4[ps, :, h, :], S0st[ps, :].rearrange("p (s c) -> p s c", c=64), [RS0st], [RS0bd])
                by, bs = nb(), nb()
                for s_ in range(NS):
                    u = NCH + s_
                    g = u // 4
                    cs_ = slice(s_ * 64, s_ * 64 + 64)
                    self.MM(self.banks[by][:, cs_], S0bd[:, s_ * 128:(s_ + 1) * 128], M1u(u), True, False, [RS0bd, RRD[g]], [self.R_bank[by]])
                    self.MM(self.banks[by][:, cs_], VTb(u), Eu(u), False, True, [RV[g], REE[g]], [self.R_bank[by]])
                    self.MM(self.banks[bs][:, cs_], Msb(u), S0st[:, cs_], True, False, [RQ[g], RS0st], [self.R_bank[bs]])
                    self.MM(self.banks[bs][:, cs_], Esb(u), VSu(u), False, True, [RK[g], self.RVS[g]], [self.R_bank[bs]])
                self.CP("act", Y_[:, PT:T], self.banks[by][:, 0:NS * 64], [self.R_bank[by]], [RY])
                self.CP("dve", owk[:, 64:(1 + NS) * 64], self.banks[bs][:, 0:NS * 64], [self.R_bank[bs]], [Rowk])
                oo = ((j * HP + m) * (1 + NS) + 1) * 64
                self.DMA("sp", self.d_owkv[:, oo:oo + NS * 64], owk[:, 64:(1 + NS) * 64], [Rowk], [self.R("d_owkv")], "stowk")
                self.DMA("sp", self.d_ycb[m], Y_, [RY], [self.R("ycb0_%d" % m)], "sty")

        def drain(gen):
            if gen is not None:
                for _ in gen:
                    pass

        def step(gen):
            if gen is None:
                return None
            try:
                next(gen)
                return gen
            except StopIteration:
                return None
        drain(prepA(0))
        for m in range(HP):
            prepB(m)
            nxt_prep = prepA(m + 1) if m + 1 < HP else None
            for _ in groups(m):
                nxt_prep = step(nxt_prep)
            for _ in seqpass(m):
                nxt_prep = step(nxt_prep)
            drain(nxt_prep)
        self.zeroed = True
        if DBG == 2 or DBG >= 20:
            return
        hq = HP // self.NQ
        for q in range(self.NQ):
            Rs1, Rs2 = self.R("seg_in%d" % q), self.R("seg_out%d" % q)
            self.DMA("sp", self.t_sgin[q].ap(), self.SEGB[:, q * hq * 128:(q + 1) * hq * 128], [self.R("segb")], [Rs1], "segst")
            self.allgather8(self.t_sgin[q], self.t_sgmid[q], self.t_sgout[q], Rs1, Rs2)
        if DBG == 3:
            return
        self.barrier()
        self.rwkv_finish(j)

    def rwkv_finish(self, j):
        c = self.cfg
        KC, HP, T, PT, NS, NU, NCH, D = c.KC, c.HP, c.T, c.PT, c.NS, c.NU, c.NCH, c.D
        arF, RF = self.arF, self.RF
        F = lambda i: arF[:, i * T:(i + 1) * T]
        slotsets = [(10, 13, 11, 7), (0, 1, 2, 3)]
        A2 = self.AR2
        A2f = A2[:, 0:10240].bitcast(F32)
        SG = A2f[:, 0:1024]
        SG3 = SG.rearrange("p (r n) -> p r n", n=128)
        Tbd8 = A2[:, 2048:3072]
        Pl = A2f[:, 1536:2112]
        Pb = [A2[:, 4224:4288], A2[:, 4288:4352]]
        Sst, Sbd = A2[:, 4352:4416], A2[:, 4416:4544]
        Ssel = A2f[:, 2304:2432]
        Ceffb = A2[:, 4864:4864 + PT]
        RSG, RT8, RPl, RSst, RSbd, RSsel, RCb = (self.R(n) for n in ("c_sg", "c_t8", "c_pl", "c_sst", "c_sbd", "c_ssel", "c_cb"))
        RPb = [self.R("c_pb0"), self.R("c_pb1")]
        bones = self.cbv(CB_BONES, 128)
        Rc = self.R_const
        SM = self.SMALL
        owk = SM[:, 640 + NS * 64:640 + NS * 64 + (1 + NS) * 64]
        Rowk = self.R("owk")
        hq = HP // self.NQ
        self.add("dve", lambda e: e.memset(Tbd8, 0.0), [], [RT8])
        self.add("dve", lambda e: e.memset(Sbd, 0.0), [], [RSbd])
        self.add("dve", lambda e: e.memset(Pb[0], 0.0), [], [RPb[0]])
        self.add("dve", lambda e: e.memset(Pl[:, 0:64], 0.0), [], [RPl])
        T84 = Tbd8.rearrange("p (r h c) -> p r h c", h=2, c=64)
        seg3 = [t.ap().rearrange("(r p) f -> p r f", p=128) for t in self.t_sgout]
        yb, ysq = self.SQ[0], self.SQ[1]
        Ryb, Rysq = self.R_sq[0], self.R_sq[1]
        bq = [0]

        def nb():
            bq[0] += 1
            return (bq[0] - 1) % 8
        def stage1(m):
            sY, sB, sC, sT = slotsets[m % 2]
            Y_, bon, cfst, t1 = F(sY), F(sB), F(sC), F(sT)
            RY, Rbon, Rcf, Rt1 = RF[sY], RF[sB], RF[sC], RF[sT]
            lw2, Rlw2 = self.LW[m % 2], self.R("lw2_%d" % (m % 2))
            lw2, Rlw2 = self.LW[m % 2], self.R("lw2_%d" % (m % 2))
            self.DMA("pool", lw2, self.d_l2[j * HP + m], [], [Rlw2], "lw2_%d" % (m % 2))
            self.DMA("sp", SG3, seg3[m // hq][:, :, (m % hq) * 128:(m % hq + 1) * 128], [self.R("seg_out%d" % (m // hq))], [RSG], "c_sg")
            self.DMA("sp", Y_, self.d_ycb[m], [self.R("ycb0_%d" % m)], [RY], "c_y%d" % (m % 2))
            self.DMA("sp", cfst[:, 0:PT], self.d_ycb[HP + m, :, 0:PT], [self.R("ycb1_%d" % m)], [Rcf], "c_cf%d" % (m % 2))
            self.DMA("sp", bon, self.d_ycb[2 * HP + m], [self.R("ycb2_%d" % m)], [Rbon], "c_bon%d" % (m % 2))
            for h in range(2):
                ps = slice(64 * h, 64 * h + 64)
                self.CP("act" if h else "dve", T84[ps, :, h, :], SG3[ps, :, 64:128], [RSG], [RT8])
            for r in range(NCORES):
                b = nb()
                cur, nxt = r % 2, (r + 1) % 2
                self.MM(self.banks[b][:, 0:64], Tbd8[:, r * 128:(r + 1) * 128], Pb[cur], True, True, [RT8, RPb[cur]], [self.R_bank[b]])
                self.TT("dve", Pb[nxt], self.banks[b][:, 0:64], SG3[:, r, 0:64], ALU.add, [self.R_bank[b], RSG], [RPb[nxt]])
                self.TT("dve", Pl[:, (r + 1) * 64:(r + 2) * 64], self.banks[b][:, 0:64], SG3[:, r, 0:64], ALU.add, [self.R_bank[b], RSG], [RPl])
                yield
            for k_, base in ((0, 0), (1, 1)):
                dst = Ssel[:, k_ * 64:(k_ + 1) * 64]
                for r in range(NCORES):
                    src = Pl[:, (r + base) * 64:(r + base + 1) * 64]
                    sc = self.SEL[:, 8 + r:9 + r]
                    if r == 0:
                        self.TS("dve", dst, src, sc, None, ALU.mult, None, [RPl, Rc], [RSsel])
                    else:
                        self.STT("dve", dst, src, sc, dst, ALU.mult, ALU.add, [RPl, Rc], [RSsel])
            self.CP("act", owk[:, 0:64], Ssel[:, 64:128], [RSsel], [Rowk])
            oo = ((j * HP + m) * (1 + NS)) * 64
            self.DMA("sp", self.d_owkv[:, oo:oo + 64], owk[:, 0:64], [Rowk], [self.R("d_owkv")], "stowk")
            self.CP("act", Sst, Ssel[:, 0:64], [RSsel], [RSst])
            for h in range(2):
                ps = slice(64 * h, 64 * h + 64)
                self.CP("dve", Sbd[ps, 64 * h:64 * h + 64], Sst[ps, :], [RSst], [RSbd])
            self.CP("act", Ceffb, cfst[:, 0:PT], [Rcf], [RCb])
            for (o, n) in [(o, min(512, PT - o)) for o in range(0, PT, 512)]:
                b = nb()
                self.MM(self.banks[b][:, 0:n], Sbd, Ceffb[:, o:o + n], True, True, [RSbd, RCb], [self.R_bank[b]])
                self.TT("dve", Y_[:, o:o + n], Y_[:, o:o + n], self.banks[b][:, 0:n], ALU.add, [self.R_bank[b]], [RY])
            yield

        def stage2(m):
            sY, sB, sC, sT = slotsets[m % 2]
            Y_, bon, cfst, t1 = F(sY), F(sB), F(sC), F(sT)
            RY, Rbon, Rcf, Rt1 = RF[sY], RF[sB], RF[sC], RF[sT]
            lw2, Rlw2 = self.LW[m % 2], self.R("lw2_%d" % (m % 2))
            self.CP("act", yb[:, :], Y_, [RY], [Ryb])
            self.ACT(ysq[:, :], Y_, AF.Square, [RY], [Rysq])
            mean, rstd = self.mean, self.rstd
            for (o, n) in self.ptiles:
                b1, b2 = nb(), nb()
                self.MM(self.banks[b1][:, 0:n], bones, yb[:, o:o + n], True, True, [Ryb, Rc], [self.R_bank[b1]])
                self.MM(self.banks[b2][:, 0:n], bones, ysq[:, o:o + n], True, True, [Rysq, Rc], [self.R_bank[b2]])
                mn, rs = mean[:, o:o + n], rstd[:, o:o + n]
                self.TS("dve", mn, self.banks[b1][:, 0:n], 1.0 / 64, None, ALU.mult, None, [self.R_bank[b1]], [self.R_stat])
                self.TT("dve", rs, mn, mn, ALU.mult, [], [self.R_stat])
                self.STT("dve", rs, self.banks[b2][:, 0:n], 1.0 / 64, rs, ALU.mult, ALU.subtract, [self.R_bank[b2]], [self.R_stat])
                self.ACT(rs, rs, AF.Sqrt, [Rc], [self.R_stat], bias=self.epsc(GN_EPS))
                self.add("dve", lambda e, rs=rs: e.reciprocal(out=rs, in_=rs), [], [self.R_stat])
                yield
            self.TT("dve", t1, Y_, mean, ALU.subtract, [RY, self.R_stat], [Rt1])
            self.TT("dve", t1, t1, rstd, ALU.mult, [self.R_stat], [Rt1])
            yield
            self.TS("dve", t1, t1, self.par(("rlng", j), m), self.par(("rlnb", j), m), ALU.mult, ALU.add, [Rc], [Rt1])
            self.TT("dve", t1, t1, bon, ALU.add, [Rbon], [Rt1])
            yield
            for (o, n) in self.ptiles:
                b = nb()
                self.MM(self.banks[b][:, 0:n], lw2[:, 384:512], self.SGL[:, o:o + n], True, False, [Rlw2, self.R("sgl")], [self.R_bank[b]])
                self.MM(self.banks[b][:, 0:n], lw2[:, 512:640], self.SGL[:, T + o:T + o + n], False, True, [Rlw2, self.R("sgl")], [self.R_bank[b]])
                self.TT("dve", self.ht3[:, m, o:o + n], t1[:, o:o + n], self.banks[b][:, 0:n], ALU.mult, [self.R_bank[b], Rt1],
                        [self.R_ht[m], self.R_hthalo])

            yield

        def step(gen):
            if gen is None:
                return None
            try:
                next(gen)
                return gen
            except StopIteration:
                return None
        g1 = stage1(0)
        while g1 is not None:
            g1 = step(g1)
        for m in range(HP):
            g2 = stage2(m)
            g1 = stage1(m + 1) if m + 1 < HP else None
            while g1 is not None or g2 is not None:
                g1 = step(g1)
                g2 = step(g2)

    def rwkv_pre(self, j):
        self.halo_recv()
        c = self.cfg
        KC, PT, NS = c.KC, c.PT, c.NS
        stg = self.SMALL[:, 0:KC * NS]
        Rst = self.R("shst")
        self.DMA("sp", stg, self.d_stshift[:, j * KC * NS:(j + 1) * KC * NS], [], [Rst], "shld")
        b0 = 30 + PT
        for kc in range(KC):
            dst = self.ht3[:, kc, b0:b0 + NS * 65].rearrange("p (s c) -> p s c", c=65)[:, :, 0:1]
            self.CP("act", dst, stg[:, kc * NS:(kc + 1) * NS].rearrange("p (s o) -> p s o", o=1), [Rst], [self.R_ht[kc]])
        n = KC * (1 + NS)
        self.DMA("sp", self.d_oshift[:, j * n:(j + 1) * n], self.OSH[:, 0:n], [self.R("osh")], [self.R("d_oshift")], "stosh")

    def hlast(self, kc, src, g):
        c = self.cfg
        PT, NS, T = c.PT, c.NS, c.T
        o = kc * (1 + NS)
        self.STT("dve", self.OSH[:, o:o + 1], src[:, PT - 1:PT], g, self.rstd[:, PT - 1:PT], ALU.mult, ALU.mult,
                 [self.R_big[kc], self.R_stat], [self.R("osh")])
        s3 = src[:, PT:T].rearrange("p (s c) -> p s c", c=64)[:, :, 63:64]
        r3 = self.rstd[:, PT:T].rearrange("p (s c) -> p s c", c=64)[:, :, 63:64]
        self.STT("dve", self.OSH[:, o + 1:o + 1 + NS].rearrange("p (s o) -> p s o", o=1), s3, g, r3, ALU.mult, ALU.mult,
                 [self.R_big[kc], self.R_stat], [self.R("osh")])

    def build(self, sublayers):
        c = self.cfg
        KC, T = c.KC, c.T
        self.DMA("sp", self.PAR[:, :], self.d_par[:, :], [], [self.R_const], "cst")
        self.DMA("sp", self.CB[:, :], self.d_cb[:, :], [], [self.R_const], "cst")
        self.DMA("sp", self.SEL[:, :], self.d_sel[:, :], [], [self.R_const], "cst")
        for i_, v_ in enumerate((RMS_EPS, LN_EPS, GN_EPS, 1e-24)):
            self.add("dve", lambda e, i_=i_, v_=v_: e.memset(self.EPS[:, i_:i_ + 1], v_), [], [self.R_const])
        for kc in range(KC):
            self.DMA("sp", self.big3[:, kc, :], self.d_x[:, kc * T:(kc + 1) * T], [], [self.R_big[kc]], "xin")
        if sublayers is None:
            sublayers = []
            for i in range(c.depth):
                sublayers.append(("mix", i))
                sublayers.append(("ffn", i))
        for (kind, i) in sublayers:
            self.barrier()
            if kind == "mix":
                if i % 2 == 1:
                    self.rwkv_alloc()
                    self.norm_in(("ng", i, 0), hlast=self.hlast)
                    self.rwkv_pre(i // 2)
                    self.barrier()
                    self.rwkv(i // 2)
                else:
                    self.norm_in(("ng", i, 0))
                    self.conformer(i // 2)
                self.barrier()
                self.resid(("ng", i, 1))
            else:
                self.norm_in(("ng", i, 2))
                self.ffn(i)
                self.barrier()
                self.resid(("ng", i, 3))
        for kc in range(KC):
            self.DMA("sp", self.d_y[:, kc * T:(kc + 1) * T], self.big3[:, kc, :], [self.R_big[kc]], [self.R("d_y")], "yout")
        self.p.emit()
        self.stack.close()


def fm_tokens(x):
    x = np.asarray(x, np.float32)
    t, d = x.shape
    return np.ascontiguousarray(x.reshape(t, d // 128, 128).transpose(2, 1, 0))


def prep_shared(cfg, inp):
    c = cfg
    sh = {}
    sh["par"] = pack_params(cfg, inp)
    sh["cb"] = const_bf16(cfg)
    sh["w_pw1"] = np.concatenate([wl(inp["conv_pw1_w"][j]) for j in range(c.NCL)], 0)
    sh["w_pw2"] = np.concatenate([wl(inp["conv_pw2_w"][j]) for j in range(c.NCL)], 0)
    sh["w_in"] = np.concatenate([wl(inp["ffn_w_in"][i]) for i in range(c.depth)], 0)
    wo = []
    for i in range(c.depth):
        w = np.asarray(inp["ffn_w_out"][i], np.float32)
        wo.append(np.ascontiguousarray(w.reshape(c.FC // c.G, c.G, 128, c.D).transpose(0, 2, 1, 3)).reshape(c.FC // c.G, 128, c.G * c.D))
    sh["w_out"] = np.concatenate(wo, 0)
    return sh


def prep_core(cfg, inp, core):
    c = cfg
    m = {}
    xp = np.asarray(inp["x_prompt"], np.float32)[0, core * c.PT:(core + 1) * c.PT]
    xs = np.asarray(inp["x_sample"], np.float32)[core * c.NS:(core + 1) * c.NS].reshape(c.NS * c.SL, c.D)
    m["xT"] = fm_tokens(np.concatenate([xp, xs], 0)).reshape(128, -1)
    sel = np.zeros((128, 17), np.float32)
    if core > 0:
        sel[:, core - 1] = 1.0
        sel[:, 16] = 1.0
    sel[:, 8 + core] = 1.0
    m["sel"] = sel
    sq = slice(core * c.NS, (core + 1) * c.NS)
    sc = np.asarray(inp["state_conv_mix"], np.float32)[:, sq]
    m["st_conv"] = np.ascontiguousarray(sc.reshape(c.NCL, c.NS, 30, c.KC, 128).transpose(4, 0, 3, 1, 2)).reshape(128, -1)
    sf = np.asarray(inp["state_ffn_conv"], np.float32)[:, sq]
    m["st_ffn"] = np.ascontiguousarray(sf.reshape(c.depth, c.NS, 2, c.FC, 128).transpose(4, 0, 3, 1, 2)).reshape(128, -1)
    return m


def assemble(cfg, results):
    c = cfg
    KC, FC, T, PT, NS, SL, D = c.KC, c.FC, c.T, c.PT, c.NS, c.SL, c.D
    yp, ys = [], []
    for r in results:
        y = np.asarray(r["yT"]).reshape(128, KC, T).transpose(2, 1, 0).reshape(T, D)
        yp.append(y[:PT])
        ys.append(y[PT:].reshape(NS, SL, D))
    y_prompt = np.concatenate(yp, 0)[None]
    y_sample = np.concatenate(ys, 0)

    def st(name, nl, nch, w):
        arrs = [np.asarray(r[name]).reshape(128, nl, nch, 1 + NS, w) for r in results]
        p = arrs[-1][:, :, :, 0, :].transpose(1, 3, 2, 0).reshape(nl, 1, w, nch * 128)
        s = np.concatenate([a[:, :, :, 1:, :].transpose(1, 3, 4, 2, 0).reshape(nl, NS, w, nch * 128) for a in arrs], 1)
        return np.ascontiguousarray(p), np.ascontiguousarray(s)
    conv_p, conv_s = st("o_conv", c.NCL, KC, 30)
    ffn_p, ffn_s = st("o_ffn", c.depth, FC, 2)
    outs = [y_prompt, y_sample, conv_p, conv_s, None, None, None, None, ffn_p, ffn_s]
    if c.NRL > 0 and "o_shift" in results[0]:
        outs[4:8] = assemble_rwkv(cfg, results)
    return tuple(outs)


def prep_shared_rwkv(cfg, inp):
    c = cfg
    if c.NRL == 0:
        return {}
    sh = {}
    rk = []
    for j in range(c.NRL):
        for nm in ("rwkv_w_r", "rwkv_w_k", "rwkv_w_v", "rwkv_w_o"):
            rk.append(wl(inp[nm][j]))
    sh["w_rkvo"] = np.concatenate(rk, 0)
    W = c.KC * 128
    l1 = np.zeros((c.NRL * 5, 128, W), np.float32)
    l2 = np.zeros((c.NRL * c.HP, 128, 5 * 128), np.float32)
    for j in range(c.NRL):
        l1[j * 5 + 0, :, :c.KC * 96] = wl(inp["rwkv_w1"][j], 96)[0]
        l1[j * 5 + 1, :, :c.KC * 96] = wl(inp["rwkv_a1"][j], 96)[0]
        g1 = wl(inp["rwkv_g1"][j], 128)
        l1[j * 5 + 2] = g1[0]
        l1[j * 5 + 3] = g1[1]
        if j > 0:
            l1[j * 5 + 4, :, :c.KC * 64] = wl(inp["rwkv_v1"][j - 1], 64)[0]
        w2 = np.asarray(inp["rwkv_w2"][j], np.float32)
        a2 = np.asarray(inp["rwkv_a2"][j], np.float32)
        g2 = np.asarray(inp["rwkv_g2"][j], np.float32)
        for m in range(c.HP):
            cs = slice(m * 128, (m + 1) * 128)
            l2[j * c.HP + m, 0:96, 0:128] = w2[:, cs]
            l2[j * c.HP + m, 0:96, 128:256] = a2[:, cs]
            if j > 0:
                l2[j * c.HP + m, 0:64, 256:384] = np.asarray(inp["rwkv_v2"][j - 1], np.float32)[:, cs]
            l2[j * c.HP + m, :, 384:512] = g2[0:128, cs]
            l2[j * c.HP + m, :, 512:640] = g2[128:256, cs]
    sh["w_l1"] = l1
    sh["w_l2"] = l2
    return sh


def prep_core_rwkv(cfg, inp, core):
    c = cfg
    if c.NRL == 0:
        return {}
    m = {}
    sq = slice(core * c.NS, (core + 1) * c.NS)
    ss = np.asarray(inp["state_rwkv_shift"], np.float32)[:, sq]
    m["st_shift"] = np.ascontiguousarray(ss.reshape(c.NRL, c.NS, c.KC, 128).transpose(3, 0, 2, 1)).reshape(128, -1)
    sw = np.asarray(inp["state_rwkv_wkv"], np.float32)[:, sq]
    sw = sw.reshape(c.NRL, c.NS, c.HP, 2, 64, 64)
    m["st_wkv"] = np.ascontiguousarray(sw.transpose(3, 5, 0, 2, 1, 4)).reshape(128, -1)
    return m


def assemble_rwkv(cfg, results):
    c = cfg
    KC, HP, NS, D = c.KC, c.HP, c.NS, c.D
    sh = [np.asarray(r["o_shift"]).reshape(128, c.NRL, KC, 1 + NS) for r in results]
    shift_p = sh[-1][:, :, :, 0].transpose(1, 2, 0).reshape(c.NRL, 1, D)
    shift_s = np.concatenate([a[:, :, :, 1:].transpose(1, 3, 2, 0).reshape(c.NRL, NS, D) for a in sh], 1)
    wk = [np.asarray(r["o_wkv"]).reshape(2, 64, c.NRL, HP, 1 + NS, 64) for r in results]
    tr = lambda a: a.transpose(2, 4, 3, 0, 5, 1).reshape(c.NRL, a.shape[4], HP * 2, 64, 64)
    wkv_p = tr(wk[-1][:, :, :, :, 0:1, :])
    wkv_s = np.concatenate([tr(a[:, :, :, :, 1:, :]) for a in wk], 1)
    return [np.ascontiguousarray(x) for x in (shift_p, shift_s, wkv_p, wkv_s)]


_CACHE = {}


def kernel(**inputs):
    cfg = Cfg()
    if "b" not in _CACHE:
        _CACHE["b"] = B(cfg)
    b = _CACHE["b"]
    inp = {k: np.asarray(v) for k, v in inputs.items()}
    sh = prep_shared(cfg, inp)
    sh.update(prep_shared_rwkv(cfg, inp))
    maps = []
    for c in range(NCORES):
        m = dict(sh)
        m.update(prep_core(cfg, inp, c))
        m.update(prep_core_rwkv(cfg, inp, c))
        maps.append(m)
    res = run_bass_kernel_spmd(b.nc, maps, core_ids=list(range(NCORES)))
    outs = assemble(cfg, [r for r in res.results])
    return tuple(np.ascontiguousarray(o, dtype=np.float32) for o in outs)
```

```python
import contextlib
import numpy as np
import ml_dtypes
import concourse.bass as bass
import concourse.mybir as mybir
from concourse.bass_utils import run_bass_kernel_spmd

F32 = mybir.dt.float32
BF16 = mybir.dt.bfloat16
AF = mybir.ActivationFunctionType
ALU = mybir.AluOpType
AX = mybir.AxisListType

NCORES = 8
DBG = 0
RMS_EPS = 1e-6
LN_EPS = 1e-5
GN_EPS = 64e-5


class Cfg:
    def __init__(self, D=2048, DFF=5632, PT=1024, NS=4, SL=64, depth=4, G=2):
        self.D, self.DFF, self.PT, self.NS, self.SL, self.depth, self.G = D, DFF, PT, NS, SL, depth, G
        self.KC = D // 128
        self.FC = DFF // 128
        self.HP = D // 128
        self.T = PT + NS * SL
        self.NCL = (depth + 1) // 2
        self.NRL = depth // 2
        self.NVL = max(self.NRL - 1, 0)
        self.HALO = 30
        self.HC = self.HALO + PT + NS * (1 + SL)
        self.UC = self.HALO + PT + NS * (self.HALO + SL)
        self.GC = 2 + PT + NS * (2 + SL)
        self.NCH = PT // 64
        self.NU = self.NCH + NS
        assert PT % 128 == 0 and SL == 64 and self.FC % G == 0


class Res:
    __slots__ = ("name", "w", "rs")

    def __init__(self, name):
        self.name = name
        self.w = None
        self.rs = []


EPOCH = 24000
ENGS = ("pe", "act", "dve", "pool", "sp")


class Prog:
    def __init__(self, nc, stack):
        self.nc = nc
        self.stack = stack
        self.ops = {e: [] for e in ENGS}
        self.cnt = {e: 0 for e in ENGS}
        self.epoch = {e: 0 for e in ENGS}
        self.seen = {e: {} for e in ENGS}
        self.sems = {}
        self.dcnt = {}
        self.nops = 0
        self.pending = {e: [] for e in ENGS}

    def sem(self, key):
        s = self.sems.get(key)
        if s is None:
            s = self.stack.enter_context(self.nc.semaphore("s%d" % len(self.sems)))
            self.sems[key] = s
        return s

    def add(self, eng, fn, reads=(), writes=(), dma=None):
        deps = list(self.pending[eng])
        self.pending[eng] = []
        for r in reads:
            if r.w is not None:
                deps.append(r.w)
        for w in writes:
            if w.w is not None:
                deps.append(w.w)
            deps.extend(w.rs)
        if dma is None:
            if self.cnt[eng] >= EPOCH:
                self.epoch[eng] += 1
                self.cnt[eng] = 0
            self.cnt[eng] += 1
            key = ("c", eng, self.epoch[eng])
            tok = (key, self.cnt[eng])
            inc = 1
        else:
            key = ("d", dma)
            self.dcnt[key] = self.dcnt.get(key, 0) + 16
            tok = (key, self.dcnt[key])
            inc = 16
        need = {}
        for (k, v) in deps:
            if eng == "pe" and k[0] == "c" and k[1] == "pe":
                continue
            if self.seen[eng].get(k, 0) >= v:
                continue
            if need.get(k, 0) < v:
                need[k] = v
        for k, v in need.items():
            self.seen[eng][k] = v
        self.ops[eng].append((list(need.items()), fn, key, inc))
        for r in reads:
            r.rs.append(tok)
        for w in writes:
            w.w = tok
            w.rs = []
        self.nops += 1
        return tok

    def emit(self, final_keys=()):
        nc = self.nc
        for e in ENGS:
            for (need, fn, key, inc) in self.ops[e]:
                self.sem(key)
                for k, _ in need:
                    self.sem(k)
        prog = self

        def run(e, engobj):
            for (need, fn, key, inc) in prog.ops[e]:
                for k, v in need:
                    engobj.wait_ge(prog.sems[k], v)
                ins = fn(engobj)
                ins.then_inc(prog.sems[key], inc)
            if e == "sp":
                for k, v in prog.dcnt.items():
                    engobj.wait_ge(prog.sems[k], v)
                for ee in ENGS:
                    if ee == "sp":
                        continue
                    for ep in range(prog.epoch[ee] + 1):
                        kk = ("c", ee, ep)
                        if kk in prog.sems:
                            vv = prog.cnt[ee] if ep == prog.epoch[ee] else EPOCH
                            if vv > 0:
                                engobj.wait_ge(prog.sems[kk], vv)

        with nc.Block() as block:
            @block.tensor
            def _(t):
                run("pe", t)

            @block.scalar
            def _(s):
                run("act", s)

            @block.vector
            def _(v):
                run("dve", v)

            @block.gpsimd
            def _(g):
                run("pool", g)

            @block.sync
            def _(s):
                run("sp", s)


def param_layout(cfg):
    off = {}
    n = 0

    def put(name, cols):
        nonlocal n
        off[name] = (n, cols)
        n += cols
    KC, FC = cfg.KC, cfg.FC
    for i in range(cfg.depth):
        for q in range(4):
            put(("ng", i, q), KC)
        put(("fdw", i), FC * 3)
        put(("fdb", i), FC)
    for j in range(cfg.NCL):
        put(("pw1b", j), 2 * KC)
        put(("dww", j), KC * 31)
        for nm in ("dwb", "clng", "clnb", "pw2b"):
            put((nm, j), KC)
    for j in range(cfg.NRL):
        for q in range(6):
            put(("mix", j, q), KC)
        for nm in ("w0", "a0", "kk", "ka", "rk", "rlng", "rlnb"):
            put((nm, j), KC)
    for j in range(cfg.NVL):
        put(("v0", j), KC)
    return off, n


def fm_vec(v):
    return np.ascontiguousarray(np.asarray(v, np.float32).reshape(-1, 128).T)


def pack_params(cfg, inp):
    off, n = param_layout(cfg)
    P = np.zeros((128, n), np.float32)

    def st(key, arr):
        o, c = off[key]
        assert arr.shape == (128, c), (key, arr.shape, c)
        P[:, o:o + c] = arr
    for i in range(cfg.depth):
        for q in range(4):
            st(("ng", i, q), fm_vec(inp["norm_g"][i, q]))
        w = np.asarray(inp["ffn_dw_w"][i], np.float32)
        st(("fdw", i), np.ascontiguousarray(w.reshape(3, cfg.FC, 128).transpose(2, 1, 0)).reshape(128, cfg.FC * 3))
        st(("fdb", i), fm_vec(inp["ffn_dw_b"][i]))
    for j in range(cfg.NCL):
        st(("pw1b", j), fm_vec(inp["conv_pw1_b"][j]))
        w = np.asarray(inp["conv_dw_w"][j], np.float32)
        st(("dww", j), np.ascontiguousarray(w.reshape(31, cfg.KC, 128).transpose(2, 1, 0)).reshape(128, cfg.KC * 31))
        st(("dwb", j), fm_vec(inp["conv_dw_b"][j]))
        st(("clng", j), fm_vec(inp["conv_ln_g"][j]))
        st(("clnb", j), fm_vec(inp["conv_ln_b"][j]))
        st(("pw2b", j), fm_vec(inp["conv_pw2_b"][j]))
    for j in range(cfg.NRL):
        for q in range(6):
            st(("mix", j, q), fm_vec(inp["rwkv_mix"][j, q]))
        st(("w0", j), fm_vec(inp["rwkv_w0"][j]))
        st(("a0", j), fm_vec(inp["rwkv_a0"][j]))
        st(("kk", j), fm_vec(inp["rwkv_k_k"][j]))
        st(("ka", j), fm_vec(inp["rwkv_k_a"][j]))
        st(("rk", j), fm_vec(np.asarray(inp["rwkv_r_k"][j]).reshape(-1)))
        st(("rlng", j), fm_vec(inp["rwkv_ln_g"][j]))
        st(("rlnb", j), fm_vec(inp["rwkv_ln_b"][j]))
    for j in range(cfg.NVL):
        st(("v0", j), fm_vec(inp["rwkv_v0"][j]))
    return P


def wl(w, mc=128):
    w = np.asarray(w, np.float32)
    K, M = w.shape
    kc = K // 128
    nm = M // mc
    return np.ascontiguousarray(w.reshape(kc, 128, nm, mc).transpose(2, 1, 0, 3)).reshape(nm, 128, kc * mc)


def const_bf16(cfg):
    C = 64
    s = np.arange(C)
    mstrict = (s[:, None] < s[None, :]).astype(np.float32)
    mincl = (s[:, None] <= s[None, :]).astype(np.float32)
    eye2 = np.eye(2, dtype=np.float32)
    Mst = np.kron(eye2, mstrict)
    MstT = np.kron(eye2, mstrict.T)
    maskG = np.concatenate([np.concatenate([mincl, mincl], 0), np.ones((128, 64), np.float32)], 1)
    ident = np.eye(128, dtype=np.float32)
    ones = np.ones((128, 128), np.float32)
    bones = np.kron(eye2, np.ones((64, 64), np.float32))
    I2 = np.concatenate([np.eye(64, dtype=np.float32)] * 2, 0)
    parts = [ident, ones, bones, I2, np.tile(Mst, (1, 4)), np.tile(MstT, (1, 4)), np.tile(maskG, (1, 4))]
    return np.concatenate(parts, 1).astype(ml_dtypes.bfloat16)


CB_ID, CB_ONES, CB_BONES, CB_I2, CB_MST, CB_MSTT, CB_MG, CB_N = 0, 128, 256, 384, 448, 960, 1472, 1984


class B:
    def __init__(self, cfg, sublayers=None):
        self.cfg = cfg
        c = cfg
        self.nc = nc = bass.Bass("TRN2", target_bir_lowering=False)
        self.stack = contextlib.ExitStack()
        self.p = Prog(nc, self.stack)
        self.poff, self.npar = param_layout(cfg)
        KC, FC, T, D = c.KC, c.FC, c.T, c.D
        di = lambda name, shape, dt=F32: nc.dram_tensor(name, list(shape), dt, kind="ExternalInput").ap()
        do = lambda name, shape, dt=F32: nc.dram_tensor(name, list(shape), dt, kind="ExternalOutput").ap()
        self.d_x = di("xT", [128, KC * T])
        self.d_par = di("par", [128, self.npar])
        self.d_cb = di("cb", [128, CB_N], BF16)
        self.d_sel = di("sel", [128, 17])
        self.d_pw1 = di("w_pw1", [c.NCL * 2 * KC, 128, KC * 128])
        self.d_pw2 = di("w_pw2", [c.NCL * KC, 128, KC * 128])
        self.d_win = di("w_in", [c.depth * 2 * FC, 128, KC * 128])
        self.d_wout = di("w_out", [c.depth * (FC // c.G), 128, c.G * D])
        self.d_stconv = di("st_conv", [128, c.NCL * KC * c.NS * 30])
        self.d_stffn = di("st_ffn", [128, c.depth * FC * c.NS * 2])
        self.d_y = do("yT", [128, KC * T])
        self.d_oconv = do("o_conv", [128, c.NCL * KC * (1 + c.NS) * 30])
        self.d_offn = do("o_ffn", [128, c.depth * FC * (1 + c.NS) * 2])
        self.rwkv_decl()
        self.d_xh = nc.dram_tensor("x_home", [128, KC * T], F32).ap()
        self.t_hxin = nc.dram_tensor("hx_in", [128, KC * 30], BF16)
        self.t_hxmid = nc.dram_tensor("hx_mid", [4 * 128, KC * 30], BF16)
        self.t_hxout = nc.dram_tensor("hx_out", [NCORES * 128, KC * 30], BF16)
        sb = lambda name, cols, dt: self.stack.enter_context(nc.sbuf_tensor(name, [128, cols], dt))
        self.BIG = sb("big", KC * T, F32)
        self.HT = sb("ht", KC * c.HC, BF16)
        self.PAR = sb("par_sb", self.npar, F32)
        self.CB = sb("cb_sb", CB_N, BF16)
        self.SEL = sb("sel_sb", 17, F32)
        self.EPS = sb("eps_sb", 4, F32)
        self.STAT = sb("stat", 2 * T, F32)
        self.WS = [sb("ws%d" % i, KC * 128, BF16) for i in range(3)]
        self.AR2 = sb("ar2", max(2 * c.G * D + c.G * T, 2 * c.UC + 31 * 128 + 2048, 10752), BF16)
        self.STG = sb("stg", 3 * T + 96, F32)
        self.OST = sb("ost", 2 * (1 + c.NS) * 32, F32)
        self.DW3 = sb("dw3", 768, BF16)
        self.SQ = [sb("sq%d" % i, T, BF16) for i in range(4)]
        self.banks = [self.stack.enter_context(nc.psum_tensor("bank%d" % i, [128, 512], F32)) for i in range(8)]
        self.R_big = [Res("big%d" % k) for k in range(KC)]
        self.R_ht = [Res("ht%d" % k) for k in range(KC)]
        self.R_hthalo = Res("hthalo")
        self.R_const = Res("const")
        self.R_stat = Res("stat")
        self.R_ws = [Res("ws%d" % i) for i in range(3)]
        self.R_bank = [Res("bank%d" % i) for i in range(8)]
        self.R_sq = [Res("sq%d" % i) for i in range(4)]
        self.R_xh = [Res("xh%d" % k) for k in range(KC)]
        self.R_misc = {}
        self.wsi = 0
        self.bki = 0
        self.sqi = 0
        self.pend_barrier = None
        self.big3 = self.BIG[:, :].rearrange("p (k t) -> p k t", t=T)
        self.ht3 = self.HT[:, :].rearrange("p (k t) -> p k t", t=c.HC)
        self.rstd = self.STAT[:, 0:T]
        self.mean = self.STAT[:, T:2 * T]
        self.ptiles = [(o, min(512, T - o)) for o in range(0, T, 512)]
        self.mt = [("p", o, min(512, c.PT - o)) for o in range(0, c.PT, 512)] + [("s",)]
        self.build(sublayers)

    def R(self, name):
        r = self.R_misc.get(name)
        if r is None:
            r = self.R_misc[name] = Res(name)
        return r

    def par(self, key, col=0, n=1):
        o, c = self.poff[key]
        return self.PAR[:, o + col:o + col + n]

    def cbv(self, off, n):
        return self.CB[:, off:off + n]

    def bank(self, pool=(0, 1, 2, 3, 4, 5, 6, 7)):
        i = pool[self.bki % len(pool)]
        self.bki += 1
        return i

    def add(self, eng, fn, R=(), W=()):
        return self.p.add(eng, fn, R, W)

    def ACT(self, out, in_, func, R, W, bias=0.0, scale=1.0):
        self.p.add("act", lambda e: e.activation(out=out, in_=in_, func=func, bias=bias, scale=scale), R, W)

    def TS(self, eng, out, in0, s1, s2, op0, op1, R, W):
        if s2 is None:
            self.p.add(eng, lambda e: e.tensor_scalar(out=out, in0=in0, scalar1=s1, scalar2=None, op0=op0), R, W)
        else:
            self.p.add(eng, lambda e: e.tensor_scalar(out=out, in0=in0, scalar1=s1, scalar2=s2, op0=op0, op1=op1), R, W)

    def TT(self, eng, out, in0, in1, op, R, W):
        self.p.add(eng, lambda e: e.tensor_tensor(out=out, in0=in0, in1=in1, op=op), R, W)

    def STT(self, eng, out, in0, scalar, in1, op0, op1, R, W):
        self.p.add(eng, lambda e: e.scalar_tensor_tensor(out=out, in0=in0, scalar=scalar, in1=in1, op0=op0, op1=op1), R, W)

    def CP(self, eng, out, in_, R, W):
        if eng == "act":
            self.ACT(out, in_, AF.Identity, R, W)
        else:
            self.p.add(eng, lambda e: e.tensor_copy(out=out, in_=in_), R, W)

    def MM(self, out, lhsT, rhs, start, stop, R, W):
        self.p.add("pe", lambda e: e.matmul(out, lhsT, rhs, start=start, stop=stop), R, W)

    def DMA(self, q, out, in_, R, W, key):
        self.p.add(q, lambda e: e.dma_start(out=out, in_=in_), R, W, dma=key)

    def wload(self, dram, idx, cols=None):
        i = self.wsi % 3
        self.wsi += 1
        cols = cols or dram.shape[-1]
        self.DMA("pool", self.WS[i][:, 0:cols], dram[idx], [], [self.R_ws[i]], "ws%d" % i)
        return self.WS[i], self.R_ws[i]

    def hrhs(self, kc, t):
        c = self.cfg
        if t[0] == "p":
            return self.ht3[:, kc, 30 + t[1]:30 + t[1] + t[2]]
        if t[0] == "s":
            b0 = 30 + c.PT
            return self.ht3[:, kc, b0:b0 + c.NS * 65].rearrange("p (s c) -> p s c", c=65)[:, :, 1:65]
        if t[0] == "h":
            return self.ht3[:, kc, 0:30]
        if t[0] == "h2":
            return self.ht3[:, kc, 28:30]
        raise ValueError(t)

    def tn(self, t):
        c = self.cfg
        return {"p": t[2] if t[0] == "p" else 0, "s": c.NS * 64, "h": 30, "h2": 2}[t[0]]

    def bk(self, b, t):
        n = self.tn(t)
        ap = self.banks[b][:, 0:n]
        if t[0] == "s":
            return ap.rearrange("p (s c) -> p s c", c=64)
        return ap

    def pl(self, row, t, three=True):
        c = self.cfg
        if t[0] == "p":
            return row[:, t[1]:t[1] + t[2]]
        ap = row[:, c.PT:c.PT + c.NS * 64]
        return ap.rearrange("p (s c) -> p s c", c=64) if three else ap

    def colstats(self, want_sum, eps):
        c = self.cfg
        KC, T = c.KC, c.T
        ones = self.cbv(CB_ONES, 128)
        sqb = (5, 6, 7)
        smb = (2, 3, 4)
        for kc in range(KC):
            qi = self.sqi % 2
            self.sqi += 1
            self.ACT(self.SQ[qi][:, :], self.big3[:, kc, :], AF.Square, [self.R_big[kc]], [self.R_sq[qi]])
            for ti, (o, n) in enumerate(self.ptiles):
                self.MM(self.banks[sqb[ti]][:, 0:n], ones, self.SQ[qi][:, o:o + n], kc == 0, kc == KC - 1,
                        [self.R_sq[qi], self.R_const], [self.R_bank[sqb[ti]]])
            if want_sum:
                self.CP("act", self.SQ[2 + qi][:, :], self.big3[:, kc, :], [self.R_big[kc]], [self.R_sq[2 + qi]])
                for ti, (o, n) in enumerate(self.ptiles):
                    self.MM(self.banks[smb[ti]][:, 0:n], ones, self.SQ[2 + qi][:, o:o + n], kc == 0, kc == KC - 1,
                            [self.R_sq[2 + qi], self.R_const], [self.R_bank[smb[ti]]])
        inv = 1.0 / c.D
        for ti, (o, n) in enumerate(self.ptiles):
            rs = self.rstd[:, o:o + n]
            if not want_sum:
                self.ACT(rs, self.banks[sqb[ti]][:, 0:n], AF.Sqrt, [self.R_bank[sqb[ti]], self.R_const], [self.R_stat],
                         bias=self.epsc(eps), scale=inv)
            else:
                mn = self.mean[:, o:o + n]
                self.TS("dve", mn, self.banks[smb[ti]][:, 0:n], inv, None, ALU.mult, None,
                        [self.R_bank[smb[ti]]], [self.R_stat])
                self.TT("dve", rs, mn, mn, ALU.mult, [self.R_stat], [self.R_stat])
                self.STT("dve", rs, self.banks[sqb[ti]][:, 0:n], inv, rs, ALU.mult, ALU.subtract,
                         [self.R_bank[sqb[ti]], self.R_stat], [self.R_stat])
                self.ACT(rs, rs, AF.Sqrt, [self.R_stat, self.R_const], [self.R_stat], bias=self.epsc(eps))
            self.add("dve", lambda e, rs=rs: e.reciprocal(out=rs, in_=rs), [self.R_stat], [self.R_stat])

    def epsc(self, eps):
        i = {RMS_EPS: 0, LN_EPS: 1, GN_EPS: 2}[eps]
        return self.EPS[:, i:i + 1]

    def norm_in(self, gkey, hlast=None, pre_exchange=None):
        c = self.cfg
        KC, T, PT, NS = c.KC, c.T, c.PT, c.NS
        self.colstats(False, RMS_EPS)
        for kc in range(KC):
            g = self.par(gkey, kc)
            src = self.big3[:, kc, :]
            self.STT("dve", self.ht3[:, kc, PT:30 + PT], src[:, PT - 30:PT], g, self.rstd[:, PT - 30:PT], ALU.mult, ALU.mult,
                     [self.R_big[kc], self.R_stat], [self.R_ht[kc]])
        if pre_exchange is not None:
            pre_exchange()
        self.halo_exchange()
        for kc in range(KC):
            g = self.par(gkey, kc)
            src = self.big3[:, kc, :]
            self.STT("dve", self.ht3[:, kc, 30:PT], src[:, 0:PT - 30], g, self.rstd[:, 0:PT - 30], ALU.mult, ALU.mult,
                     [self.R_big[kc], self.R_stat], [self.R_ht[kc]])
            b0 = 30 + PT
            dst = self.ht3[:, kc, b0:b0 + NS * 65].rearrange("p (s c) -> p s c", c=65)[:, :, 1:65]
            self.STT("dve", dst, self.pl(src, ("s",)), g, self.pl(self.rstd, ("s",)), ALU.mult, ALU.mult,
                     [self.R_big[kc], self.R_stat], [self.R_ht[kc]])
            if hlast is not None:
                hlast(kc, src, g)
            self.DMA("sp", self.d_xh[:, kc * T:(kc + 1) * T], src, [self.R_big[kc]], [self.R_xh[kc]], "xst%d" % kc)

    def halo_exchange(self):
        c = self.cfg
        KC, PT = c.KC, c.PT
        n = KC * 30
        Rd1, Rd2 = self.R("hx_in"), self.R("hx_out")
        self.DMA("sp", self.t_hxin.ap().rearrange("p (k c) -> p k c", c=30), self.ht3[:, :, PT:PT + 30],
                 self.R_ht, [Rd1], "hx1")
        self.allgather8(self.t_hxin, self.t_hxmid, self.t_hxout, Rd1, Rd2)
        src = self.t_hxout.ap().rearrange("(r p) f -> p r f", p=128)
        for i in range(4):
            self.DMA("sp", self.SQ[i][:, 0:2 * n].rearrange("p (r f) -> p r f", f=n), src[:, 2 * i:2 * i + 2, :],
                     [Rd2], [self.R_sq[i]], "hx2_%d" % i)
        self.halo_pending = True

    def halo_recv(self):
        if not getattr(self, "halo_pending", False):
            return
        self.halo_pending = False
        c = self.cfg
        KC = c.KC
        n = KC * 30
        halo = self.ht3[:, :, 0:30]
        for r in range(NCORES):
            s = self.SEL[:, r:r + 1]
            piece = self.SQ[r // 2][:, (r % 2) * n:(r % 2 + 1) * n].rearrange("p (k c) -> p k c", c=30)
            if r == 0:
                self.TS("dve", halo, piece, s, None, ALU.mult, None, [self.R_sq[r // 2], self.R_const], [self.R_hthalo])
            else:
                self.STT("dve", halo, piece, s, halo, ALU.mult, ALU.add, [self.R_sq[r // 2], self.R_const], [self.R_hthalo])

    def resid(self, gkey):
        c = self.cfg
        KC, T = c.KC, c.T
        self.colstats(False, RMS_EPS)
        xs = [self.STG[:, 0:T], self.STG[:, T:2 * T]]
        Rx = [self.R("xs0"), self.R("xs1")]
        for kc in range(KC):
            i = kc % 2
            self.DMA("sp", xs[i], self.d_xh[:, kc * T:(kc + 1) * T], [self.R_xh[kc]], [Rx[i]], "xld%d" % i)
            b = self.big3[:, kc, :]
            self.TT("dve", b, b, self.rstd, ALU.mult, [self.R_stat], [self.R_big[kc]])
            self.STT("dve", b, b, self.par(gkey, kc), xs[i], ALU.mult, ALU.add, [Rx[i], self.R_const], [self.R_big[kc]])

    def allgather8(self, tin, tmid, tout, R_in, R_out):
        Rm = self.R("agmid_" + tmid.name)
        self.add("pool", lambda e: e.collective_compute("AllGather", ALU.bypass, replica_groups=[[0, 1, 2, 3], [4, 5, 6, 7]],
                                                        ins=[tin.ap().opt()], outs=[tmid.ap().opt()]), [R_in], [Rm])
        self.add("pool", lambda e: e.collective_compute("AllGather", ALU.bypass, replica_groups=[[0, 4], [1, 5], [2, 6], [3, 7]],
                                                        ins=[tmid.ap().opt()], outs=[tout.ap().opt()]), [Rm], [R_out])

    def barrier(self):
        p = self.p
        toks = []
        for e in ENGS:
            if p.cnt[e] > 0:
                toks.append((("c", e, p.epoch[e]), p.cnt[e]))
        for k, v in p.dcnt.items():
            toks.append((k, v))
        for e in ENGS:
            p.pending[e] = list(toks)

    def conformer(self, j):
        c = self.cfg
        KC, T, PT, NS, UC = c.KC, c.T, c.PT, c.NS, c.UC
        ub = [self.AR2[:, 0:UC], self.AR2[:, UC:2 * UC]]
        Rub = [self.R("ub0"), self.R("ub1")]
        dwd = self.AR2[:, 2 * UC:2 * UC + 31 * 128]
        Rdwd = self.R("dwd")
        sgs = [self.AR2[:, 2 * UC + 31 * 128 + i * 1024:2 * UC + 31 * 128 + (i + 1) * 1024].bitcast(F32) for i in range(2)]
        Rsg = [self.R("sg0"), self.R("sg1")]
        sth = self.STG[:, 2 * T:2 * T + 64]
        Rsth = self.R("sth")
        ident = self.cbv(CB_ID, 128)
        sgi = 0
        for m in range(KC):
            ws_l, R_l = self.wload(self.d_pw1, j * 2 * KC + m)
            ws_g, R_g = self.wload(self.d_pw1, j * 2 * KC + KC + m)
            u, Ru = ub[m % 2], Rub[m % 2]
            ost = self.OST[:, (m % 2) * (1 + NS) * 32:(m % 2) * (1 + NS) * 32 + (1 + NS) * 30]
            Rost = self.R("ost%d" % (m % 2))
            for s in range(NS):
                o = ((j * KC + m) * NS + s) * 30
                stg = self.STG[:, 2 * T + 32 * (s % 2):2 * T + 32 * (s % 2) + 30]
                Rs = self.R("sth%d" % (s % 2))
                self.DMA("sp", stg, self.d_stconv[:, o:o + 30], [], [Rs], "sth%d" % (s % 2))
                ubase = 30 + PT + s * 94
                self.CP("act", u[:, ubase:ubase + 30], stg, [Rs], [Ru])
            for t in self.mt + [("h",)]:
                bA, bB = self.bank(), self.bank()
                if t[0] == "h":
                    self.halo_recv()
                for kc in range(KC):
                    rr = [self.R_ht[kc], R_l] + ([self.R_hthalo] if t[0] == "h" else [])
                    self.MM(self.bk(bA, t), ws_l[:, kc * 128:(kc + 1) * 128], self.hrhs(kc, t), kc == 0, kc == KC - 1,
                            rr, [self.R_bank[bA]])
                for kc in range(KC):
                    rr = [self.R_ht[kc], R_g] + ([self.R_hthalo] if t[0] == "h" else [])
                    self.MM(self.bk(bB, t), ws_g[:, kc * 128:(kc + 1) * 128], self.hrhs(kc, t), kc == 0, kc == KC - 1,
                            rr, [self.R_bank[bB]])
                n = self.tn(t)
                sg, Rs_ = sgs[sgi % 2], Rsg[sgi % 2]
                sgi += 1
                self.ACT(sg[:, 0:n], self.banks[bB][:, 0:n], AF.Sigmoid, [self.R_bank[bB], self.R_const], [Rs_],
                         bias=self.par(("pw1b", j), KC + m))
                bl = self.par(("pw1b", j), m)
                if t[0] == "h":
                    self.STT("dve", u[:, 0:30], self.banks[bA][:, 0:30], bl, sg[:, 0:30], ALU.add, ALU.mult,
                             [self.R_bank[bA], Rs_, self.R_const], [Ru])
                    self.TS("dve", u[:, 0:30], u[:, 0:30], self.SEL[:, 16:17], None, ALU.mult, None, [self.R_const], [Ru])
                elif t[0] == "p":
                    self.STT("dve", u[:, 30 + t[1]:30 + t[1] + n], self.banks[bA][:, 0:n], bl, sg[:, 0:n], ALU.add, ALU.mult,
                             [self.R_bank[bA], Rs_, self.R_const], [Ru])
                    if t[1] + n == PT:
                        self.STT("dve", ost[:, 0:30], self.banks[bA][:, n - 30:n], bl, sg[:, n - 30:n], ALU.add, ALU.mult,
                                 [self.R_bank[bA], Rs_, self.R_const], [Rost])
                else:
                    b0 = 30 + PT
                    dst = u[:, b0:b0 + NS * 94].rearrange("p (s c) -> p s c", c=94)[:, :, 30:94]
                    sg3 = sg[:, 0:n].rearrange("p (s c) -> p s c", c=64)
                    self.STT("dve", dst, self.bk(bA, t), bl, sg3, ALU.add, ALU.mult,
                             [self.R_bank[bA], Rs_, self.R_const], [Ru])
                    self.STT("dve", ost[:, 30:(1 + NS) * 30].rearrange("p (s c) -> p s c", c=30), self.bk(bA, t)[:, :, 34:64], bl,
                             sg3[:, :, 34:64], ALU.add, ALU.mult, [self.R_bank[bA], Rs_, self.R_const], [Rost])
            oo = (j * KC + m) * (1 + NS) * 30
            self.DMA("sp", self.d_oconv[:, oo:oo + (1 + NS) * 30], ost, [Rost], [self.R("d_oconv")], "ost%d" % (m % 2))
            wk = self.par(("dww", j), m * 31, 31)
            self.TT("dve", dwd.rearrange("p (k j) -> p k j", j=128), ident.unsqueeze(1).broadcast_to([128, 31, 128]),
                    wk.unsqueeze(2).broadcast_to([128, 31, 128]), ALU.mult, [self.R_const], [Rdwd])
            for t in self.mt:
                b = self.bank()
                n = self.tn(t)
                for k in range(31):
                    if t[0] == "p":
                        rhs = u[:, t[1] + k:t[1] + k + n]
                    else:
                        b0 = 30 + PT
                        rhs = u[:, b0:b0 + NS * 94].rearrange("p (s c) -> p s c", c=94)[:, :, k:k + 64]
                    self.MM(self.bk(b, t), dwd[:, k * 128:(k + 1) * 128], rhs, k == 0, k == 30, [Ru, Rdwd], [self.R_bank[b]])
                self.ACT(self.pl(self.big3[:, m, :], t, three=False), self.banks[b][:, 0:n], AF.Identity,
                         [self.R_bank[b], self.R_const], [self.R_big[m]], bias=self.par(("dwb", j), m))
        self.colstats(True, LN_EPS)
        tmp = self.STG[:, 0:T]
        tmp2 = self.STG[:, T:2 * T]
        Rt, Rt2 = self.R("xs0"), self.R("xs1")
        for kc in range(KC):
            b = self.big3[:, kc, :]
            self.TT("dve", tmp, b, self.mean, ALU.subtract, [self.R_big[kc], self.R_stat], [Rt])
            self.TT("dve", tmp, tmp, self.rstd, ALU.mult, [self.R_stat], [Rt])
            self.TS("dve", tmp, tmp, self.par(("clng", j), kc), self.par(("clnb", j), kc), ALU.mult, ALU.add, [self.R_const], [Rt])
            self.ACT(tmp2, tmp, AF.Sigmoid, [Rt], [Rt2])
            self.TT("dve", self.ht3[:, kc, 0:T], tmp, tmp2, ALU.mult, [Rt, Rt2], [self.R_ht[kc], self.R_hthalo])
        for m in range(KC):
            ws, Rw = self.wload(self.d_pw2, j * KC + m)
            for t in self.mt:
                b = self.bank()
                n = self.tn(t)
                for kc in range(KC):
                    self.MM(self.banks[b][:, 0:n], ws[:, kc * 128:(kc + 1) * 128], self.pl(self.ht3[:, kc, 0:T], t, three=False),
                            kc == 0, kc == KC - 1, [self.R_ht[kc], Rw], [self.R_bank[b]])
                self.ACT(self.pl(self.big3[:, m, :], t, three=False), self.banks[b][:, 0:n], AF.Identity,
                         [self.R_bank[b], self.R_const], [self.R_big[m]], bias=self.par(("pw2b", j), m))

    def ffn(self, i):
        c = self.cfg
        KC, FC, T, PT, NS, G, D, GC = c.KC, c.FC, c.T, c.PT, c.NS, c.G, c.D, c.GC
        wo = [self.AR2[:, 0:G * D], self.AR2[:, G * D:2 * G * D]]
        Rwo = [self.R("wo0"), self.R("wo1")]
        zg = self.AR2[:, 2 * G * D:2 * G * D + G * T].rearrange("p (g t) -> p g t", t=T)
        Rzg = [self.R("zg%d" % gi) for gi in range(G)]
        t2 = self.STG[:, T:2 * T]
        Rt2 = self.R("xs1")
        stgb = self.STG[:, 0:T].bitcast(BF16)
        gps = [stgb[:, 0:GC]]
        dw3s = [self.DW3[:, 0:384], self.DW3[:, 384:768]]
        Rgps = [self.R("xs0"), self.R("xs0")]
        Rdw = self.R("dw3")
        shst = self.STG[:, 2 * T:2 * T + 2 * NS]
        Rsh = self.R("stg2")
        ident = self.cbv(CB_ID, 128)
        for g in range(FC // G):
            wsl, Rw_o = wo[g % 2], Rwo[g % 2]
            self.DMA("pool", wsl, self.d_wout[i * (FC // G) + g], [], [Rw_o], "wo%d" % (g % 2))
            for gi in range(G):
                f = g * G + gi
                gp, Rgp = gps[0], Rgps[0]
                dw3 = dw3s[f % 2]
                g3 = gp[:, 2 + PT:2 + PT + NS * 66].rearrange("p (s c) -> p s c", c=66)
                ws_u, R_u = self.wload(self.d_win, i * 2 * FC + f)
                ws_g, R_g = self.wload(self.d_win, i * 2 * FC + FC + f)
                so = ((i * FC + f) * NS) * 2
                self.DMA("sp", shst, self.d_stffn[:, so:so + NS * 2], [], [Rsh], "gph")
                self.CP("act", g3[:, :, 0:2], shst.rearrange("p (s c) -> p s c", c=2), [Rsh], [Rgp])
                sl = f % 2
                ofs = self.OST[:, sl * (1 + NS) * 32:sl * (1 + NS) * 32 + (1 + NS) * 2]
                Rofs = self.R("ost%d" % sl)
                for t in self.mt + [("h2",)]:
                    b = self.bank()
                    n = self.tn(t)
                    if t[0] == "h2":
                        self.halo_recv()
                    for kc in range(KC):
                        rr = [self.R_ht[kc], R_g] + ([self.R_hthalo] if t[0] == "h2" else [])
                        self.MM(self.bk(b, t), ws_g[:, kc * 128:(kc + 1) * 128], self.hrhs(kc, t), kc == 0, kc == KC - 1,
                                rr, [self.R_bank[b]])
                    if t[0] == "h2":
                        dst = gp[:, 0:2]
                    elif t[0] == "p":
                        dst = gp[:, 2 + t[1]:2 + t[1] + n]
                    else:
                        dst = g3[:, :, 2:66]
                    self.CP("act", dst, self.bk(b, t), [self.R_bank[b]], [Rgp])
                    if t[0] == "p" and t[1] + n == PT:
                        self.CP("act", ofs[:, 0:2], self.banks[b][:, n - 2:n], [self.R_bank[b]], [Rofs])
                    if t[0] == "s":
                        self.CP("act", ofs[:, 2:(1 + NS) * 2].rearrange("p (s c) -> p s c", c=2), self.bk(b, t)[:, :, 62:64],
                                [self.R_bank[b]], [Rofs])
                oo = (i * FC + f) * (1 + NS) * 2
                self.DMA("sp", self.d_offn[:, oo:oo + (1 + NS) * 2], ofs, [Rofs], [self.R("d_offn")], "ost%d" % sl)
                wk = self.par(("fdw", i), f * 3, 3)
                self.TT("dve", dw3.rearrange("p (k j) -> p k j", j=128), ident.unsqueeze(1).broadcast_to([128, 3, 128]),
                        wk.unsqueeze(2).broadcast_to([128, 3, 128]), ALU.mult, [self.R_const], [Rdw])
                bb = self.par(("fdb", i), f)
                for t in self.mt:
                    b = self.bank()
                    n = self.tn(t)
                    for k in range(3):
                        rhs = gp[:, t[1] + k:t[1] + k + n] if t[0] == "p" else g3[:, :, k:k + 64]
                        self.MM(self.bk(b, t), dw3[:, k * 128:(k + 1) * 128], rhs, k == 0, k == 2, [Rgp, Rdw], [self.R_bank[b]])
                    self.ACT(self.pl(t2, t, three=False), self.banks[b][:, 0:n], AF.Gelu_apprx_tanh, [self.R_bank[b], self.R_const],
                             [Rt2], bias=bb)
                for t in self.mt:
                    b = self.bank()
                    n = self.tn(t)
                    for kc in range(KC):
                        self.MM(self.bk(b, t), ws_u[:, kc * 128:(kc + 1) * 128], self.hrhs(kc, t), kc == 0, kc == KC - 1,
                                [self.R_ht[kc], R_u], [self.R_bank[b]])
                    self.TT("dve", self.pl(zg[:, gi, :], t, three=False), self.banks[b][:, 0:n], self.pl(t2, t, three=False),
                            ALU.mult, [self.R_bank[b], Rt2], [Rzg[gi]])
            for m in range(KC):
                for t in self.mt:
                    b = self.bank()
                    n = self.tn(t)
                    for gi in range(G):
                        self.MM(self.banks[b][:, 0:n], wsl[:, gi * D + m * 128:gi * D + (m + 1) * 128],
                                self.pl(zg[:, gi, :], t, three=False), gi == 0, gi == G - 1, [Rzg[gi], Rw_o], [self.R_bank[b]])
                    dst = self.pl(self.big3[:, m, :], t, three=False)
                    if g == 0:
                        self.CP("act", dst, self.banks[b][:, 0:n], [self.R_bank[b]], [self.R_big[m]])
                    else:
                        self.TT("dve", dst, dst, self.banks[b][:, 0:n], ALU.add, [self.R_bank[b]], [self.R_big[m]])

    def rwkv_decl(self):
        c = self.cfg
        nc = self.nc
        if c.NRL == 0:
            return
        KC, HP, T, NS = c.KC, c.HP, c.T, c.NS
        di = lambda name, shape, dt=F32: nc.dram_tensor(name, list(shape), dt, kind="ExternalInput").ap()
        do = lambda name, shape, dt=F32: nc.dram_tensor(name, list(shape), dt, kind="ExternalOutput").ap()
        self.d_rkvo = di("w_rkvo", [c.NRL * 4 * HP, 128, KC * 128])
        self.d_l1 = di("w_l1", [c.NRL * 5, 128, KC * 128])
        self.d_l2 = di("w_l2", [c.NRL * HP, 128, 5 * 128])
        self.d_stshift = di("st_shift", [128, c.NRL * KC * NS])
        self.d_stwkv = di("st_wkv", [128, c.NRL * HP * NS * 64])
        self.d_oshift = do("o_shift", [128, c.NRL * KC * (1 + NS)])
        self.d_owkv = do("o_wkv", [128, c.NRL * HP * (1 + NS) * 64])
        self.d_rkv = nc.dram_tensor("rkv_s", [c.NRL * 3 * HP, 128, T], F32).ap()
        self.d_ycb = nc.dram_tensor("ycb_s", [3 * HP, 128, T], F32).ap()
        self.NQ = 4 if HP % 4 == 0 else 1
        hq = HP // self.NQ
        self.t_sgin = [nc.dram_tensor("seg_in%d" % q, [128, hq * 128], F32) for q in range(self.NQ)]
        self.t_sgmid = [nc.dram_tensor("seg_mid%d" % q, [4 * 128, hq * 128], F32) for q in range(self.NQ)]
        self.t_sgout = [nc.dram_tensor("seg_out%d" % q, [NCORES * 128, hq * 128], F32) for q in range(self.NQ)]

    def rwkv_alloc(self):
        c = self.cfg
        KC, T, NS, HP = c.KC, c.T, c.NS, c.HP
        if not hasattr(self, "LW"):
            sb0 = lambda name, cols, dt: self.stack.enter_context(self.nc.sbuf_tensor(name, [128, cols], dt))
            self.LW = [self.AR2[:, 8448:9088], self.AR2[:, 9088:9728]]
            self.SGL = self.STG[:, 2 * T:3 * T].bitcast(BF16)
            self.OSH = sb0("osh", KC * (1 + NS) + KC, F32)
            self.SEGB = self.STAT[:, 0:HP * 128]
            self.SMALL = self.AR2[:, 5888:8448].bitcast(F32)

    def rwkv(self, j):
        c = self.cfg
        KC, HP, T, PT, NS, NU, NCH, D = c.KC, c.HP, c.T, c.PT, c.NS, c.NU, c.NCH, c.D
        sb = lambda name, cols, dt: self.stack.enter_context(self.nc.sbuf_tensor(name + "_%d" % j, [128, cols], dt))
        i_layer = 2 * j + 1
        bigb = self.BIG[:, :].bitcast(BF16)
        xm3 = bigb[:, 0:KC * T].rearrange("p (k t) -> p k t", t=T)
        dd3 = bigb[:, KC * T:2 * KC * T].rearrange("p (k t) -> p k t", t=T)
        R_xm = [Res("xm%d" % k) for k in range(KC)]
        R_dd = [Res("dd%d" % k) for k in range(KC)]
        ht3 = self.ht3
        b0 = 30 + PT
        hs3 = lambda kc: ht3[:, kc, b0:b0 + NS * 65].rearrange("p (s c) -> p s c", c=65)
        ident = self.cbv(CB_ID, 128)
        bones = self.cbv(CB_BONES, 128)
        I2 = self.cbv(CB_I2, 64)
        omk = self.OSH[:, KC * (1 + NS):KC * (1 + NS) + KC]
        self.TS("dve", omk, self.par(("ka", j), 0, KC), -1.0, 1.0, ALU.mult, ALU.add, [self.R_const], [self.R("omk")])
        for kc in range(KC):
            rr = [self.R_ht[kc], self.R_hthalo]
            eng = "dve"
            self.TT(eng, dd3[:, kc, 0:PT], ht3[:, kc, 29:29 + PT], ht3[:, kc, 30:30 + PT], ALU.subtract, rr, [R_dd[kc]])
            self.TT(eng, dd3[:, kc, PT:T].rearrange("p (s c) -> p s c", c=64), hs3(kc)[:, :, 0:64], hs3(kc)[:, :, 1:65],
                    ALU.subtract, rr, [R_dd[kc]])
        tw, xa1, xv1 = self.SQ[0], self.SQ[1], self.SQ[2]
        R_tw, R_xa1, R_xv1, R_sgl = self.R_sq[0], self.R_sq[1], self.R_sq[2], self.R("sgl")
        stg = [self.STG[:, 0:T], self.STG[:, T:2 * T]]
        Rstg = [self.R("xs0"), self.R("xs1")]
        stgi = 0
        for q in range(6):
            for kc in range(KC):
                mx = self.par(("mix", j, q), kc)
                eng = "dve"
                xs_ = xm3[:, kc, PT:T].rearrange("p (s c) -> p s c", c=64)
                ds_ = dd3[:, kc, PT:T].rearrange("p (s c) -> p s c", c=64)
                rr_ = [R_dd[kc], self.R_ht[kc], self.R_const]
                if eng == "dve":
                    self.STT(eng, xm3[:, kc, 0:PT], dd3[:, kc, 0:PT], mx, ht3[:, kc, 30:30 + PT], ALU.mult, ALU.add, rr_, [R_xm[kc]])
                    self.STT(eng, xs_, ds_, mx, hs3(kc)[:, :, 1:65], ALU.mult, ALU.add, rr_, [R_xm[kc]])
                else:
                    self.TS(eng, xm3[:, kc, 0:PT], dd3[:, kc, 0:PT], mx, None, ALU.mult, None, rr_, [R_xm[kc]])
                    self.TT(eng, xm3[:, kc, 0:PT], xm3[:, kc, 0:PT], ht3[:, kc, 30:30 + PT], ALU.add, rr_, [R_xm[kc]])
                    self.TS(eng, xs_, ds_, mx, None, ALU.mult, None, rr_, [R_xm[kc]])
                    self.TT(eng, xs_, xs_, hs3(kc)[:, :, 1:65], ALU.add, rr_, [R_xm[kc]])
            if q in (0, 2, 3):
                qq = {0: 0, 2: 1, 3: 2}[q]
                for m in range(HP):
                    ws, Rw = self.wload(self.d_rkvo, (j * 4 + qq) * HP + m)
                    sg_, Rs_ = stg[stgi % 2], Rstg[stgi % 2]
                    stgi += 1
                    for (o, n) in self.ptiles:
                        b = self.bank()
                        for kc in range(KC):
                            self.MM(self.banks[b][:, 0:n], ws[:, kc * 128:(kc + 1) * 128], xm3[:, kc, o:o + n], kc == 0, kc == KC - 1,
                                    [R_xm[kc], Rw], [self.R_bank[b]])
                        self.CP("act", sg_[:, o:o + n], self.banks[b][:, 0:n], [self.R_bank[b]], [Rs_])
                    self.DMA("sp", self.d_rkv[(j * 3 + qq) * HP + m], sg_, [Rs_], [self.R("rkv%d_%d_%d" % (j, qq, m))], "rkvst%d" % ((stgi - 1) % 2))
            if q in (1, 4, 5) or (q == 3 and j > 0):
                specs = {1: [(0, 96, tw, R_tw, AF.Tanh, 0)], 4: [(1, 96, xa1, R_xa1, AF.Identity, 0)],
                         5: [(2, 128, self.SGL, R_sgl, AF.Sigmoid, 0), (3, 128, self.SGL, R_sgl, AF.Sigmoid, T)],
                         3: [(4, 64, xv1, R_xv1, AF.Identity, 0)]}[q]
                for (row, mc, dstt, Rd, fn, co) in specs:
                    ws, Rw = self.wload(self.d_l1, j * 5 + row)
                    for (o, n) in self.ptiles:
                        b = self.bank()
                        for kc in range(KC):
                            self.MM(self.banks[b][0:mc, 0:n], ws[:, kc * mc:(kc + 1) * mc], xm3[:, kc, o:o + n], kc == 0, kc == KC - 1,
                                    [R_xm[kc], Rw], [self.R_bank[b]])
                        self.ACT(dstt[0:mc, co + o:co + o + n], self.banks[b][0:mc, 0:n], fn, [self.R_bank[b]], [Rd])
        if DBG == 1:
            return
        self.barrier()
        self.rwkv_scan(j)
        self.barrier()
        if DBG in (2, 3, 4) or DBG >= 20:
            return
        for m in range(HP):
            ws, Rw = self.wload(self.d_rkvo, (j * 4 + 3) * HP + m)
            for (o, n) in self.ptiles:
                b = self.bank()
                for kc in range(KC):
                    self.MM(self.banks[b][:, 0:n], ws[:, kc * 128:(kc + 1) * 128], ht3[:, kc, o:o + n], kc == 0, kc == KC - 1,
                            [self.R_ht[kc], Rw], [self.R_bank[b]])
                self.CP("act", self.big3[:, m, o:o + n], self.banks[b][:, 0:n], [self.R_bank[b]], [self.R_big[m]])

    def rwkv_scan(self, j):
        c = self.cfg
        KC, HP, T, PT, NS, NU, NCH, D = c.KC, c.HP, c.T, c.PT, c.NS, c.NU, c.NCH, c.D
        assert NU % 4 == 0
        NB = NU * 128
        needB = 5 * NB + NB + NU * 64 + 8 * 512
        if not hasattr(self, "arF"):
            sb0 = lambda name, cols, dt: self.stack.enter_context(self.nc.sbuf_tensor(name, [128, cols], dt))
            self.arF = self.BIG if KC >= 15 else sb0("arF", 15 * T, F32)
            self.arB = self.HT if KC * c.HC >= needB else sb0("arB", needB, BF16)
            self.RF = [Res("f%d" % i) for i in range(14)]
            self.RQ = [[Res("bd%d_%d" % (i, g)) for g in range(NU // 4)] for i in range(6)]
            self.RVS = [Res("vs%d" % g) for g in range(NU // 4)]
            self.RT8 = [Res("t8_%d" % i) for i in range(8)]
            self.RT8b = [Res("t8b_%d" % i) for i in range(8)]
            self.RT8c = [Res("t8c_%d" % i) for i in range(8)]
            self.RT8d = [Res("t8d_%d" % i) for i in range(8)]
            self.RT8e = [Res("t8e_%d" % i) for i in range(8)]
            self.zeroed = False
        arF, arB, RF = self.arF, self.arB, self.RF
        F = lambda i: arF[:, i * T:(i + 1) * T]
        r_, k_, v_, a_, lw_, kk_, b_, cs0, cs1, e_, Y_, tmp, vf_, bon = [F(i) for i in range(14)]
        Rr, Rk, Rv, Ra, Rlw, Rkk, Rb, Rcs0, Rcs1, Re, RY, Rtmp, Rvf, Rbon = RF
        U3 = lambda ap: ap.rearrange("p (u c) -> p u c", c=64)
        Qbd, Kbd, Pbd, Vbd, RD, EE = [arB[:, i * NB:(i + 1) * NB] for i in range(6)]
        RQ, RK, RP, RV, RRD, REE = self.RQ
        VS = arB[:, 6 * NB:6 * NB + NU * 64]
        T8 = [arB[:, 6 * NB + NU * 64 + i * 512:6 * NB + NU * 64 + (i + 1) * 512] for i in range(8)]
        A4, AT4, S0, T0, AkT4, X4, H4, PT4 = T8
        RA4, RAT4, RS0, RT0, RAk, RX, RH, RPT = self.RT8
        bd4 = lambda ap: ap.rearrange("p (u h c) -> p u h c", h=2, c=64)
        ident = self.cbv(CB_ID, 128)
        bones = self.cbv(CB_BONES, 128)
        I2 = self.cbv(CB_I2, 64)
        MST, MSTT, MG = self.cbv(CB_MST, 512), self.cbv(CB_MSTT, 512), self.cbv(CB_MG, 512)
        Rc = self.R_const
        G4 = NU // 4
        SM = self.SMALL
        smb = SM[:, 0:640].bitcast(BF16)
        SS = [smb[:, 0:128], smb[:, 128:256]]
        STbd, TTbd = smb[:, 256:384], smb[:, 384:512]
        S0st = smb[:, 512:512 + NS * 64]
        S0bd = smb[:, 768:768 + NS * 128]
        RSS = [self.R("ss0"), self.R("ss1")]
        RSTbd, RTTbd, RS0st, RS0bd = self.R("stbd"), self.R("ttbd"), self.R("s0st"), self.R("s0bd")
        s0stg = SM[:, 640:640 + NS * 64]
        owk = SM[:, 640 + NS * 64:640 + NS * 64 + (1 + NS) * 64]
        Rs0stg, Rowk = self.R("s0stg"), self.R("owk")
        if True:
            for buf, RR in ((Qbd, RQ), (Kbd, RK), (Pbd, RP), (Vbd, RV)):
                self.add("dve", lambda e, buf=buf: e.memset(buf, 0.0), [], RR)
            self.add("dve", lambda e: e.memset(STbd, 0.0), [], [RSTbd])
            self.add("dve", lambda e: e.memset(TTbd, 0.0), [], [RTTbd])
            self.add("dve", lambda e: e.memset(S0bd, 0.0), [], [RS0bd])
        allg = lambda RR: list(RR)
        bq = [0]

        def nb():
            bq[0] += 1
            return (bq[0] - 1) % 8
        pt_ = self.ptiles
        def prepA(m):
                lw2, Rlw2 = self.LW[m % 2], self.R("lw2_%d" % (m % 2))
                self.DMA("pool", lw2, self.d_l2[j * HP + m], [], [Rlw2], "lw2_%d" % (m % 2))
                for (dst, Rd, qq) in ((r_, Rr, 0), (k_, Rk, 1), (v_, Rv, 2)):
                    self.DMA("sp", dst, self.d_rkv[(j * 3 + qq) * HP + m], [self.R("rkv%d_%d_%d" % (j, qq, m))], [Rd], "ld%d" % qq)
                if j > 0:
                    self.DMA("sp", vf_, self.d_rkv[(0 * 3 + 2) * HP + m], [self.R("rkv%d_%d_%d" % (0, 2, m))], [Rvf], "ldvf")
                tw, xa1, xv1 = self.SQ[0], self.SQ[1], self.SQ[2]
                for (o, n) in pt_:
                    b = nb()
                    self.MM(self.banks[b][:, 0:n], lw2[0:96, 0:128], tw[0:96, o:o + n], True, True, [Rlw2, self.R_sq[0]], [self.R_bank[b]])
                    self.ACT(lw_[:, o:o + n], self.banks[b][:, 0:n], AF.Sigmoid, [self.R_bank[b], Rc], [Rlw], bias=self.par(("w0", j), m))
                    b = nb()
                    self.MM(self.banks[b][:, 0:n], lw2[0:96, 128:256], xa1[0:96, o:o + n], True, True, [Rlw2, self.R_sq[1]], [self.R_bank[b]])
                    self.ACT(a_[:, o:o + n], self.banks[b][:, 0:n], AF.Sigmoid, [self.R_bank[b], Rc], [Ra], bias=self.par(("a0", j), m))
                    if j > 0:
                        b = nb()
                        self.MM(self.banks[b][:, 0:n], lw2[0:64, 256:384], xv1[0:64, o:o + n], True, True, [Rlw2, self.R_sq[2]], [self.R_bank[b]])
                        self.ACT(tmp[:, o:o + n], self.banks[b][:, 0:n], AF.Sigmoid, [self.R_bank[b], Rc], [Rtmp], bias=self.par(("v0", j - 1), m))
                self.ACT(lw_, lw_, AF.Identity, [], [Rlw], scale=-0.6065306597126334)
                if j > 0:
                    self.TT("pool", vf_, vf_, v_, ALU.subtract, [Rv], [Rvf])
                    self.TT("pool", vf_, vf_, tmp, ALU.mult, [Rtmp], [Rvf])
                    self.TT("pool", v_, v_, vf_, ALU.add, [Rvf], [Rv])
                yield
                self.ACT(kk_, k_, AF.Identity, [Rk, Rc], [Rkk], scale=self.par(("kk", j), m))
                sq, Rsq = self.SQ[3], self.R_sq[3]
                self.ACT(sq[:, :], kk_, AF.Square, [Rkk], [Rsq])
                for (o, n) in pt_:
                    b = nb()
                    self.MM(self.banks[b][:, 0:n], bones, sq[:, o:o + n], True, True, [Rsq, Rc], [self.R_bank[b]])
                    self.ACT(tmp[:, o:o + n], self.banks[b][:, 0:n], AF.Sqrt, [self.R_bank[b]], [Rtmp])
                self.TS("dve", tmp, tmp, 1e-12, None, ALU.max, None, [], [Rtmp])
                self.add("dve", lambda e: e.reciprocal(out=tmp, in_=tmp), [], [Rtmp])
                self.TT("dve", kk_, kk_, tmp, ALU.mult, [Rtmp], [Rkk])
                yield
                omk = self.OSH[:, KC * (1 + NS) + m:KC * (1 + NS) + m + 1]
                self.TS("dve", tmp, a_, self.par(("ka", j), m), omk, ALU.mult, ALU.add, [Ra, Rc, self.R("omk")], [Rtmp])
                self.TT("dve", k_, k_, tmp, ALU.mult, [Rtmp], [Rk])
                self.TT("dve", b_, kk_, a_, ALU.mult, [Rkk, Ra], [Rb])
                yield
                self.STT("dve", sq[:, :], r_, self.par(("rk", j), m), k_, ALU.mult, ALU.mult, [Rr, Rk, Rc], [Rsq])
                for (o, n) in pt_:
                    b = nb()
                    self.MM(self.banks[b][:, 0:n], bones, sq[:, o:o + n], True, True, [Rsq, Rc], [self.R_bank[b]])
                    self.TT("dve", bon[:, o:o + n], self.banks[b][:, 0:n], v_[:, o:o + n], ALU.mult, [self.R_bank[b], Rv], [Rbon])
                self.DMA("sp", self.d_ycb[2 * HP + m], bon, [Rbon], [self.R("ycb2_%d" % m)], "stbon")
                src, Rs_, dst, Rd_ = lw_, Rlw, cs0, Rcs0
                for d in (1, 2, 4, 8, 16, 32):
                    self.TT("pool", U3(dst)[:, :, d:64], U3(src)[:, :, d:64], U3(src)[:, :, 0:64 - d], ALU.add, [Rs_], [Rd_])
                    self.CP("act", U3(dst)[:, :, 0:d], U3(src)[:, :, 0:d], [Rs_], [Rd_])
                    yield
                    if src is lw_:
                        src, Rs_, dst, Rd_ = cs0, Rcs0, cs1, Rcs1
                    else:
                        src, Rs_, dst, Rd_ = dst, Rd_, src, Rs_
                cs, Rcs, oth, Roth = src, Rs_, dst, Rd_
                gcb = self.SMALL[:, 1216:1216 + NU]
                Rgcb = self.R("gcb")
                self.ACT(e_, cs, AF.Exp, [Rcs], [Re])
                self.TT("pool", r_, r_, e_, ALU.mult, [Re], [Rr])
                self.CP("act", gcb.unsqueeze(2), U3(e_)[:, :, 63:64], [Re], [Rgcb])
                yield
                self.TT("dve", oth, cs, lw_, ALU.subtract, [Rcs, Rlw], [Roth])
                self.ACT(e_, oth, AF.Exp, [Roth], [Re])
                self.TT("pool", kk_, kk_, e_, ALU.mult, [Re], [Rkk])
                yield
                self.ACT(e_, cs, AF.Exp, [Rcs], [Re], scale=-1.0)
                self.TT("dve", b_, b_, e_, ALU.mult, [Re], [Rb])
                self.TT("pool", k_, k_, e_, ALU.mult, [Re], [Rk])
                yield

        def prepB(m):
            RD3 = RD.rearrange("p (u n) -> p u n", n=128)
            gcb = self.SMALL[:, 1216:1216 + NU]
            Rgcb = self.R("gcb")
            self.CP("act", RD3[:, :, 0:64], U3(r_), [Rr], allg(RRD))
            self.TT("dve", RD3[:, :, 64:128], I2.unsqueeze(1).broadcast_to([128, NU, 64]),
                    gcb.unsqueeze(2).broadcast_to([128, NU, 64]), ALU.mult, [Rgcb, Rc], allg(RRD))
            for h in range(2):
                ps = slice(64 * h, 64 * h + 64)
                self.CP("pool", bd4(Pbd)[ps, :, h, :], U3(kk_)[ps], [Rkk], allg(RP))
                self.CP("dve" if h else "pool", bd4(Qbd)[ps, :, h, :], U3(b_)[ps], [Rb], allg(RQ))
                self.CP("act" if h else "dve", bd4(Kbd)[ps, :, h, :], U3(k_)[ps], [Rk], allg(RK))
                self.CP("act", bd4(Vbd)[ps, :, h, :], U3(v_)[ps], [Rv], allg(RV))

        def groups(m):
                def group_steps(g, T8s, RT8s):
                    A4, AT4, S0, T0, AkT4, X4, H4, PT4 = T8s
                    RA4, RAT4, RS0, RT0, RAk, RX, RH, RPT = RT8s
                    us = list(range(4 * g, 4 * g + 4))

                    def mmu(lbuf, Rl, rfn, Rr_, oc=128, lfn=None):
                        b = nb()
                        for ui, u in enumerate(us):
                            l = lbuf[:, u * 128:(u + 1) * 128] if lfn is None else lfn(ui)
                            self.MM(self.banks[b][:, ui * oc:(ui + 1) * oc], l, rfn(ui, u), True, True, list(Rl) + list(Rr_), [self.R_bank[b]])
                        return b
                    ub = lambda buf: (lambda ui, u: buf[:, u * 128:(u + 1) * 128])
                    t4 = lambda buf: (lambda ui, u=None: buf[:, ui * 128:(ui + 1) * 128])
                    cst = lambda ap: (lambda ui, u: ap)
                    b = mmu(Qbd, [RQ[g]], ub(Pbd), [RP[g]])
                    self.TT("dve", A4, self.banks[b][:, :], MST, ALU.mult, [self.R_bank[b], Rc], [RA4])
                    b = mmu(Pbd, [RP[g]], ub(Qbd), [RQ[g]])
                    self.TT("dve", AT4, self.banks[b][:, :], MSTT, ALU.mult, [self.R_bank[b], Rc], [RAT4])
                    yield
                    b = mmu(Pbd, [RP[g]], ub(Kbd), [RK[g]])
                    self.TT("dve", AkT4, self.banks[b][:, :], MSTT, ALU.mult, [self.R_bank[b], Rc], [RAk])
                    b = mmu(Qbd, [RQ[g]], ub(RD), [RRD[g]])
                    self.TT("dve", X4, self.banks[b][:, :], MG, ALU.mult, [self.R_bank[b], Rc], [RX])
                    yield
                    b = mmu(Kbd, [RK[g]], ub(RD), [RRD[g]])
                    self.TT("dve", H4, self.banks[b][:, :], MG, ALU.mult, [self.R_bank[b], Rc], [RH])
                    b = mmu(Pbd, [RP[g]], cst(ident), [Rc])
                    self.CP("act", PT4, self.banks[b][:, :], [self.R_bank[b]], [RPT])
                    yield
                    b2 = mmu(Vbd, [RV[g]], cst(I2), [Rc], oc=64)
                    b = mmu(Vbd, [RV[g]], cst(ident), [Rc])
                    self.CP("act", VS[:, 4 * g * 64:(4 * g + 4) * 64], self.banks[b2][:, 0:256], [self.R_bank[b2]], [self.RVS[g]])
                    self.CP("act", Vbd[:, 4 * g * 128:(4 * g + 4) * 128], self.banks[b][:, :], [self.R_bank[b]], [RV[g]])
                    yield
                    b = mmu(None, [RAT4], t4(X4), [RX], lfn=t4(AT4))
                    self.TT("dve", X4, X4, self.banks[b][:, :], ALU.subtract, [self.R_bank[b]], [RX])
                    yield
                    Pc, RPc, PTc, RPTc = A4, RA4, AT4, RAT4
                    Pn, RPn, PTn, RPTn = S0, RS0, T0, RT0
                    for lvl in range(5):
                        if lvl < 4:
                            b = mmu(None, [RPTc], t4(Pc), [RPc], lfn=t4(PTc))
                            self.CP("act", Pn, self.banks[b][:, :], [self.R_bank[b]], [RPn])
                        b = mmu(None, [RPc], t4(PTc), [RPTc], lfn=t4(Pc))
                        self.CP("act", PTn, self.banks[b][:, :], [self.R_bank[b]], [RPTn])
                        yield
                        b = mmu(None, [RPTn], t4(X4), [RX], lfn=t4(PTn))
                        self.TT("dve", X4, X4, self.banks[b][:, :], ALU.add, [self.R_bank[b]], [RX])
                        yield
                        Pc, RPc, PTc, RPTc, Pn, RPn, PTn, RPTn = Pn, RPn, PTn, RPTn, Pc, RPc, PTc, RPTc
                    b = mmu(None, [RAk], t4(X4), [RX], lfn=t4(AkT4))
                    gs = slice(4 * g * 128, (4 * g + 4) * 128)
                    self.TT("dve", EE[:, gs], H4, self.banks[b][:, :], ALU.subtract, [self.R_bank[b], RH], [REE[g]])
                    b = mmu(None, [RPT], t4(X4), [RX], lfn=t4(PT4))
                    self.TT("dve", RD[:, gs], RD[:, gs], self.banks[b][:, :], ALU.subtract, [self.R_bank[b]], [RRD[g]])
                    yield
                    RDg = RD[:, gs].rearrange("p (u n) -> p u n", n=128)
                    EEg = EE[:, gs].rearrange("p (u n) -> p u n", n=128)
                    for h in range(2):
                        ps = slice(64 * h, 64 * h + 64)
                        self.CP("act", bd4(Qbd[:, gs])[ps, :, h, :], RDg[ps, :, 64:128], [RRD[g]], [RQ[g]])
                        self.CP("act", bd4(Kbd[:, gs])[ps, :, h, :], EEg[ps, :, 64:128], [REE[g]], [RK[g]])

                T8b = [self.AR2[:, i * 512:(i + 1) * 512] for i in range(8)]
                NW = 3 if self.cfg.KC * 128 >= 2048 else 2
                sets = [(T8, self.RT8), (T8b, self.RT8b)]
                if NW == 3:
                    sets.append(([self.WS[i // 4][:, (i % 4) * 512:(i % 4 + 1) * 512] for i in range(8)], self.RT8c))
                    if KC >= 16 and G4 >= 5:
                        NW = 5
                        stb = self.STG[:, 0:2 * T].bitcast(BF16)
                        sets.append(([stb[:, i * 512:(i + 1) * 512] for i in range(8)], self.RT8d))
                        f15 = F(15).bitcast(BF16)
                        sets.append(([self.WS[2][:, i * 512:(i + 1) * 512] for i in range(4)] +
                                     [f15[:, i * 512:(i + 1) * 512] for i in range(4)], self.RT8e))
                for g0 in range(0, G4, NW):
                    gens = [group_steps(g, *sets[(g - g0) % NW]) for g in range(g0, min(g0 + NW, G4))]
                    while gens:
                        for gen in list(gens):
                            try:
                                next(gen)
                            except StopIteration:
                                gens.remove(gen)
                        yield

        def seqpass(m):
                M1u = lambda u: RD[:, u * 128:u * 128 + 64]
                Eu = lambda u: EE[:, u * 128:u * 128 + 64]
                Msb = lambda u: Qbd[:, u * 128:(u + 1) * 128]
                Esb = lambda u: Kbd[:, u * 128:(u + 1) * 128]
                VTb = lambda u: Vbd[:, u * 128:(u + 1) * 128]
                VSu = lambda u: VS[:, u * 64:(u + 1) * 64]
                self.add("dve", lambda e: e.memset(SS[0][:, 0:64], 0.0), [], [RSS[0]])
                self.CP("dve", SS[0][:, 64:128], I2, [Rc], [RSS[0]])
                for h in range(2):
                    ps = slice(64 * h, 64 * h + 64)
                    self.add("dve", lambda e, ps=ps, h=h: e.memset(STbd[ps, 64 * h:64 * h + 64], 0.0), [], [RSTbd])
                    self.CP("dve", TTbd[ps, 64 * h:64 * h + 64], I2[ps, :], [Rc], [RTTbd])
                cf = F(14)
                Rcf_ = self.R("f14")
                for cgrp in range(0, NCH, 4):
                    by, bc = nb(), nb()
                    cl = list(range(cgrp, min(cgrp + 4, NCH)))
                    for ci, ch in enumerate(cl):
                        g = ch // 4
                        cur, nxt = SS[ch % 2], SS[(ch + 1) % 2]
                        Rcur, Rnxt = RSS[ch % 2], RSS[(ch + 1) % 2]
                        cs_ = slice(ci * 64, ci * 64 + 64)
                        bs = nb()
                        self.MM(self.banks[bs][:, 0:64], Msb(ch), cur[:, 0:64], True, False, [RQ[g], Rcur], [self.R_bank[bs]])
                        self.MM(self.banks[bs][:, 0:64], Esb(ch), VSu(ch), False, True, [RK[g], self.RVS[g]], [self.R_bank[bs]])
                        self.MM(self.banks[bs][:, 64:128], Msb(ch), cur[:, 64:128], True, True, [RQ[g], Rcur], [self.R_bank[bs]])
                        self.CP("dve", nxt, self.banks[bs][:, 0:128], [self.R_bank[bs]], [Rnxt])
                        self.MM(self.banks[by][:, cs_], STbd, M1u(ch), True, False, [RSTbd, RRD[g]], [self.R_bank[by]])
                        self.MM(self.banks[by][:, cs_], VTb(ch), Eu(ch), False, True, [RV[g], REE[g]], [self.R_bank[by]])
                        self.MM(self.banks[bc][:, cs_], TTbd, M1u(ch), True, True, [RTTbd, RRD[g]], [self.R_bank[bc]])
                        if ch == NCH - 1:
                            self.CP("dve", self.SEGB[:, m * 128:m * 128 + 64], self.banks[bs][:, 0:64], [self.R_bank[bs]], [self.R("segb")])
                        for h in range(2):
                            ps = slice(64 * h, 64 * h + 64)
                            self.CP("act", STbd[ps, 64 * h:64 * h + 64], nxt[ps, 0:64], [Rnxt], [RSTbd])
                            self.CP("dve", TTbd[ps, 64 * h:64 * h + 64], nxt[ps, 64:128], [Rnxt], [RTTbd])
                        yield
                    w = len(cl) * 64
                    self.CP("act", Y_[:, cgrp * 64:cgrp * 64 + w], self.banks[by][:, 0:w], [self.R_bank[by]], [RY])
                    self.CP("dve", cf[:, cgrp * 64:cgrp * 64 + w], self.banks[bc][:, 0:w], [self.R_bank[bc]], [Rcf_])
                b = nb()
                self.MM(self.banks[b][:, 0:64], TTbd, I2, True, True, [RTTbd, Rc], [self.R_bank[b]])
                self.CP("act", self.SEGB[:, m * 128 + 64:m * 128 + 128], self.banks[b][:, 0:64], [self.R_bank[b]], [self.R("segb")])
                self.DMA("sp", self.d_ycb[HP + m, :, 0:PT], cf[:, 0:PT], [Rcf_], [self.R("ycb1_%d" % m)], "stcf")
                so = ((j * HP + m) * NS) * 64
                self.DMA("sp", s0stg, self.d_stwkv[:, so:so + NS * 64], [], [Rs0stg], "s0ld")
                self.CP("act", S0st, s0stg, [Rs0stg], [RS0st])
                S0bd4 = S0bd.rearrange("p (s h c) -> p s h c", h=2, c=64)
                for h in range(2):
                    ps = slice(64 * h, 64 * h + 64)
                    self.CP("dve", S0bd4[ps, :, h, :], S0st[ps, :].rearrange("p (s c) -> p s c", c=64), [RS0st], [RS0bd])
                by, bs = nb(), nb()
                for s_ in range(NS):
                    u = NCH + s_
                    g = u // 4
                    cs_ = slice(s_ * 64, s_ * 64 + 64)
                    self.MM(self.banks[by][:, cs_], S0bd[:, s_ * 128:(s_ + 1) * 128], M1u(u), True, False, [RS0bd, RRD[g]], [self.R_bank[by]])
                    self.MM(self.banks[by][:, cs_], VTb(u), Eu(u), False, True, [RV[g], REE[g]], [self.R_bank[by]])
                    self.MM(self.banks[bs][:, cs_], Msb(u), S0st[:, cs_], True, False, [RQ[g], RS0st], [self.R_bank[bs]])
                    self.MM(self.banks[bs][:, cs_], Esb(u), VSu(u), False, True, [RK[g], self.RVS[g]], [self.R_bank[bs]])
                self.CP("act", Y_[:, PT:T], self.banks[by][:, 0:NS * 64], [self.R_bank[by]], [RY])
                self.CP("dve", owk[:, 64:(1 + NS) * 64], self.banks[bs][:, 0:NS * 64], [self.R_bank[bs]], [Rowk])
                oo = ((j * HP + m) * (1 + NS) + 1) * 64
                self.DMA("sp", self.d_owkv[:, oo:oo + NS * 64], owk[:, 64:(1 + NS) * 64], [Rowk], [self.R("d_owkv")], "stowk")
                self.DMA("sp", self.d_ycb[m], Y_, [RY], [self.R("ycb0_%d" % m)], "sty")

        def drain(gen):
            if gen is not None:
                for _ in gen:
                    pass

        def step(gen):
            if gen is None:
                return None
            try:
                next(gen)
                return gen
            except StopIteration:
                return None
        drain(prepA(0))
        for m in range(HP):
            prepB(m)
            nxt_prep = prepA(m + 1) if m + 1 < HP else None
            for _ in groups(m):
                nxt_prep = step(nxt_prep)
            for _ in seqpass(m):
                nxt_prep = step(nxt_prep)
            drain(nxt_prep)
        self.zeroed = True
        if DBG == 2 or DBG >= 20:
            return
        hq = HP // self.NQ
        for q in range(self.NQ):
            Rs1, Rs2 = self.R("seg_in%d" % q), self.R("seg_out%d" % q)
            self.DMA("sp", self.t_sgin[q].ap(), self.SEGB[:, q * hq * 128:(q + 1) * hq * 128], [self.R("segb")], [Rs1], "segst")
            self.allgather8(self.t_sgin[q], self.t_sgmid[q], self.t_sgout[q], Rs1, Rs2)
        if DBG == 3:
            return
        self.barrier()
        self.rwkv_finish(j)

    def rwkv_finish(self, j):
        c = self.cfg
        KC, HP, T, PT, NS, NU, NCH, D = c.KC, c.HP, c.T, c.PT, c.NS, c.NU, c.NCH, c.D
        arF, RF = self.arF, self.RF
        F = lambda i: arF[:, i * T:(i + 1) * T]
        slotsets = [(10, 13, 11, 7), (0, 1, 2, 3)]
        A2 = self.AR2
        A2f = A2[:, 0:10240].bitcast(F32)
        SG = A2f[:, 0:1024]
        SG3 = SG.rearrange("p (r n) -> p r n", n=128)
        Tbd8 = A2[:, 2048:3072]
        Pl = A2f[:, 1536:2112]
        Pb = [A2[:, 4224:4288], A2[:, 4288:4352]]
        Sst, Sbd = A2[:, 4352:4416], A2[:, 4416:4544]
        Ssel = A2f[:, 2304:2432]
        Ceffb = A2[:, 4864:4864 + PT]
        RSG, RT8, RPl, RSst, RSbd, RSsel, RCb = (self.R(n) for n in ("c_sg", "c_t8", "c_pl", "c_sst", "c_sbd", "c_ssel", "c_cb"))
        RPb = [self.R("c_pb0"), self.R("c_pb1")]
        bones = self.cbv(CB_BONES, 128)
        Rc = self.R_const
        SM = self.SMALL
        owk = SM[:, 640 + NS * 64:640 + NS * 64 + (1 + NS) * 64]
        Rowk = self.R("owk")
        hq = HP // self.NQ
        self.add("dve", lambda e: e.memset(Tbd8, 0.0), [], [RT8])
        self.add("dve", lambda e: e.memset(Sbd, 0.0), [], [RSbd])
        self.add("dve", lambda e: e.memset(Pb[0], 0.0), [], [RPb[0]])
        self.add("dve", lambda e: e.memset(Pl[:, 0:64], 0.0), [], [RPl])
        T84 = Tbd8.rearrange("p (r h c) -> p r h c", h=2, c=64)
        seg3 = [t.ap().rearrange("(r p) f -> p r f", p=128) for t in self.t_sgout]
        yb, ysq = self.SQ[0], self.SQ[1]
        Ryb, Rysq = self.R_sq[0], self.R_sq[1]
        bq = [0]

        def nb():
            bq[0] += 1
            return (bq[0] - 1) % 8
        def stage1(m):
            sY, sB, sC, sT = slotsets[m % 2]
            Y_, bon, cfst, t1 = F(sY), F(sB), F(sC), F(sT)
            RY, Rbon, Rcf, Rt1 = RF[sY], RF[sB], RF[sC], RF[sT]
            lw2, Rlw2 = self.LW[m % 2], self.R("lw2_%d" % (m % 2))
            lw2, Rlw2 = self.LW[m % 2], self.R("lw2_%d" % (m % 2))
            self.DMA("pool", lw2, self.d_l2[j * HP + m], [], [Rlw2], "lw2_%d" % (m % 2))
            self.DMA("sp", SG3, seg3[m // hq][:, :, (m % hq) * 128:(m % hq + 1) * 128], [self.R("seg_out%d" % (m // hq))], [RSG], "c_sg")
            self.DMA("sp", Y_, self.d_ycb[m], [self.R("ycb0_%d" % m)], [RY], "c_y%d" % (m % 2))
            self.DMA("sp", cfst[:, 0:PT], self.d_ycb[HP + m, :, 0:PT], [self.R("ycb1_%d" % m)], [Rcf], "c_cf%d" % (m % 2))
            self.DMA("sp", bon, self.d_ycb[2 * HP + m], [self.R("ycb2_%d" % m)], [Rbon], "c_bon%d" % (m % 2))
            for h in range(2):
                ps = slice(64 * h, 64 * h + 64)
                self.CP("act" if h else "dve", T84[ps, :, h, :], SG3[ps, :, 64:128], [RSG], [RT8])
            for r in range(NCORES):
                b = nb()
                cur, nxt = r % 2, (r + 1) % 2
                self.MM(self.banks[b][:, 0:64], Tbd8[:, r * 128:(r + 1) * 128], Pb[cur], True, True, [RT8, RPb[cur]], [self.R_bank[b]])
                self.TT("dve", Pb[nxt], self.banks[b][:, 0:64], SG3[:, r, 0:64], ALU.add, [self.R_bank[b], RSG], [RPb[nxt]])
                self.TT("dve", Pl[:, (r + 1) * 64:(r + 2) * 64], self.banks[b][:, 0:64], SG3[:, r, 0:64], ALU.add, [self.R_bank[b], RSG], [RPl])
                yield
            for k_, base in ((0, 0), (1, 1)):
                dst = Ssel[:, k_ * 64:(k_ + 1) * 64]
                for r in range(NCORES):
                    src = Pl[:, (r + base) * 64:(r + base + 1) * 64]
                    sc = self.SEL[:, 8 + r:9 + r]
                    if r == 0:
                        self.TS("dve", dst, src, sc, None, ALU.mult, None, [RPl, Rc], [RSsel])
                    else:
                        self.STT("dve", dst, src, sc, dst, ALU.mult, ALU.add, [RPl, Rc], [RSsel])
            self.CP("act", owk[:, 0:64], Ssel[:, 64:128], [RSsel], [Rowk])
            oo = ((j * HP + m) * (1 + NS)) * 64
            self.DMA("sp", self.d_owkv[:, oo:oo + 64], owk[:, 0:64], [Rowk], [self.R("d_owkv")], "stowk")
            self.CP("act", Sst, Ssel[:, 0:64], [RSsel], [RSst])
            for h in range(2):
                ps = slice(64 * h, 64 * h + 64)
                self.CP("dve", Sbd[ps, 64 * h:64 * h + 64], Sst[ps, :], [RSst], [RSbd])
            self.CP("act", Ceffb, cfst[:, 0:PT], [Rcf], [RCb])
            for (o, n) in [(o, min(512, PT - o)) for o in range(0, PT, 512)]:
                b = nb()
                self.MM(self.banks[b][:, 0:n], Sbd, Ceffb[:, o:o + n], True, True, [RSbd, RCb], [self.R_bank[b]])
                self.TT("dve", Y_[:, o:o + n], Y_[:, o:o + n], self.banks[b][:, 0:n], ALU.add, [self.R_bank[b]], [RY])
            yield

        def stage2(m):
            sY, sB, sC, sT = slotsets[m % 2]
            Y_, bon, cfst, t1 = F(sY), F(sB), F(sC), F(sT)
            RY, Rbon, Rcf, Rt1 = RF[sY], RF[sB], RF[sC], RF[sT]
            lw2, Rlw2 = self.LW[m % 2], self.R("lw2_%d" % (m % 2))
            self.CP("act", yb[:, :], Y_, [RY], [Ryb])
            self.ACT(ysq[:, :], Y_, AF.Square, [RY], [Rysq])
            mean, rstd = self.mean, self.rstd
            for (o, n) in self.ptiles:
                b1, b2 = nb(), nb()
                self.MM(self.banks[b1][:, 0:n], bones, yb[:, o:o + n], True, True, [Ryb, Rc], [self.R_bank[b1]])
                self.MM(self.banks[b2][:, 0:n], bones, ysq[:, o:o + n], True, True, [Rysq, Rc], [self.R_bank[b2]])
                mn, rs = mean[:, o:o + n], rstd[:, o:o + n]
                self.TS("dve", mn, self.banks[b1][:, 0:n], 1.0 / 64, None, ALU.mult, None, [self.R_bank[b1]], [self.R_stat])
                self.TT("dve", rs, mn, mn, ALU.mult, [], [self.R_stat])
                self.STT("dve", rs, self.banks[b2][:, 0:n], 1.0 / 64, rs, ALU.mult, ALU.subtract, [self.R_bank[b2]], [self.R_stat])
                self.ACT(rs, rs, AF.Sqrt, [Rc], [self.R_stat], bias=self.epsc(GN_EPS))
                self.add("dve", lambda e, rs=rs: e.reciprocal(out=rs, in_=rs), [], [self.R_stat])
                yield
            self.TT("dve", t1, Y_, mean, ALU.subtract, [RY, self.R_stat], [Rt1])
            self.TT("dve", t1, t1, rstd, ALU.mult, [self.R_stat], [Rt1])
            yield
            self.TS("dve", t1, t1, self.par(("rlng", j), m), self.par(("rlnb", j), m), ALU.mult, ALU.add, [Rc], [Rt1])
            self.TT("dve", t1, t1, bon, ALU.add, [Rbon], [Rt1])
            yield
            for (o, n) in self.ptiles:
                b = nb()
                self.MM(self.banks[b][:, 0:n], lw2[:, 384:512], self.SGL[:, o:o + n], True, False, [Rlw2, self.R("sgl")], [self.R_bank[b]])
                self.MM(self.banks[b][:, 0:n], lw2[:, 512:640], self.SGL[:, T + o:T + o + n], False, True, [Rlw2, self.R("sgl")], [self.R_bank[b]])
                self.TT("dve", self.ht3[:, m, o:o + n], t1[:, o:o + n], self.banks[b][:, 0:n], ALU.mult, [self.R_bank[b], Rt1],
                        [self.R_ht[m], self.R_hthalo])

            yield

        def step(gen):
            if gen is None:
                return None
            try:
                next(gen)
                return gen
            except StopIteration:
                return None
        g1 = stage1(0)
        while g1 is not None:
            g1 = step(g1)
        for m in range(HP):
            g2 = stage2(m)
            g1 = stage1(m + 1) if m + 1 < HP else None
            while g1 is not None or g2 is not None:
                g1 = step(g1)
                g2 = step(g2)

    def rwkv_pre(self, j):
        self.halo_recv()
        c = self.cfg
        KC, PT, NS = c.KC, c.PT, c.NS
        stg = self.SMALL[:, 0:KC * NS]
        Rst = self.R("shst")
        self.DMA("sp", stg, self.d_stshift[:, j * KC * NS:(j + 1) * KC * NS], [], [Rst], "shld")
        b0 = 30 + PT
        for kc in range(KC):
            dst = self.ht3[:, kc, b0:b0 + NS * 65].rearrange("p (s c) -> p s c", c=65)[:, :, 0:1]
            self.CP("act", dst, stg[:, kc * NS:(kc + 1) * NS].rearrange("p (s o) -> p s o", o=1), [Rst], [self.R_ht[kc]])
        n = KC * (1 + NS)
        self.DMA("sp", self.d_oshift[:, j * n:(j + 1) * n], self.OSH[:, 0:n], [self.R("osh")], [self.R("d_oshift")], "stosh")

    def hlast(self, kc, src, g):
        c = self.cfg
        PT, NS, T = c.PT, c.NS, c.T
        o = kc * (1 + NS)
        self.STT("dve", self.OSH[:, o:o + 1], src[:, PT - 1:PT], g, self.rstd[:, PT - 1:PT], ALU.mult, ALU.mult,
                 [self.R_big[kc], self.R_stat], [self.R("osh")])
        s3 = src[:, PT:T].rearrange("p (s c) -> p s c", c=64)[:, :, 63:64]
        r3 = self.rstd[:, PT:T].rearrange("p (s c) -> p s c", c=64)[:, :, 63:64]
        self.STT("dve", self.OSH[:, o + 1:o + 1 + NS].rearrange("p (s o) -> p s o", o=1), s3, g, r3, ALU.mult, ALU.mult,
                 [self.R_big[kc], self.R_stat], [self.R("osh")])

    def build(self, sublayers):
        c = self.cfg
        KC, T = c.KC, c.T
        self.DMA("sp", self.PAR[:, :], self.d_par[:, :], [], [self.R_const], "cst")
        self.DMA("sp", self.CB[:, :], self.d_cb[:, :], [], [self.R_const], "cst")
        self.DMA("sp", self.SEL[:, :], self.d_sel[:, :], [], [self.R_const], "cst")
        for i_, v_ in enumerate((RMS_EPS, LN_EPS, GN_EPS, 1e-24)):
            self.add("dve", lambda e, i_=i_, v_=v_: e.memset(self.EPS[:, i_:i_ + 1], v_), [], [self.R_const])
        for kc in range(KC):
            self.DMA("sp", self.big3[:, kc, :], self.d_x[:, kc * T:(kc + 1) * T], [], [self.R_big[kc]], "xin")
        if sublayers is None:
            sublayers = []
            for i in range(c.depth):
                sublayers.append(("mix", i))
                sublayers.append(("ffn", i))
        for (kind, i) in sublayers:
            self.barrier()
            if kind == "mix":
                if i % 2 == 1:
                    self.rwkv_alloc()
                    self.norm_in(("ng", i, 0), hlast=self.hlast)
                    self.rwkv_pre(i // 2)
                    self.barrier()
                    self.rwkv(i // 2)
                else:
                    self.norm_in(("ng", i, 0))
                    self.conformer(i // 2)
                self.barrier()
                self.resid(("ng", i, 1))
            else:
                self.norm_in(("ng", i, 2))
                self.ffn(i)
                self.barrier()
                self.resid(("ng", i, 3))
        for kc in range(KC):
            self.DMA("sp", self.d_y[:, kc * T:(kc + 1) * T], self.big3[:, kc, :], [self.R_big[kc]], [self.R("d_y")], "yout")
        self.p.emit()
        self.stack.close()


def fm_tokens(x):
    x = np.asarray(x, np.float32)
    t, d = x.shape
    return np.ascontiguousarray(x.reshape(t, d // 128, 128).transpose(2, 1, 0))


def prep_shared(cfg, inp):
    c = cfg
    sh = {}
    sh["par"] = pack_params(cfg, inp)
    sh["cb"] = const_bf16(cfg)
    sh["w_pw1"] = np.concatenate([wl(inp["conv_pw1_w"][j]) for j in range(c.NCL)], 0)
    sh["w_pw2"] = np.concatenate([wl(inp["conv_pw2_w"][j]) for j in range(c.NCL)], 0)
    sh["w_in"] = np.concatenate([wl(inp["ffn_w_in"][i]) for i in range(c.depth)], 0)
    wo = []
    for i in range(c.depth):
        w = np.asarray(inp["ffn_w_out"][i], np.float32)
        wo.append(np.ascontiguousarray(w.reshape(c.FC // c.G, c.G, 128, c.D).transpose(0, 2, 1, 3)).reshape(c.FC // c.G, 128, c.G * c.D))
    sh["w_out"] = np.concatenate(wo, 0)
    return sh


def prep_core(cfg, inp, core):
    c = cfg
    m = {}
    xp = np.asarray(inp["x_prompt"], np.float32)[0, core * c.PT:(core + 1) * c.PT]
    xs = np.asarray(inp["x_sample"], np.float32)[core * c.NS:(core + 1) * c.NS].reshape(c.NS * c.SL, c.D)
    m["xT"] = fm_tokens(np.concatenate([xp, xs], 0)).reshape(128, -1)
    sel = np.zeros((128, 17), np.float32)
    if core > 0:
        sel[:, core - 1] = 1.0
        sel[:, 16] = 1.0
    sel[:, 8 + core] = 1.0
    m["sel"] = sel
    sq = slice(core * c.NS, (core + 1) * c.NS)
    sc = np.asarray(inp["state_conv_mix"], np.float32)[:, sq]
    m["st_conv"] = np.ascontiguousarray(sc.reshape(c.NCL, c.NS, 30, c.KC, 128).transpose(4, 0, 3, 1, 2)).reshape(128, -1)
    sf = np.asarray(inp["state_ffn_conv"], np.float32)[:, sq]
    m["st_ffn"] = np.ascontiguousarray(sf.reshape(c.depth, c.NS, 2, c.FC, 128).transpose(4, 0, 3, 1, 2)).reshape(128, -1)
    return m


def assemble(cfg, results):
    c = cfg
    KC, FC, T, PT, NS, SL, D = c.KC, c.FC, c.T, c.PT, c.NS, c.SL, c.D
    yp, ys = [], []
    for r in results:
        y = np.asarray(r["yT"]).reshape(128, KC, T).transpose(2, 1, 0).reshape(T, D)
        yp.append(y[:PT])
        ys.append(y[PT:].reshape(NS, SL, D))
    y_prompt = np.concatenate(yp, 0)[None]
    y_sample = np.concatenate(ys, 0)

    def st(name, nl, nch, w):
        arrs = [np.asarray(r[name]).reshape(128, nl, nch, 1 + NS, w) for r in results]
        p = arrs[-1][:, :, :, 0, :].transpose(1, 3, 2, 0).reshape(nl, 1, w, nch * 128)
        s = np.concatenate([a[:, :, :, 1:, :].transpose(1, 3, 4, 2, 0).reshape(nl, NS, w, nch * 128) for a in arrs], 1)
        return np.ascontiguousarray(p), np.ascontiguousarray(s)
    conv_p, conv_s = st("o_conv", c.NCL, KC, 30)
    ffn_p, ffn_s = st("o_ffn", c.depth, FC, 2)
    outs = [y_prompt, y_sample, conv_p, conv_s, None, None, None, None, ffn_p, ffn_s]
    if c.NRL > 0 and "o_shift" in results[0]:
        outs[4:8] = assemble_rwkv(cfg, results)
    return tuple(outs)


def prep_shared_rwkv(cfg, inp):
    c = cfg
    if c.NRL == 0:
        return {}
    sh = {}
    rk = []
    for j in range(c.NRL):
        for nm in ("rwkv_w_r", "rwkv_w_k", "rwkv_w_v", "rwkv_w_o"):
            rk.append(wl(inp[nm][j]))
    sh["w_rkvo"] = np.concatenate(rk, 0)
    W = c.KC * 128
    l1 = np.zeros((c.NRL * 5, 128, W), np.float32)
    l2 = np.zeros((c.NRL * c.HP, 128, 5 * 128), np.float32)
    for j in range(c.NRL):
        l1[j * 5 + 0, :, :c.KC * 96] = wl(inp["rwkv_w1"][j], 96)[0]
        l1[j * 5 + 1, :, :c.KC * 96] = wl(inp["rwkv_a1"][j], 96)[0]
        g1 = wl(inp["rwkv_g1"][j], 128)
        l1[j * 5 + 2] = g1[0]
        l1[j * 5 + 3] = g1[1]
        if j > 0:
            l1[j * 5 + 4, :, :c.KC * 64] = wl(inp["rwkv_v1"][j - 1], 64)[0]
        w2 = np.asarray(inp["rwkv_w2"][j], np.float32)
        a2 = np.asarray(inp["rwkv_a2"][j], np.float32)
        g2 = np.asarray(inp["rwkv_g2"][j], np.float32)
        for m in range(c.HP):
            cs = slice(m * 128, (m + 1) * 128)
            l2[j * c.HP + m, 0:96, 0:128] = w2[:, cs]
            l2[j * c.HP + m, 0:96, 128:256] = a2[:, cs]
            if j > 0:
                l2[j * c.HP + m, 0:64, 256:384] = np.asarray(inp["rwkv_v2"][j - 1], np.float32)[:, cs]
            l2[j * c.HP + m, :, 384:512] = g2[0:128, cs]
            l2[j * c.HP + m, :, 512:640] = g2[128:256, cs]
    sh["w_l1"] = l1
    sh["w_l2"] = l2
    return sh


def prep_core_rwkv(cfg, inp, core):
    c = cfg
    if c.NRL == 0:
        return {}
    m = {}
    sq = slice(core * c.NS, (core + 1) * c.NS)
    ss = np.asarray(inp["state_rwkv_shift"], np.float32)[:, sq]
    m["st_shift"] = np.ascontiguousarray(ss.reshape(c.NRL, c.NS, c.KC, 128).transpose(3, 0, 2, 1)).reshape(128, -1)
    sw = np.asarray(inp["state_rwkv_wkv"], np.float32)[:, sq]
    sw = sw.reshape(c.NRL, c.NS, c.HP, 2, 64, 64)
    m["st_wkv"] = np.ascontiguousarray(sw.transpose(3, 5, 0, 2, 1, 4)).reshape(128, -1)
    return m


def assemble_rwkv(cfg, results):
    c = cfg
    KC, HP, NS, D = c.KC, c.HP, c.NS, c.D
    sh = [np.asarray(r["o_shift"]).reshape(128, c.NRL, KC, 1 + NS) for r in results]
    shift_p = sh[-1][:, :, :, 0].transpose(1, 2, 0).reshape(c.NRL, 1, D)
    shift_s = np.concatenate([a[:, :, :, 1:].transpose(1, 3, 2, 0).reshape(c.NRL, NS, D) for a in sh], 1)
    wk = [np.asarray(r["o_wkv"]).reshape(2, 64, c.NRL, HP, 1 + NS, 64) for r in results]
    tr = lambda a: a.transpose(2, 4, 3, 0, 5, 1).reshape(c.NRL, a.shape[4], HP * 2, 64, 64)
    wkv_p = tr(wk[-1][:, :, :, :, 0:1, :])
    wkv_s = np.concatenate([tr(a[:, :, :, :, 1:, :]) for a in wk], 1)
    return [np.ascontiguousarray(x) for x in (shift_p, shift_s, wkv_p, wkv_s)]


_CACHE = {}


def kernel(**inputs):
    cfg = Cfg()
    if "b" not in _CACHE:
        _CACHE["b"] = B(cfg)
    b = _CACHE["b"]
    inp = {k: np.asarray(v) for k, v in inputs.items()}
    sh = prep_shared(cfg, inp)
    sh.update(prep_shared_rwkv(cfg, inp))
    maps = []
    for c in range(NCORES):
        m = dict(sh)
        m.update(prep_core(cfg, inp, c))
        m.update(prep_core_rwkv(cfg, inp, c))
        maps.append(m)
    res = run_bass_kernel_spmd(b.nc, maps, core_ids=list(range(NCORES)))
    outs = assemble(cfg, [r for r in res.results])
    return tuple(np.ascontiguousarray(o, dtype=np.float32) for o in outs)
```

```python
import contextlib
import numpy as np
import ml_dtypes
import concourse.bass as bass
import concourse.mybir as mybir
from concourse.bass_utils import run_bass_kernel_spmd

F32 = mybir.dt.float32
BF16 = mybir.dt.bfloat16
AF = mybir.ActivationFunctionType
ALU = mybir.AluOpType
AX = mybir.AxisListType

NCORES = 8
DBG = 0
RMS_EPS = 1e-6
LN_EPS = 1e-5
GN_EPS = 64e-5


class Cfg:
    def __init__(self, D=2048, DFF=5632, PT=1024, NS=4, SL=64, depth=4, G=2):
        self.D, self.DFF, self.PT, self.NS, self.SL, self.depth, self.G = D, DFF, PT, NS, SL, depth, G
        self.KC = D // 128
        self.FC = DFF // 128
        self.HP = D // 128
        self.T = PT + NS * SL
        self.NCL = (depth + 1) // 2
        self.NRL = depth // 2
        self.NVL = max(self.NRL - 1, 0)
        self.HALO = 30
        self.HC = self.HALO + PT + NS * (1 + SL)
        self.UC = self.HALO + PT + NS * (self.HALO + SL)
        self.GC = 2 + PT + NS * (2 + SL)
        self.NCH = PT // 64
        self.NU = self.NCH + NS
        assert PT % 128 == 0 and SL == 64 and self.FC % G == 0


class Res:
    __slots__ = ("name", "w", "rs")

    def __init__(self, name):
        self.name = name
        self.w = None
        self.rs = []


EPOCH = 24000
ENGS = ("pe", "act", "dve", "pool", "sp")


class Prog:
    def __init__(self, nc, stack):
        self.nc = nc
        self.stack = stack
        self.ops = {e: [] for e in ENGS}
        self.cnt = {e: 0 for e in ENGS}
        self.epoch = {e: 0 for e in ENGS}
        self.seen = {e: {} for e in ENGS}
        self.sems = {}
        self.dcnt = {}
        self.nops = 0
        self.pending = {e: [] for e in ENGS}

    def sem(self, key):
        s = self.sems.get(key)
        if s is None:
            s = self.stack.enter_context(self.nc.semaphore("s%d" % len(self.sems)))
            self.sems[key] = s
        return s

    def add(self, eng, fn, reads=(), writes=(), dma=None):
        deps = list(self.pending[eng])
        self.pending[eng] = []
        for r in reads:
            if r.w is not None:
                deps.append(r.w)
        for w in writes:
            if w.w is not None:
                deps.append(w.w)
            deps.extend(w.rs)
        if dma is None:
            if self.cnt[eng] >= EPOCH:
                self.epoch[eng] += 1
                self.cnt[eng] = 0
            self.cnt[eng] += 1
            key = ("c", eng, self.epoch[eng])
            tok = (key, self.cnt[eng])
            inc = 1
        else:
            key = ("d", dma)
            self.dcnt[key] = self.dcnt.get(key, 0) + 16
            tok = (key, self.dcnt[key])
            inc = 16
        need = {}
        for (k, v) in deps:
            if eng == "pe" and k[0] == "c" and k[1] == "pe":
                continue
            if self.seen[eng].get(k, 0) >= v:
                continue
            if need.get(k, 0) < v:
                need[k] = v
        for k, v in need.items():
            self.seen[eng][k] = v
        self.ops[eng].append((list(need.items()), fn, key, inc))
        for r in reads:
            r.rs.append(tok)
        for w in writes:
            w.w = tok
            w.rs = []
        self.nops += 1
        return tok

    def emit(self, final_keys=()):
        nc = self.nc
        for e in ENGS:
            for (need, fn, key, inc) in self.ops[e]:
                self.sem(key)
                for k, _ in need:
                    self.sem(k)
        prog = self

        def run(e, engobj):
            for (need, fn, key, inc) in prog.ops[e]:
                for k, v in need:
                    engobj.wait_ge(prog.sems[k], v)
                ins = fn(engobj)
                ins.then_inc(prog.sems[key], inc)
            if e == "sp":
                for k, v in prog.dcnt.items():
                    engobj.wait_ge(prog.sems[k], v)
                for ee in ENGS:
                    if ee == "sp":
                        continue
                    for ep in range(prog.epoch[ee] + 1):
                        kk = ("c", ee, ep)
                        if kk in prog.sems:
                            vv = prog.cnt[ee] if ep == prog.epoch[ee] else EPOCH
                            if vv > 0:
                                engobj.wait_ge(prog.sems[kk], vv)

        with nc.Block() as block:
            @block.tensor
            def _(t):
                run("pe", t)

            @block.scalar
            def _(s):
                run("act", s)

            @block.vector
            def _(v):
                run("dve", v)

            @block.gpsimd
            def _(g):
                run("pool", g)

            @block.sync
            def _(s):
                run("sp", s)


def param_layout(cfg):
    off = {}
    n = 0

    def put(name, cols):
        nonlocal n
        off[name] = (n, cols)
        n += cols
    KC, FC = cfg.KC, cfg.FC
    for i in range(cfg.depth):
        for q in range(4):
            put(("ng", i, q), KC)
        put(("fdw", i), FC * 3)
        put(("fdb", i), FC)
    for j in range(cfg.NCL):
        put(("pw1b", j), 2 * KC)
        put(("dww", j), KC * 31)
        for nm in ("dwb", "clng", "clnb", "pw2b"):
            put((nm, j), KC)
    for j in range(cfg.NRL):
        for q in range(6):
            put(("mix", j, q), KC)
        for nm in ("w0", "a0", "kk", "ka", "rk", "rlng", "rlnb"):
            put((nm, j), KC)
    for j in range(cfg.NVL):
        put(("v0", j), KC)
    return off, n


def fm_vec(v):
    return np.ascontiguousarray(np.asarray(v, np.float32).reshape(-1, 128).T)


def pack_params(cfg, inp):
    off, n = param_layout(cfg)
    P = np.zeros((128, n), np.float32)

    def st(key, arr):
        o, c = off[key]
        assert arr.shape == (128, c), (key, arr.shape, c)
        P[:, o:o + c] = arr
    for i in range(cfg.depth):
        for q in range(4):
            st(("ng", i, q), fm_vec(inp["norm_g"][i, q]))
        w = np.asarray(inp["ffn_dw_w"][i], np.float32)
        st(("fdw", i), np.ascontiguousarray(w.reshape(3, cfg.FC, 128).transpose(2, 1, 0)).reshape(128, cfg.FC * 3))
        st(("fdb", i), fm_vec(inp["ffn_dw_b"][i]))
    for j in range(cfg.NCL):
        st(("pw1b", j), fm_vec(inp["conv_pw1_b"][j]))
        w = np.asarray(inp["conv_dw_w"][j], np.float32)
        st(("dww", j), np.ascontiguousarray(w.reshape(31, cfg.KC, 128).transpose(2, 1, 0)).reshape(128, cfg.KC * 31))
        st(("dwb", j), fm_vec(inp["conv_dw_b"][j]))
        st(("clng", j), fm_vec(inp["conv_ln_g"][j]))
        st(("clnb", j), fm_vec(inp["conv_ln_b"][j]))
        st(("pw2b", j), fm_vec(inp["conv_pw2_b"][j]))
    for j in range(cfg.NRL):
        for q in range(6):
            st(("mix", j, q), fm_vec(inp["rwkv_mix"][j, q]))
        st(("w0", j), fm_vec(inp["rwkv_w0"][j]))
        st(("a0", j), fm_vec(inp["rwkv_a0"][j]))
        st(("kk", j), fm_vec(inp["rwkv_k_k"][j]))
        st(("ka", j), fm_vec(inp["rwkv_k_a"][j]))
        st(("rk", j), fm_vec(np.asarray(inp["rwkv_r_k"][j]).reshape(-1)))
        st(("rlng", j), fm_vec(inp["rwkv_ln_g"][j]))
        st(("rlnb", j), fm_vec(inp["rwkv_ln_b"][j]))
    for j in range(cfg.NVL):
        st(("v0", j), fm_vec(inp["rwkv_v0"][j]))
    return P


def wl(w, mc=128):
    w = np.asarray(w, np.float32)
    K, M = w.shape
    kc = K // 128
    nm = M // mc
    return np.ascontiguousarray(w.reshape(kc, 128, nm, mc).transpose(2, 1, 0, 3)).reshape(nm, 128, kc * mc)


def const_bf16(cfg):
    C = 64
    s = np.arange(C)
    mstrict = (s[:, None] < s[None, :]).astype(np.float32)
    mincl = (s[:, None] <= s[None, :]).astype(np.float32)
    eye2 = np.eye(2, dtype=np.float32)
    Mst = np.kron(eye2, mstrict)
    MstT = np.kron(eye2, mstrict.T)
    maskG = np.concatenate([np.concatenate([mincl, mincl], 0), np.ones((128, 64), np.float32)], 1)
    ident = np.eye(128, dtype=np.float32)
    ones = np.ones((128, 128), np.float32)
    bones = np.kron(eye2, np.ones((64, 64), np.float32))
    I2 = np.concatenate([np.eye(64, dtype=np.float32)] * 2, 0)
    parts = [ident, ones, bones, I2, np.tile(Mst, (1, 4)), np.tile(MstT, (1, 4)), np.tile(maskG, (1, 4))]
    return np.concatenate(parts, 1).astype(ml_dtypes.bfloat16)


CB_ID, CB_ONES, CB_BONES, CB_I2, CB_MST, CB_MSTT, CB_MG, CB_N = 0, 128, 256, 384, 448, 960, 1472, 1984


class B:
    def __init__(self, cfg, sublayers=None):
        self.cfg = cfg
        c = cfg
        self.nc = nc = bass.Bass("TRN2", target_bir_lowering=False)
        self.stack = contextlib.ExitStack()
        self.p = Prog(nc, self.stack)
        self.poff, self.npar = param_layout(cfg)
        KC, FC, T, D = c.KC, c.FC, c.T, c.D
        di = lambda name, shape, dt=F32: nc.dram_tensor(name, list(shape), dt, kind="ExternalInput").ap()
        do = lambda name, shape, dt=F32: nc.dram_tensor(name, list(shape), dt, kind="ExternalOutput").ap()
        self.d_x = di("xT", [128, KC * T])
        self.d_par = di("par", [128, self.npar])
        self.d_cb = di("cb", [128, CB_N], BF16)
        self.d_sel = di("sel", [128, 17])
        self.d_pw1 = di("w_pw1", [c.NCL * 2 * KC, 128, KC * 128])
        self.d_pw2 = di("w_pw2", [c.NCL * KC, 128, KC * 128])
        self.d_win = di("w_in", [c.depth * 2 * FC, 128, KC * 128])
        self.d_wout = di("w_out", [c.depth * (FC // c.G), 128, c.G * D])
        self.d_stconv = di("st_conv", [128, c.NCL * KC * c.NS * 30])
        self.d_stffn = di("st_ffn", [128, c.depth * FC * c.NS * 2])
        self.d_y = do("yT", [128, KC * T])
        self.d_oconv = do("o_conv", [128, c.NCL * KC * (1 + c.NS) * 30])
        self.d_offn = do("o_ffn", [128, c.depth * FC * (1 + c.NS) * 2])
        self.rwkv_decl()
        self.d_xh = nc.dram_tensor("x_home", [128, KC * T], F32).ap()
        self.t_hxin = nc.dram_tensor("hx_in", [128, KC * 30], BF16)
        self.t_hxmid = nc.dram_tensor("hx_mid", [4 * 128, KC * 30], BF16)
        self.t_hxout = nc.dram_tensor("hx_out", [NCORES * 128, KC * 30], BF16)
        sb = lambda name, cols, dt: self.stack.enter_context(nc.sbuf_tensor(name, [128, cols], dt))
        self.BIG = sb("big", KC * T, F32)
        self.HT = sb("ht", KC * c.HC, BF16)
        self.PAR = sb("par_sb", self.npar, F32)
        self.CB = sb("cb_sb", CB_N, BF16)
        self.SEL = sb("sel_sb", 17, F32)
        self.EPS = sb("eps_sb", 4, F32)
        self.STAT = sb("stat", 2 * T, F32)
        self.WS = [sb("ws%d" % i, KC * 128, BF16) for i in range(3)]
        self.AR2 = sb("ar2", max(2 * c.G * D + c.G * T, 2 * c.UC + 31 * 128 + 2048, 10752), BF16)
        self.STG = sb("stg", 3 * T + 96, F32)
        self.OST = sb("ost", 2 * (1 + c.NS) * 32, F32)
        self.DW3 = sb("dw3", 768, BF16)
        self.SQ = [sb("sq%d" % i, T, BF16) for i in range(4)]
        self.banks = [self.stack.enter_context(nc.psum_tensor("bank%d" % i, [128, 512], F32)) for i in range(8)]
        self.R_big = [Res("big%d" % k) for k in range(KC)]
        self.R_ht = [Res("ht%d" % k) for k in range(KC)]
        self.R_hthalo = Res("hthalo")
        self.R_const = Res("const")
        self.R_stat = Res("stat")
        self.R_ws = [Res("ws%d" % i) for i in range(3)]
        self.R_bank = [Res("bank%d" % i) for i in range(8)]
        self.R_sq = [Res("sq%d" % i) for i in range(4)]
        self.R_xh = [Res("xh%d" % k) for k in range(KC)]
        self.R_misc = {}
        self.wsi = 0
        self.bki = 0
        self.sqi = 0
        self.pend_barrier = None
        self.big3 = self.BIG[:, :].rearrange("p (k t) -> p k t", t=T)
        self.ht3 = self.HT[:, :].rearrange("p (k t) -> p k t", t=c.HC)
        self.rstd = self.STAT[:, 0:T]
        self.mean = self.STAT[:, T:2 * T]
        self.ptiles = [(o, min(512, T - o)) for o in range(0, T, 512)]
        self.mt = [("p", o, min(512, c.PT - o)) for o in range(0, c.PT, 512)] + [("s",)]
        self.build(sublayers)

    def R(self, name):
        r = self.R_misc.get(name)
        if r is None:
            r = self.R_misc[name] = Res(name)
        return r

    def par(self, key, col=0, n=1):
        o, c = self.poff[key]
        return self.PAR[:, o + col:o + col + n]

    def cbv(self, off, n):
        return self.CB[:, off:off + n]

    def bank(self, pool=(0, 1, 2, 3, 4, 5, 6, 7)):
        i = pool[self.bki % len(pool)]
        self.bki += 1
        return i

    def add(self, eng, fn, R=(), W=()):
        return self.p.add(eng, fn, R, W)

    def ACT(self, out, in_, func, R, W, bias=0.0, scale=1.0):
        self.p.add("act", lambda e: e.activation(out=out, in_=in_, func=func, bias=bias, scale=scale), R, W)

    def TS(self, eng, out, in0, s1, s2, op0, op1, R, W):
        if s2 is None:
            self.p.add(eng, lambda e: e.tensor_scalar(out=out, in0=in0, scalar1=s1, scalar2=None, op0=op0), R, W)
        else:
            self.p.add(eng, lambda e: e.tensor_scalar(out=out, in0=in0, scalar1=s1, scalar2=s2, op0=op0, op1=op1), R, W)

    def TT(self, eng, out, in0, in1, op, R, W):
        self.p.add(eng, lambda e: e.tensor_tensor(out=out, in0=in0, in1=in1, op=op), R, W)

    def STT(self, eng, out, in0, scalar, in1, op0, op1, R, W):
        self.p.add(eng, lambda e: e.scalar_tensor_tensor(out=out, in0=in0, scalar=scalar, in1=in1, op0=op0, op1=op1), R, W)

    def CP(self, eng, out, in_, R, W):
        if eng == "act":
            self.ACT(out, in_, AF.Identity, R, W)
        else:
            self.p.add(eng, lambda e: e.tensor_copy(out=out, in_=in_), R, W)

    def MM(self, out, lhsT, rhs, start, stop, R, W):
        self.p.add("pe", lambda e: e.matmul(out, lhsT, rhs, start=start, stop=stop), R, W)

    def DMA(self, q, out, in_, R, W, key):
        self.p.add(q, lambda e: e.dma_start(out=out, in_=in_), R, W, dma=key)

    def wload(self, dram, idx, cols=None):
        i = self.wsi % 3
        self.wsi += 1
        cols = cols or dram.shape[-1]
        self.DMA("pool", self.WS[i][:, 0:cols], dram[idx], [], [self.R_ws[i]], "ws%d" % i)
        return self.WS[i], self.R_ws[i]

    def hrhs(self, kc, t):
        c = self.cfg
        if t[0] == "p":
            return self.ht3[:, kc, 30 + t[1]:30 + t[1] + t[2]]
        if t[0] == "s":
            b0 = 30 + c.PT
            return self.ht3[:, kc, b0:b0 + c.NS * 65].rearrange("p (s c) -> p s c", c=65)[:, :, 1:65]
        if t[0] == "h":
            return self.ht3[:, kc, 0:30]
        if t[0] == "h2":
            return self.ht3[:, kc, 28:30]
        raise ValueError(t)

    def tn(self, t):
        c = self.cfg
        return {"p": t[2] if t[0] == "p" else 0, "s": c.NS * 64, "h": 30, "h2": 2}[t[0]]

    def bk(self, b, t):
        n = self.tn(t)
        ap = self.banks[b][:, 0:n]
        if t[0] == "s":
            return ap.rearrange("p (s c) -> p s c", c=64)
        return ap

    def pl(self, row, t, three=True):
        c = self.cfg
        if t[0] == "p":
            return row[:, t[1]:t[1] + t[2]]
        ap = row[:, c.PT:c.PT + c.NS * 64]
        return ap.rearrange("p (s c) -> p s c", c=64) if three else ap

    def colstats(self, want_sum, eps):
        c = self.cfg
        KC, T = c.KC, c.T
        ones = self.cbv(CB_ONES, 128)
        sqb = (5, 6, 7)
        smb = (2, 3, 4)
        for kc in range(KC):
            qi = self.sqi % 2
            self.sqi += 1
            self.ACT(self.SQ[qi][:, :], self.big3[:, kc, :], AF.Square, [self.R_big[kc]], [self.R_sq[qi]])
            for ti, (o, n) in enumerate(self.ptiles):
                self.MM(self.banks[sqb[ti]][:, 0:n], ones, self.SQ[qi][:, o:o + n], kc == 0, kc == KC - 1,
                        [self.R_sq[qi], self.R_const], [self.R_bank[sqb[ti]]])
            if want_sum:
                self.CP("act", self.SQ[2 + qi][:, :], self.big3[:, kc, :], [self.R_big[kc]], [self.R_sq[2 + qi]])
                for ti, (o, n) in enumerate(self.ptiles):
                    self.MM(self.banks[smb[ti]][:, 0:n], ones, self.SQ[2 + qi][:, o:o + n], kc == 0, kc == KC - 1,
                            [self.R_sq[2 + qi], self.R_const], [self.R_bank[smb[ti]]])
        inv = 1.0 / c.D
        for ti, (o, n) in enumerate(self.ptiles):
            rs = self.rstd[:, o:o + n]
            if not want_sum:
                self.ACT(rs, self.banks[sqb[ti]][:, 0:n], AF.Sqrt, [self.R_bank[sqb[ti]], self.R_const], [self.R_stat],
                         bias=self.epsc(eps), scale=inv)
            else:
                mn = self.mean[:, o:o + n]
                self.TS("dve", mn, self.banks[smb[ti]][:, 0:n], inv, None, ALU.mult, None,
                        [self.R_bank[smb[ti]]], [self.R_stat])
                self.TT("dve", rs, mn, mn, ALU.mult, [self.R_stat], [self.R_stat])
                self.STT("dve", rs, self.banks[sqb[ti]][:, 0:n], inv, rs, ALU.mult, ALU.subtract,
                         [self.R_bank[sqb[ti]], self.R_stat], [self.R_stat])
                self.ACT(rs, rs, AF.Sqrt, [self.R_stat, self.R_const], [self.R_stat], bias=self.epsc(eps))
            self.add("dve", lambda e, rs=rs: e.reciprocal(out=rs, in_=rs), [self.R_stat], [self.R_stat])

    def epsc(self, eps):
        i = {RMS_EPS: 0, LN_EPS: 1, GN_EPS: 2}[eps]
        return self.EPS[:, i:i + 1]

    def norm_in(self, gkey, hlast=None, pre_exchange=None):
        c = self.cfg
        KC, T, PT, NS = c.KC, c.T, c.PT, c.NS
        self.colstats(False, RMS_EPS)
        for kc in range(KC):
            g = self.par(gkey, kc)
            src = self.big3[:, kc, :]
            self.STT("dve", self.ht3[:, kc, PT:30 + PT], src[:, PT - 30:PT], g, self.rstd[:, PT - 30:PT], ALU.mult, ALU.mult,
                     [self.R_big[kc], self.R_stat], [self.R_ht[kc]])
        if pre_exchange is not None:
            pre_exchange()
        self.halo_exchange()
        for kc in range(KC):
            g = self.par(gkey, kc)
            src = self.big3[:, kc, :]
            self.STT("dve", self.ht3[:, kc, 30:PT], src[:, 0:PT - 30], g, self.rstd[:, 0:PT - 30], ALU.mult, ALU.mult,
                     [self.R_big[kc], self.R_stat], [self.R_ht[kc]])
            b0 = 30 + PT
            dst = self.ht3[:, kc, b0:b0 + NS * 65].rearrange("p (s c) -> p s c", c=65)[:, :, 1:65]
            self.STT("dve", dst, self.pl(src, ("s",)), g, self.pl(self.rstd, ("s",)), ALU.mult, ALU.mult,
                     [self.R_big[kc], self.R_stat], [self.R_ht[kc]])
            if hlast is not None:
                hlast(kc, src, g)
            self.DMA("sp", self.d_xh[:, kc * T:(kc + 1) * T], src, [self.R_big[kc]], [self.R_xh[kc]], "xst%d" % kc)

    def halo_exchange(self):
        c = self.cfg
        KC, PT = c.KC, c.PT
        n = KC * 30
        Rd1, Rd2 = self.R("hx_in"), self.R("hx_out")
        self.DMA("sp", self.t_hxin.ap().rearrange("p (k c) -> p k c", c=30), self.ht3[:, :, PT:PT + 30],
                 self.R_ht, [Rd1], "hx1")
        self.allgather8(self.t_hxin, self.t_hxmid, self.t_hxout, Rd1, Rd2)
        src = self.t_hxout.ap().rearrange("(r p) f -> p r f", p=128)
        for i in range(4):
            self.DMA("sp", self.SQ[i][:, 0:2 * n].rearrange("p (r f) -> p r f", f=n), src[:, 2 * i:2 * i + 2, :],
                     [Rd2], [self.R_sq[i]], "hx2_%d" % i)
        self.halo_pending = True

    def halo_recv(self):
        if not getattr(self, "halo_pending", False):
            return
        self.halo_pending = False
        c = self.cfg
        KC = c.KC
        n = KC * 30
        halo = self.ht3[:, :, 0:30]
        for r in range(NCORES):
            s = self.SEL[:, r:r + 1]
            piece = self.SQ[r // 2][:, (r % 2) * n:(r % 2 + 1) * n].rearrange("p (k c) -> p k c", c=30)
            if r == 0:
                self.TS("dve", halo, piece, s, None, ALU.mult, None, [self.R_sq[r // 2], self.R_const], [self.R_hthalo])
            else:
                self.STT("dve", halo, piece, s, halo, ALU.mult, ALU.add, [self.R_sq[r // 2], self.R_const], [self.R_hthalo])

    def resid(self, gkey):
        c = self.cfg
        KC, T = c.KC, c.T
        self.colstats(False, RMS_EPS)
        xs = [self.STG[:, 0:T], self.STG[:, T:2 * T]]
        Rx = [self.R("xs0"), self.R("xs1")]
        for kc in range(KC):
            i = kc % 2
            self.DMA("sp", xs[i], self.d_xh[:, kc * T:(kc + 1) * T], [self.R_xh[kc]], [Rx[i]], "xld%d" % i)
            b = self.big3[:, kc, :]
            self.TT("dve", b, b, self.rstd, ALU.mult, [self.R_stat], [self.R_big[kc]])
            self.STT("dve", b, b, self.par(gkey, kc), xs[i], ALU.mult, ALU.add, [Rx[i], self.R_const], [self.R_big[kc]])

    def allgather8(self, tin, tmid, tout, R_in, R_out):
        Rm = self.R("agmid_" + tmid.name)
        self.add("pool", lambda e: e.collective_compute("AllGather", ALU.bypass, replica_groups=[[0, 1, 2, 3], [4, 5, 6, 7]],
                                                        ins=[tin.ap().opt()], outs=[tmid.ap().opt()]), [R_in], [Rm])
        self.add("pool", lambda e: e.collective_compute("AllGather", ALU.bypass, replica_groups=[[0, 4], [1, 5], [2, 6], [3, 7]],
                                                        ins=[tmid.ap().opt()], outs=[tout.ap().opt()]), [Rm], [R_out])

    def barrier(self):
        p = self.p
        toks = []
        for e in ENGS:
            if p.cnt[e] > 0:
                toks.append((("c", e, p.epoch[e]), p.cnt[e]))
        for k, v in p.dcnt.items():
            toks.append((k, v))
        for e in ENGS:
            p.pending[e] = list(toks)

    def conformer(self, j):
        c = self.cfg
        KC, T, PT, NS, UC = c.KC, c.T, c.PT, c.NS, c.UC
        ub = [self.AR2[:, 0:UC], self.AR2[:, UC:2 * UC]]
        Rub = [self.R("ub0"), self.R("ub1")]
        dwd = self.AR2[:, 2 * UC:2 * UC + 31 * 128]
        Rdwd = self.R("dwd")
        sgs = [self.AR2[:, 2 * UC + 31 * 128 + i * 1024:2 * UC + 31 * 128 + (i + 1) * 1024].bitcast(F32) for i in range(2)]
        Rsg = [self.R("sg0"), self.R("sg1")]
        sth = self.STG[:, 2 * T:2 * T + 64]
        Rsth = self.R("sth")
        ident = self.cbv(CB_ID, 128)
        sgi = 0
        for m in range(KC):
            ws_l, R_l = self.wload(self.d_pw1, j * 2 * KC + m)
            ws_g, R_g = self.wload(self.d_pw1, j * 2 * KC + KC + m)
            u, Ru = ub[m % 2], Rub[m % 2]
            ost = self.OST[:, (m % 2) * (1 + NS) * 32:(m % 2) * (1 + NS) * 32 + (1 + NS) * 30]
            Rost = self.R("ost%d" % (m % 2))
            for s in range(NS):
                o = ((j * KC + m) * NS + s) * 30
                stg = self.STG[:, 2 * T + 32 * (s % 2):2 * T + 32 * (s % 2) + 30]
                Rs = self.R("sth%d" % (s % 2))
                self.DMA("sp", stg, self.d_stconv[:, o:o + 30], [], [Rs], "sth%d" % (s % 2))
                ubase = 30 + PT + s * 94
                self.CP("act", u[:, ubase:ubase + 30], stg, [Rs], [Ru])
            for t in self.mt + [("h",)]:
                bA, bB = self.bank(), self.bank()
                if t[0] == "h":
                    self.halo_recv()
                for kc in range(KC):
                    rr = [self.R_ht[kc], R_l] + ([self.R_hthalo] if t[0] == "h" else [])
                    self.MM(self.bk(bA, t), ws_l[:, kc * 128:(kc + 1) * 128], self.hrhs(kc, t), kc == 0, kc == KC - 1,
                            rr, [self.R_bank[bA]])
                for kc in range(KC):
                    rr = [self.R_ht[kc], R_g] + ([self.R_hthalo] if t[0] == "h" else [])
                    self.MM(self.bk(bB, t), ws_g[:, kc * 128:(kc + 1) * 128], self.hrhs(kc, t), kc == 0, kc == KC - 1,
                            rr, [self.R_bank[bB]])
                n = self.tn(t)
                sg, Rs_ = sgs[sgi % 2], Rsg[sgi % 2]
                sgi += 1
                self.ACT(sg[:, 0:n], self.banks[bB][:, 0:n], AF.Sigmoid, [self.R_bank[bB], self.R_const], [Rs_],
                         bias=self.par(("pw1b", j), KC + m))
                bl = self.par(("pw1b", j), m)
                if t[0] == "h":
                    self.STT("dve", u[:, 0:30], self.banks[bA][:, 0:30], bl, sg[:, 0:30], ALU.add, ALU.mult,
                             [self.R_bank[bA], Rs_, self.R_const], [Ru])
                    self.TS("dve", u[:, 0:30], u[:, 0:30], self.SEL[:, 16:17], None, ALU.mult, None, [self.R_const], [Ru])
                elif t[0] == "p":
                    self.STT("dve", u[:, 30 + t[1]:30 + t[1] + n], self.banks[bA][:, 0:n], bl, sg[:, 0:n], ALU.add, ALU.mult,
                             [self.R_bank[bA], Rs_, self.R_const], [Ru])
                    if t[1] + n == PT:
                        self.STT("dve", ost[:, 0:30], self.banks[bA][:, n - 30:n], bl, sg[:, n - 30:n], ALU.add, ALU.mult,
                                 [self.R_bank[bA], Rs_, self.R_const], [Rost])
                else:
                    b0 = 30 + PT
                    dst = u[:, b0:b0 + NS * 94].rearrange("p (s c) -> p s c", c=94)[:, :, 30:94]
                    sg3 = sg[:, 0:n].rearrange("p (s c) -> p s c", c=64)
                    self.STT("dve", dst, self.bk(bA, t), bl, sg3, ALU.add, ALU.mult,
                             [self.R_bank[bA], Rs_, self.R_const], [Ru])
                    self.STT("dve", ost[:, 30:(1 + NS) * 30].rearrange("p (s c) -> p s c", c=30), self.bk(bA, t)[:, :, 34:64], bl,
                             sg3[:, :, 34:64], ALU.add, ALU.mult, [self.R_bank[bA], Rs_, self.R_const], [Rost])
            oo = (j * KC + m) * (1 + NS) * 30
            self.DMA("sp", self.d_oconv[:, oo:oo + (1 + NS) * 30], ost, [Rost], [self.R("d_oconv")], "ost%d" % (m % 2))
            wk = self.par(("dww", j), m * 31, 31)
            self.TT("dve", dwd.rearrange("p (k j) -> p k j", j=128), ident.unsqueeze(1).broadcast_to([128, 31, 128]),
                    wk.unsqueeze(2).broadcast_to([128, 31, 128]), ALU.mult, [self.R_const], [Rdwd])
            for t in self.mt:
                b = self.bank()
                n = self.tn(t)
                for k in range(31):
                    if t[0] == "p":
                        rhs = u[:, t[1] + k:t[1] + k + n]
                    else:
                        b0 = 30 + PT
                        rhs = u[:, b0:b0 + NS * 94].rearrange("p (s c) -> p s c", c=94)[:, :, k:k + 64]
                    self.MM(self.bk(b, t), dwd[:, k * 128:(k + 1) * 128], rhs, k == 0, k == 30, [Ru, Rdwd], [self.R_bank[b]])
                self.ACT(self.pl(self.big3[:, m, :], t, three=False), self.banks[b][:, 0:n], AF.Identity,
                         [self.R_bank[b], self.R_const], [self.R_big[m]], bias=self.par(("dwb", j), m))
        self.colstats(True, LN_EPS)
        tmp = self.STG[:, 0:T]
        tmp2 = self.STG[:, T:2 * T]
        Rt, Rt2 = self.R("xs0"), self.R("xs1")
        for kc in range(KC):
            b = self.big3[:, kc, :]
            self.TT("dve", tmp, b, self.mean, ALU.subtract, [self.R_big[kc], self.R_stat], [Rt])
            self.TT("dve", tmp, tmp, self.rstd, ALU.mult, [self.R_stat], [Rt])
            self.TS("dve", tmp, tmp, self.par(("clng", j), kc), self.par(("clnb", j), kc), ALU.mult, ALU.add, [self.R_const], [Rt])
            self.ACT(tmp2, tmp, AF.Sigmoid, [Rt], [Rt2])
            self.TT("dve", self.ht3[:, kc, 0:T], tmp, tmp2, ALU.mult, [Rt, Rt2], [self.R_ht[kc], self.R_hthalo])
        for m in range(KC):
            ws, Rw = self.wload(self.d_pw2, j * KC + m)
            for t in self.mt:
                b = self.bank()
                n = self.tn(t)
                for kc in range(KC):
                    self.MM(self.banks[b][:, 0:n], ws[:, kc * 128:(kc + 1) * 128], self.pl(self.ht3[:, kc, 0:T], t, three=False),
                            kc == 0, kc == KC - 1, [self.R_ht[kc], Rw], [self.R_bank[b]])
                self.ACT(self.pl(self.big3[:, m, :], t, three=False), self.banks[b][:, 0:n], AF.Identity,
                         [self.R_bank[b], self.R_const], [self.R_big[m]], bias=self.par(("pw2b", j), m))

    def ffn(self, i):
        c = self.cfg
        KC, FC, T, PT, NS, G, D, GC = c.KC, c.FC, c.T, c.PT, c.NS, c.G, c.D, c.GC
        wo = [self.AR2[:, 0:G * D], self.AR2[:, G * D:2 * G * D]]
        Rwo = [self.R("wo0"), self.R("wo1")]
        zg = self.AR2[:, 2 * G * D:2 * G * D + G * T].rearrange("p (g t) -> p g t", t=T)
        Rzg = [self.R("zg%d" % gi) for gi in range(G)]
        t2 = self.STG[:, T:2 * T]
        Rt2 = self.R("xs1")
        stgb = self.STG[:, 0:T].bitcast(BF16)
        gps = [stgb[:, 0:GC]]
        dw3s = [self.DW3[:, 0:384], self.DW3[:, 384:768]]
        Rgps = [self.R("xs0"), self.R("xs0")]
        Rdw = self.R("dw3")
        shst = self.STG[:, 2 * T:2 * T + 2 * NS]
        Rsh = self.R("stg2")
        ident = self.cbv(CB_ID, 128)
        for g in range(FC // G):
            wsl, Rw_o = wo[g % 2], Rwo[g % 2]
            self.DMA("pool", wsl, self.d_wout[i * (FC // G) + g], [], [Rw_o], "wo%d" % (g % 2))
            for gi in range(G):
                f = g * G + gi
                gp, Rgp = gps[0], Rgps[0]
                dw3 = dw3s[f % 2]
                g3 = gp[:, 2 + PT:2 + PT + NS * 66].rearrange("p (s c) -> p s c", c=66)
                ws_u, R_u = self.wload(self.d_win, i * 2 * FC + f)
                ws_g, R_g = self.wload(self.d_win, i * 2 * FC + FC + f)
                so = ((i * FC + f) * NS) * 2
                self.DMA("sp", shst, self.d_stffn[:, so:so + NS * 2], [], [Rsh], "gph")
                self.CP("act", g3[:, :, 0:2], shst.rearrange("p (s c) -> p s c", c=2), [Rsh], [Rgp])
                sl = f % 2
                ofs = self.OST[:, sl * (1 + NS) * 32:sl * (1 + NS) * 32 + (1 + NS) * 2]
                Rofs = self.R("ost%d" % sl)
                for t in self.mt + [("h2",)]:
                    b = self.bank()
                    n = self.tn(t)
                    if t[0] == "h2":
                        self.halo_recv()
                    for kc in range(KC):
                        rr = [self.R_ht[kc], R_g] + ([self.R_hthalo] if t[0] == "h2" else [])
                        self.MM(self.bk(b, t), ws_g[:, kc * 128:(kc + 1) * 128], self.hrhs(kc, t), kc == 0, kc == KC - 1,
                                rr, [self.R_bank[b]])
                    if t[0] == "h2":
                        dst = gp[:, 0:2]
                    elif t[0] == "p":
                        dst = gp[:, 2 + t[1]:2 + t[1] + n]
                    else:
                        dst = g3[:, :, 2:66]
                    self.CP("act", dst, self.bk(b, t), [self.R_bank[b]], [Rgp])
                    if t[0] == "p" and t[1] + n == PT:
                        self.CP("act", ofs[:, 0:2], self.banks[b][:, n - 2:n], [self.R_bank[b]], [Rofs])
                    if t[0] == "s":
                        self.CP("act", ofs[:, 2:(1 + NS) * 2].rearrange("p (s c) -> p s c", c=2), self.bk(b, t)[:, :, 62:64],
                                [self.R_bank[b]], [Rofs])
                oo = (i * FC + f) * (1 + NS) * 2
                self.DMA("sp", self.d_offn[:, oo:oo + (1 + NS) * 2], ofs, [Rofs], [self.R("d_offn")], "ost%d" % sl)
                ubanks = []
                for t in self.mt:
                    b = self.bank()
                    ubanks.append(b)
                    for kc in range(KC):
                        self.MM(self.bk(b, t), ws_u[:, kc * 128:(kc + 1) * 128], self.hrhs(kc, t), kc == 0, kc == KC - 1,
                                [self.R_ht[kc], R_u], [self.R_bank[b]])
                wk = self.par(("fdw", i), f * 3, 3)
                self.TT("dve", dw3.rearrange("p (k j) -> p k j", j=128), ident.unsqueeze(1).broadcast_to([128, 3, 128]),
                        wk.unsqueeze(2).broadcast_to([128, 3, 128]), ALU.mult, [self.R_const], [Rdw])
                bb = self.par(("fdb", i), f)
                for t in self.mt:
                    b = self.bank()
                    n = self.tn(t)
                    for k in range(3):
                        rhs = gp[:, t[1] + k:t[1] + k + n] if t[0] == "p" else g3[:, :, k:k + 64]
                        self.MM(self.bk(b, t), dw3[:, k * 128:(k + 1) * 128], rhs, k == 0, k == 2, [Rgp, Rdw], [self.R_bank[b]])
                    self.ACT(self.pl(t2, t, three=False), self.banks[b][:, 0:n], AF.Gelu_apprx_tanh, [self.R_bank[b], self.R_const],
                             [Rt2], bias=bb)
                for t, b in zip(self.mt, ubanks):
                    n = self.tn(t)
                    self.TT("dve", self.pl(zg[:, gi, :], t, three=False), self.banks[b][:, 0:n], self.pl(t2, t, three=False),
                            ALU.mult, [self.R_bank[b], Rt2], [Rzg[gi]])
            for m in range(KC):
                for t in self.mt:
                    b = self.bank()
                    n = self.tn(t)
                    for gi in range(G):
                        self.MM(self.banks[b][:, 0:n], wsl[:, gi * D + m * 128:gi * D + (m + 1) * 128],
                                self.pl(zg[:, gi, :], t, three=False), gi == 0, gi == G - 1, [Rzg[gi], Rw_o], [self.R_bank[b]])
                    dst = self.pl(self.big3[:, m, :], t, three=False)
                    if g == 0:
                        self.CP("act", dst, self.banks[b][:, 0:n], [self.R_bank[b]], [self.R_big[m]])
                    else:
                        self.TT("dve", dst, dst, self.banks[b][:, 0:n], ALU.add, [self.R_bank[b]], [self.R_big[m]])

    def rwkv_decl(self):
        c = self.cfg
        nc = self.nc
        if c.NRL == 0:
            return
        KC, HP, T, NS = c.KC, c.HP, c.T, c.NS
        di = lambda name, shape, dt=F32: nc.dram_tensor(name, list(shape), dt, kind="ExternalInput").ap()
        do = lambda name, shape, dt=F32: nc.dram_tensor(name, list(shape), dt, kind="ExternalOutput").ap()
        self.d_rkvo = di("w_rkvo", [c.NRL * 4 * HP, 128, KC * 128])
        self.d_l1 = di("w_l1", [c.NRL * 5, 128, KC * 128])
        self.d_l2 = di("w_l2", [c.NRL * HP, 128, 5 * 128])
        self.d_stshift = di("st_shift", [128, c.NRL * KC * NS])
        self.d_stwkv = di("st_wkv", [128, c.NRL * HP * NS * 64])
        self.d_oshift = do("o_shift", [128, c.NRL * KC * (1 + NS)])
        self.d_owkv = do("o_wkv", [128, c.NRL * HP * (1 + NS) * 64])
        self.d_rkv = nc.dram_tensor("rkv_s", [c.NRL * 3 * HP, 128, T], F32).ap()
        self.d_ycb = nc.dram_tensor("ycb_s", [3 * HP, 128, T], F32).ap()
        self.NQ = 4 if HP % 4 == 0 else 1
        hq = HP // self.NQ
        self.t_sgin = [nc.dram_tensor("seg_in%d" % q, [128, hq * 128], F32) for q in range(self.NQ)]
        self.t_sgmid = [nc.dram_tensor("seg_mid%d" % q, [4 * 128, hq * 128], F32) for q in range(self.NQ)]
        self.t_sgout = [nc.dram_tensor("seg_out%d" % q, [NCORES * 128, hq * 128], F32) for q in range(self.NQ)]

    def rwkv_alloc(self):
        c = self.cfg
        KC, T, NS, HP = c.KC, c.T, c.NS, c.HP
        if not hasattr(self, "LW"):
            sb0 = lambda name, cols, dt: self.stack.enter_context(self.nc.sbuf_tensor(name, [128, cols], dt))
            self.LW = [self.AR2[:, 8448:9088], self.AR2[:, 9088:9728]]
            self.SGL = self.STG[:, 2 * T:3 * T].bitcast(BF16)
            self.OSH = sb0("osh", KC * (1 + NS) + KC, F32)
            self.SEGB = self.STAT[:, 0:HP * 128]
            self.SMALL = self.AR2[:, 5888:8448].bitcast(F32)

    def rwkv(self, j):
        c = self.cfg
        KC, HP, T, PT, NS, NU, NCH, D = c.KC, c.HP, c.T, c.PT, c.NS, c.NU, c.NCH, c.D
        sb = lambda name, cols, dt: self.stack.enter_context(self.nc.sbuf_tensor(name + "_%d" % j, [128, cols], dt))
        i_layer = 2 * j + 1
        bigb = self.BIG[:, :].bitcast(BF16)
        xm3 = bigb[:, 0:KC * T].rearrange("p (k t) -> p k t", t=T)
        dd3 = bigb[:, KC * T:2 * KC * T].rearrange("p (k t) -> p k t", t=T)
        R_xm = [Res("xm%d" % k) for k in range(KC)]
        R_dd = [Res("dd%d" % k) for k in range(KC)]
        ht3 = self.ht3
        b0 = 30 + PT
        hs3 = lambda kc: ht3[:, kc, b0:b0 + NS * 65].rearrange("p (s c) -> p s c", c=65)
        ident = self.cbv(CB_ID, 128)
        bones = self.cbv(CB_BONES, 128)
        I2 = self.cbv(CB_I2, 64)
        omk = self.OSH[:, KC * (1 + NS):KC * (1 + NS) + KC]
        self.TS("dve", omk, self.par(("ka", j), 0, KC), -1.0, 1.0, ALU.mult, ALU.add, [self.R_const], [self.R("omk")])
        for kc in range(KC):
            rr = [self.R_ht[kc], self.R_hthalo]
            eng = "dve"
            self.TT(eng, dd3[:, kc, 0:PT], ht3[:, kc, 29:29 + PT], ht3[:, kc, 30:30 + PT], ALU.subtract, rr, [R_dd[kc]])
            self.TT(eng, dd3[:, kc, PT:T].rearrange("p (s c) -> p s c", c=64), hs3(kc)[:, :, 0:64], hs3(kc)[:, :, 1:65],
                    ALU.subtract, rr, [R_dd[kc]])
        tw, xa1, xv1 = self.SQ[0], self.SQ[1], self.SQ[2]
        R_tw, R_xa1, R_xv1, R_sgl = self.R_sq[0], self.R_sq[1], self.R_sq[2], self.R("sgl")
        stg = [self.STG[:, 0:T], self.STG[:, T:2 * T]]
        Rstg = [self.R("xs0"), self.R("xs1")]
        stgi = 0
        for q in range(6):
            for kc in range(KC):
                mx = self.par(("mix", j, q), kc)
                eng = "dve"
                xs_ = xm3[:, kc, PT:T].rearrange("p (s c) -> p s c", c=64)
                ds_ = dd3[:, kc, PT:T].rearrange("p (s c) -> p s c", c=64)
                rr_ = [R_dd[kc], self.R_ht[kc], self.R_const]
                if eng == "dve":
                    self.STT(eng, xm3[:, kc, 0:PT], dd3[:, kc, 0:PT], mx, ht3[:, kc, 30:30 + PT], ALU.mult, ALU.add, rr_, [R_xm[kc]])
                    self.STT(eng, xs_, ds_, mx, hs3(kc)[:, :, 1:65], ALU.mult, ALU.add, rr_, [R_xm[kc]])
                else:
                    self.TS(eng, xm3[:, kc, 0:PT], dd3[:, kc, 0:PT], mx, None, ALU.mult, None, rr_, [R_xm[kc]])
                    self.TT(eng, xm3[:, kc, 0:PT], xm3[:, kc, 0:PT], ht3[:, kc, 30:30 + PT], ALU.add, rr_, [R_xm[kc]])
                    self.TS(eng, xs_, ds_, mx, None, ALU.mult, None, rr_, [R_xm[kc]])
                    self.TT(eng, xs_, xs_, hs3(kc)[:, :, 1:65], ALU.add, rr_, [R_xm[kc]])
            if q in (0, 2, 3):
                qq = {0: 0, 2: 1, 3: 2}[q]
                for m in range(HP):
                    ws, Rw = self.wload(self.d_rkvo, (j * 4 + qq) * HP + m)
                    sg_, Rs_ = stg[stgi % 2], Rstg[stgi % 2]
                    stgi += 1
                    for (o, n) in self.ptiles:
                        b = self.bank()
                        for kc in range(KC):
                            self.MM(self.banks[b][:, 0:n], ws[:, kc * 128:(kc + 1) * 128], xm3[:, kc, o:o + n], kc == 0, kc == KC - 1,
                                    [R_xm[kc], Rw], [self.R_bank[b]])
                        self.CP("act", sg_[:, o:o + n], self.banks[b][:, 0:n], [self.R_bank[b]], [Rs_])
                    self.DMA("sp", self.d_rkv[(j * 3 + qq) * HP + m], sg_, [Rs_], [self.R("rkv%d_%d_%d" % (j, qq, m))], "rkvst%d" % ((stgi - 1) % 2))
            if q in (1, 4, 5) or (q == 3 and j > 0):
                specs = {1: [(0, 96, tw, R_tw, AF.Tanh, 0)], 4: [(1, 96, xa1, R_xa1, AF.Identity, 0)],
                         5: [(2, 128, self.SGL, R_sgl, AF.Sigmoid, 0), (3, 128, self.SGL, R_sgl, AF.Sigmoid, T)],
                         3: [(4, 64, xv1, R_xv1, AF.Identity, 0)]}[q]
                for (row, mc, dstt, Rd, fn, co) in specs:
                    ws, Rw = self.wload(self.d_l1, j * 5 + row)
                    for (o, n) in self.ptiles:
                        b = self.bank()
                        for kc in range(KC):
                            self.MM(self.banks[b][0:mc, 0:n], ws[:, kc * mc:(kc + 1) * mc], xm3[:, kc, o:o + n], kc == 0, kc == KC - 1,
                                    [R_xm[kc], Rw], [self.R_bank[b]])
                        self.ACT(dstt[0:mc, co + o:co + o + n], self.banks[b][0:mc, 0:n], fn, [self.R_bank[b]], [Rd])
        if DBG == 1:
            return
        self.barrier()
        self.rwkv_scan(j)
        self.barrier()
        if DBG in (2, 3, 4) or DBG >= 20:
            return
        for m in range(HP):
            ws, Rw = self.wload(self.d_rkvo, (j * 4 + 3) * HP + m)
            for (o, n) in self.ptiles:
                b = self.bank()
                for kc in range(KC):
                    self.MM(self.banks[b][:, 0:n], ws[:, kc * 128:(kc + 1) * 128], ht3[:, kc, o:o + n], kc == 0, kc == KC - 1,
                            [self.R_ht[kc], Rw], [self.R_bank[b]])
                self.CP("act", self.big3[:, m, o:o + n], self.banks[b][:, 0:n], [self.R_bank[b]], [self.R_big[m]])

    def rwkv_scan(self, j):
        c = self.cfg
        KC, HP, T, PT, NS, NU, NCH, D = c.KC, c.HP, c.T, c.PT, c.NS, c.NU, c.NCH, c.D
        assert NU % 4 == 0
        NB = NU * 128
        needB = 5 * NB + NB + NU * 64 + 8 * 512
        if not hasattr(self, "arF"):
            sb0 = lambda name, cols, dt: self.stack.enter_context(self.nc.sbuf_tensor(name, [128, cols], dt))
            self.arF = self.BIG if KC >= 15 else sb0("arF", 15 * T, F32)
            self.arB = self.HT if KC * c.HC >= needB else sb0("arB", needB, BF16)
            self.RF = [Res("f%d" % i) for i in range(14)]
            self.RQ = [[Res("bd%d_%d" % (i, g)) for g in range(NU // 4)] for i in range(6)]
            self.RVS = [Res("vs%d" % g) for g in range(NU // 4)]
            self.RT8 = [Res("t8_%d" % i) for i in range(8)]
            self.RT8b = [Res("t8b_%d" % i) for i in range(8)]
            self.RT8c = [Res("t8c_%d" % i) for i in range(8)]
            self.RT8d = [Res("t8d_%d" % i) for i in range(8)]
            self.RT8e = [Res("t8e_%d" % i) for i in range(8)]
            self.zeroed = False
        arF, arB, RF = self.arF, self.arB, self.RF
        F = lambda i: arF[:, i * T:(i + 1) * T]
        r_, k_, v_, a_, lw_, kk_, b_, cs0, cs1, e_, Y_, tmp, vf_, bon = [F(i) for i in range(14)]
        Rr, Rk, Rv, Ra, Rlw, Rkk, Rb, Rcs0, Rcs1, Re, RY, Rtmp, Rvf, Rbon = RF
        U3 = lambda ap: ap.rearrange("p (u c) -> p u c", c=64)
        Qbd, Kbd, Pbd, Vbd, RD, EE = [arB[:, i * NB:(i + 1) * NB] for i in range(6)]
        RQ, RK, RP, RV, RRD, REE = self.RQ
        VS = arB[:, 6 * NB:6 * NB + NU * 64]
        T8 = [arB[:, 6 * NB + NU * 64 + i * 512:6 * NB + NU * 64 + (i + 1) * 512] for i in range(8)]
        A4, AT4, S0, T0, AkT4, X4, H4, PT4 = T8
        RA4, RAT4, RS0, RT0, RAk, RX, RH, RPT = self.RT8
        bd4 = lambda ap: ap.rearrange("p (u h c) -> p u h c", h=2, c=64)
        ident = self.cbv(CB_ID, 128)
        bones = self.cbv(CB_BONES, 128)
        I2 = self.cbv(CB_I2, 64)
        MST, MSTT, MG = self.cbv(CB_MST, 512), self.cbv(CB_MSTT, 512), self.cbv(CB_MG, 512)
        Rc = self.R_const
        G4 = NU // 4
        SM = self.SMALL
        smb = SM[:, 0:640].bitcast(BF16)
        SS = [smb[:, 0:128], smb[:, 128:256]]
        STbd, TTbd = smb[:, 256:384], smb[:, 384:512]
        S0st = smb[:, 512:512 + NS * 64]
        S0bd = smb[:, 768:768 + NS * 128]
        RSS = [self.R("ss0"), self.R("ss1")]
        RSTbd, RTTbd, RS0st, RS0bd = self.R("stbd"), self.R("ttbd"), self.R("s0st"), self.R("s0bd")
        s0stg = SM[:, 640:640 + NS * 64]
        owk = SM[:, 640 + NS * 64:640 + NS * 64 + (1 + NS) * 64]
        Rs0stg, Rowk = self.R("s0stg"), self.R("owk")
        if True:
            for buf, RR in ((Qbd, RQ), (Kbd, RK), (Pbd, RP), (Vbd, RV)):
                self.add("dve", lambda e, buf=buf: e.memset(buf, 0.0), [], RR)
            self.add("dve", lambda e: e.memset(STbd, 0.0), [], [RSTbd])
            self.add("dve", lambda e: e.memset(TTbd, 0.0), [], [RTTbd])
            self.add("dve", lambda e: e.memset(S0bd, 0.0), [], [RS0bd])
        allg = lambda RR: list(RR)
        bq = [0]

        def nb():
            bq[0] += 1
            return (bq[0] - 1) % 8
        pt_ = self.ptiles
        def prepA(m):
                lw2, Rlw2 = self.LW[m % 2], self.R("lw2_%d" % (m % 2))
                self.DMA("pool", lw2, self.d_l2[j * HP + m], [], [Rlw2], "lw2_%d" % (m % 2))
                for (dst, Rd, qq) in ((r_, Rr, 0), (k_, Rk, 1), (v_, Rv, 2)):
                    self.DMA("sp", dst, self.d_rkv[(j * 3 + qq) * HP + m], [self.R("rkv%d_%d_%d" % (j, qq, m))], [Rd], "ld%d" % qq)
                if j > 0:
                    self.DMA("sp", vf_, self.d_rkv[(0 * 3 + 2) * HP + m], [self.R("rkv%d_%d_%d" % (0, 2, m))], [Rvf], "ldvf")
                tw, xa1, xv1 = self.SQ[0], self.SQ[1], self.SQ[2]
                for (o, n) in pt_:
                    b = nb()
                    self.MM(self.banks[b][:, 0:n], lw2[0:96, 0:128], tw[0:96, o:o + n], True, True, [Rlw2, self.R_sq[0]], [self.R_bank[b]])
                    self.ACT(lw_[:, o:o + n], self.banks[b][:, 0:n], AF.Sigmoid, [self.R_bank[b], Rc], [Rlw], bias=self.par(("w0", j), m))
                    b = nb()
                    self.MM(self.banks[b][:, 0:n], lw2[0:96, 128:256], xa1[0:96, o:o + n], True, True, [Rlw2, self.R_sq[1]], [self.R_bank[b]])
                    self.ACT(a_[:, o:o + n], self.banks[b][:, 0:n], AF.Sigmoid, [self.R_bank[b], Rc], [Ra], bias=self.par(("a0", j), m))
                    if j > 0:
                        b = nb()
                        self.MM(self.banks[b][:, 0:n], lw2[0:64, 256:384], xv1[0:64, o:o + n], True, True, [Rlw2, self.R_sq[2]], [self.R_bank[b]])
                        self.ACT(tmp[:, o:o + n], self.banks[b][:, 0:n], AF.Sigmoid, [self.R_bank[b], Rc], [Rtmp], bias=self.par(("v0", j - 1), m))
                self.ACT(lw_, lw_, AF.Identity, [], [Rlw], scale=-0.6065306597126334)
                if j > 0:
                    self.TT("pool", vf_, vf_, v_, ALU.subtract, [Rv], [Rvf])
                    self.TT("pool", vf_, vf_, tmp, ALU.mult, [Rtmp], [Rvf])
                    self.TT("pool", v_, v_, vf_, ALU.add, [Rvf], [Rv])
                yield
                self.ACT(kk_, k_, AF.Identity, [Rk, Rc], [Rkk], scale=self.par(("kk", j), m))
                sq, Rsq = self.SQ[3], self.R_sq[3]
                self.ACT(sq[:, :], kk_, AF.Square, [Rkk], [Rsq])
                for (o, n) in pt_:
                    b = nb()
                    self.MM(self.banks[b][:, 0:n], bones, sq[:, o:o + n], True, True, [Rsq, Rc], [self.R_bank[b]])
                    self.ACT(tmp[:, o:o + n], self.banks[b][:, 0:n], AF.Sqrt, [self.R_bank[b]], [Rtmp])
                self.TS("dve", tmp, tmp, 1e-12, None, ALU.max, None, [], [Rtmp])
                self.add("dve", lambda e: e.reciprocal(out=tmp, in_=tmp), [], [Rtmp])
                self.TT("dve", kk_, kk_, tmp, ALU.mult, [Rtmp], [Rkk])
                yield
                omk = self.OSH[:, KC * (1 + NS) + m:KC * (1 + NS) + m + 1]
                self.TS("dve", tmp, a_, self.par(("ka", j), m), omk, ALU.mult, ALU.add, [Ra, Rc, self.R("omk")], [Rtmp])
                self.TT("dve", k_, k_, tmp, ALU.mult, [Rtmp], [Rk])
                self.TT("dve", b_, kk_, a_, ALU.mult, [Rkk, Ra], [Rb])
                yield
                self.STT("dve", sq[:, :], r_, self.par(("rk", j), m), k_, ALU.mult, ALU.mult, [Rr, Rk, Rc], [Rsq])
                for (o, n) in pt_:
                    b = nb()
                    self.MM(self.banks[b][:, 0:n], bones, sq[:, o:o + n], True, True, [Rsq, Rc], [self.R_bank[b]])
                    self.TT("dve", bon[:, o:o + n], self.banks[b][:, 0:n], v_[:, o:o + n], ALU.mult, [self.R_bank[b], Rv], [Rbon])
                self.DMA("sp", self.d_ycb[2 * HP + m], bon, [Rbon], [self.R("ycb2_%d" % m)], "stbon")
                src, Rs_, dst, Rd_ = lw_, Rlw, cs0, Rcs0
                for d in (1, 2, 4, 8, 16, 32):
                    self.TT("pool", U3(dst)[:, :, d:64], U3(src)[:, :, d:64], U3(src)[:, :, 0:64 - d], ALU.add, [Rs_], [Rd_])
                    self.CP("act", U3(dst)[:, :, 0:d], U3(src)[:, :, 0:d], [Rs_], [Rd_])
                    yield
                    if src is lw_:
                        src, Rs_, dst, Rd_ = cs0, Rcs0, cs1, Rcs1
                    else:
                        src, Rs_, dst, Rd_ = dst, Rd_, src, Rs_
                cs, Rcs, oth, Roth = src, Rs_, dst, Rd_
                gcb = self.SMALL[:, 1216:1216 + NU]
                Rgcb = self.R("gcb")
                self.ACT(e_, cs, AF.Exp, [Rcs], [Re])
                self.TT("pool", r_, r_, e_, ALU.mult, [Re], [Rr])
                self.CP("act", gcb.unsqueeze(2), U3(e_)[:, :, 63:64], [Re], [Rgcb])
                yield
                self.TT("dve", oth, cs, lw_, ALU.subtract, [Rcs, Rlw], [Roth])
                self.ACT(e_, oth, AF.Exp, [Roth], [Re])
                self.TT("pool", kk_, kk_, e_, ALU.mult, [Re], [Rkk])
                yield
                self.ACT(e_, cs, AF.Exp, [Rcs], [Re], scale=-1.0)
                self.TT("dve", b_, b_, e_, ALU.mult, [Re], [Rb])
                self.TT("pool", k_, k_, e_, ALU.mult, [Re], [Rk])
                yield

        def prepB(m):
            RD3 = RD.rearrange("p (u n) -> p u n", n=128)
            gcb = self.SMALL[:, 1216:1216 + NU]
            Rgcb = self.R("gcb")
            self.CP("act", RD3[:, :, 0:64], U3(r_), [Rr], allg(RRD))
            self.TT("dve", RD3[:, :, 64:128], I2.unsqueeze(1).broadcast_to([128, NU, 64]),
                    gcb.unsqueeze(2).broadcast_to([128, NU, 64]), ALU.mult, [Rgcb, Rc], allg(RRD))
            for h in range(2):
                ps = slice(64 * h, 64 * h + 64)
                self.CP("pool", bd4(Pbd)[ps, :, h, :], U3(kk_)[ps], [Rkk], allg(RP))
                self.CP("dve" if h else "pool", bd4(Qbd)[ps, :, h, :], U3(b_)[ps], [Rb], allg(RQ))
                self.CP("act" if h else "dve", bd4(Kbd)[ps, :, h, :], U3(k_)[ps], [Rk], allg(RK))
                self.CP("act", bd4(Vbd)[ps, :, h, :], U3(v_)[ps], [Rv], allg(RV))

        def groups(m):
                def group_steps(g, T8s, RT8s):
                    A4, AT4, S0, T0, AkT4, X4, H4, PT4 = T8s
                    RA4, RAT4, RS0, RT0, RAk, RX, RH, RPT = RT8s
                    us = list(range(4 * g, 4 * g + 4))

                    def mmu(lbuf, Rl, rfn, Rr_, oc=128, lfn=None):
                        b = nb()
                        for ui, u in enumerate(us):
                            l = lbuf[:, u * 128:(u + 1) * 128] if lfn is None else lfn(ui)
                            self.MM(self.banks[b][:, ui * oc:(ui + 1) * oc], l, rfn(ui, u), True, True, list(Rl) + list(Rr_), [self.R_bank[b]])
                        return b
                    ub = lambda buf: (lambda ui, u: buf[:, u * 128:(u + 1) * 128])
                    t4 = lambda buf: (lambda ui, u=None: buf[:, ui * 128:(ui + 1) * 128])
                    cst = lambda ap: (lambda ui, u: ap)
                    b = mmu(Qbd, [RQ[g]], ub(Pbd), [RP[g]])
                    self.TT("dve", A4, self.banks[b][:, :], MST, ALU.mult, [self.R_bank[b], Rc], [RA4])
                    b = mmu(Pbd, [RP[g]], ub(Qbd), [RQ[g]])
                    self.TT("dve", AT4, self.banks[b][:, :], MSTT, ALU.mult, [self.R_bank[b], Rc], [RAT4])
                    yield
                    b = mmu(Pbd, [RP[g]], ub(Kbd), [RK[g]])
                    self.TT("dve", AkT4, self.banks[b][:, :], MSTT, ALU.mult, [self.R_bank[b], Rc], [RAk])
                    b = mmu(Qbd, [RQ[g]], ub(RD), [RRD[g]])
                    self.TT("dve", X4, self.banks[b][:, :], MG, ALU.mult, [self.R_bank[b], Rc], [RX])
                    yield
                    b = mmu(Kbd, [RK[g]], ub(RD), [RRD[g]])
                    self.TT("dve", H4, self.banks[b][:, :], MG, ALU.mult, [self.R_bank[b], Rc], [RH])
                    b = mmu(Pbd, [RP[g]], cst(ident), [Rc])
                    self.CP("act", PT4, self.banks[b][:, :], [self.R_bank[b]], [RPT])
                    yield
                    b2 = mmu(Vbd, [RV[g]], cst(I2), [Rc], oc=64)
                    b = mmu(Vbd, [RV[g]], cst(ident), [Rc])
                    self.CP("act", VS[:, 4 * g * 64:(4 * g + 4) * 64], self.banks[b2][:, 0:256], [self.R_bank[b2]], [self.RVS[g]])
                    self.CP("act", Vbd[:, 4 * g * 128:(4 * g + 4) * 128], self.banks[b][:, :], [self.R_bank[b]], [RV[g]])
                    yield
                    b = mmu(None, [RAT4], t4(X4), [RX], lfn=t4(AT4))
                    self.TT("dve", X4, X4, self.banks[b][:, :], ALU.subtract, [self.R_bank[b]], [RX])
                    yield
                    Pc, RPc, PTc, RPTc = A4, RA4, AT4, RAT4
                    Pn, RPn, PTn, RPTn = S0, RS0, T0, RT0
                    for lvl in range(5):
                        if lvl < 4:
                            b = mmu(None, [RPTc], t4(Pc), [RPc], lfn=t4(PTc))
                            self.CP("act", Pn, self.banks[b][:, :], [self.R_bank[b]], [RPn])
                        b = mmu(None, [RPc], t4(PTc), [RPTc], lfn=t4(Pc))
                        self.CP("act", PTn, self.banks[b][:, :], [self.R_bank[b]], [RPTn])
                        yield
                        b = mmu(None, [RPTn], t4(X4), [RX], lfn=t4(PTn))
                        self.TT("dve", X4, X4, self.banks[b][:, :], ALU.add, [self.R_bank[b]], [RX])
                        yield
                        Pc, RPc, PTc, RPTc, Pn, RPn, PTn, RPTn = Pn, RPn, PTn, RPTn, Pc, RPc, PTc, RPTc
                    b = mmu(None, [RAk], t4(X4), [RX], lfn=t4(AkT4))
                    gs = slice(4 * g * 128, (4 * g + 4) * 128)
                    self.TT("dve", EE[:, gs], H4, self.banks[b][:, :], ALU.subtract, [self.R_bank[b], RH], [REE[g]])
                    b = mmu(None, [RPT], t4(X4), [RX], lfn=t4(PT4))
                    self.TT("dve", RD[:, gs], RD[:, gs], self.banks[b][:, :], ALU.subtract, [self.R_bank[b]], [RRD[g]])
                    yield
                    RDg = RD[:, gs].rearrange("p (u n) -> p u n", n=128)
                    EEg = EE[:, gs].rearrange("p (u n) -> p u n", n=128)
                    for h in range(2):
                        ps = slice(64 * h, 64 * h + 64)
                        self.CP("act", bd4(Qbd[:, gs])[ps, :, h, :], RDg[ps, :, 64:128], [RRD[g]], [RQ[g]])
                        self.CP("act", bd4(Kbd[:, gs])[ps, :, h, :], EEg[ps, :, 64:128], [REE[g]], [RK[g]])

                T8b = [self.AR2[:, i * 512:(i + 1) * 512] for i in range(8)]
                NW = 3 if self.cfg.KC * 128 >= 2048 else 2
                sets = [(T8, self.RT8), (T8b, self.RT8b)]
                if NW == 3:
                    sets.append(([self.WS[i // 4][:, (i % 4) * 512:(i % 4 + 1) * 512] for i in range(8)], self.RT8c))
                    if KC >= 16 and G4 >= 5:
                        NW = 5
                        stb = self.STG[:, 0:2 * T].bitcast(BF16)
                        sets.append(([stb[:, i * 512:(i + 1) * 512] for i in range(8)], self.RT8d))
                        f15 = F(15).bitcast(BF16)
                        sets.append(([self.WS[2][:, i * 512:(i + 1) * 512] for i in range(4)] +
                                     [f15[:, i * 512:(i + 1) * 512] for i in range(4)], self.RT8e))
                for g0 in range(0, G4, NW):
                    gens = [group_steps(g, *sets[(g - g0) % NW]) for g in range(g0, min(g0 + NW, G4))]
                    while gens:
                        for gen in list(gens):
                            try:
                                next(gen)
                            except StopIteration:
                                gens.remove(gen)
                        yield

        def seqpass(m):
                M1u = lambda u: RD[:, u * 128:u * 128 + 64]
                Eu = lambda u: EE[:, u * 128:u * 128 + 64]
                Msb = lambda u: Qbd[:, u * 128:(u + 1) * 128]
                Esb = lambda u: Kbd[:, u * 128:(u + 1) * 128]
                VTb = lambda u: Vbd[:, u * 128:(u + 1) * 128]
                VSu = lambda u: VS[:, u * 64:(u + 1) * 64]
                self.add("dve", lambda e: e.memset(SS[0][:, 0:64], 0.0), [], [RSS[0]])
                self.CP("dve", SS[0][:, 64:128], I2, [Rc], [RSS[0]])
                for h in range(2):
                    ps = slice(64 * h, 64 * h + 64)
                    self.add("dve", lambda e, ps=ps, h=h: e.memset(STbd[ps, 64 * h:64 * h + 64], 0.0), [], [RSTbd])
                    self.CP("dve", TTbd[ps, 64 * h:64 * h + 64], I2[ps, :], [Rc], [RTTbd])
                cf = F(14)
                Rcf_ = self.R("f14")
                for cgrp in range(0, NCH, 4):
                    by, bc = nb(), nb()
                    cl = list(range(cgrp, min(cgrp + 4, NCH)))
                    for ci, ch in enumerate(cl):
                        g = ch // 4
                        cur, nxt = SS[ch % 2], SS[(ch + 1) % 2]
                        Rcur, Rnxt = RSS[ch % 2], RSS[(ch + 1) % 2]
                        cs_ = slice(ci * 64, ci * 64 + 64)
                        bs = nb()
                        self.MM(self.banks[bs][:, 0:64], Msb(ch), cur[:, 0:64], True, False, [RQ[g], Rcur], [self.R_bank[bs]])
                        self.MM(self.banks[bs][:, 0:64], Esb(ch), VSu(ch), False, True, [RK[g], self.RVS[g]], [self.R_bank[bs]])
                        self.MM(self.banks[bs][:, 64:128], Msb(ch), cur[:, 64:128], True, True, [RQ[g], Rcur], [self.R_bank[bs]])
                        self.CP("dve", nxt, self.banks[bs][:, 0:128], [self.R_bank[bs]], [Rnxt])
                        self.MM(self.banks[by][:, cs_], STbd, M1u(ch), True, False, [RSTbd, RRD[g]], [self.R_bank[by]])
                        self.MM(self.banks[by][:, cs_], VTb(ch), Eu(ch), False, True, [RV[g], REE[g]], [self.R_bank[by]])
                        self.MM(self.banks[bc][:, cs_], TTbd, M1u(ch), True, True, [RTTbd, RRD[g]], [self.R_bank[bc]])
                        if ch == NCH - 1:
                            self.CP("dve", self.SEGB[:, m * 128:m * 128 + 64], self.banks[bs][:, 0:64], [self.R_bank[bs]], [self.R("segb")])
                        for h in range(2):
                            ps = slice(64 * h, 64 * h + 64)
                            self.CP("act", STbd[ps, 64 * h:64 * h + 64], nxt[ps, 0:64], [Rnxt], [RSTbd])
                            self.CP("dve", TTbd[ps, 64 * h:64 * h + 64], nxt[ps, 64:128], [Rnxt], [RTTbd])
                        yield
                    w = len(cl) * 64
                    self.CP("act", Y_[:, cgrp * 64:cgrp * 64 + w], self.banks[by][:, 0:w], [self.R_bank[by]], [RY])
                    self.CP("dve", cf[:, cgrp * 64:cgrp * 64 + w], self.banks[bc][:, 0:w], [self.R_bank[bc]], [Rcf_])
                b = nb()
                self.MM(self.banks[b][:, 0:64], TTbd, I2, True, True, [RTTbd, Rc], [self.R_bank[b]])
                self.CP("act", self.SEGB[:, m * 128 + 64:m * 128 + 128], self.banks[b][:, 0:64], [self.R_bank[b]], [self.R("segb")])
                self.DMA("sp", self.d_ycb[HP + m, :, 0:PT], cf[:, 0:PT], [Rcf_], [self.R("ycb1_%d" % m)], "stcf")
                so = ((j * HP + m) * NS) * 64
                self.DMA("sp", s0stg, self.d_stwkv[:, so:so + NS * 64], [], [Rs0stg], "s0ld")
                self.CP("act", S0st, s0stg, [Rs0stg], [RS0st])
                S0bd4 = S0bd.rearrange("p (s h c) -> p s h c", h=2, c=64)
                for h in range(2):
                    ps = slice(64 * h, 64 * h + 64)
                    self.CP("dve", S0bd4[ps, :, h, :], S0st[ps, :].rearrange("p (s c) -> p s c", c=64), [RS0st], [RS0bd])
                by, bs = nb(), nb()
                for s_ in range(NS):
                    u = NCH + s_
                    g = u // 4
                    cs_ = slice(s_ * 64, s_ * 64 + 64)
                    self.MM(self.banks[by][:, cs_], S0bd[:, s_ * 128:(s_ + 1) * 128], M1u(u), True, False, [RS0bd, RRD[g]], [self.R_bank[by]])
                    self.MM(self.banks[by][:, cs_], VTb(u), Eu(u), False, True, [RV[g], REE[g]], [self.R_bank[by]])
                    self.MM(self.banks[bs][:, cs_], Msb(u), S0st[:, cs_], True, False, [RQ[g], RS0st], [self.R_bank[bs]])
                    self.MM(self.banks[bs][:, cs_], Esb(u), VSu(u), False, True, [RK[g], self.RVS[g]], [self.R_bank[bs]])
                self.CP("act", Y_[:, PT:T], self.banks[by][:, 0:NS * 64], [self.R_bank[by]], [RY])
                self.CP("dve", owk[:, 64:(1 + NS) * 64], self.banks[bs][:, 0:NS * 64], [self.R_bank[bs]], [Rowk])
                oo = ((j * HP + m) * (1 + NS) + 1) * 64
                self.DMA("sp", self.d_owkv[:, oo:oo + NS * 64], owk[:, 64:(1 + NS) * 64], [Rowk], [self.R("d_owkv")], "stowk")
                self.DMA("sp", self.d_ycb[m], Y_, [RY], [self.R("ycb0_%d" % m)], "sty")

        def drain(gen):
            if gen is not None:
                for _ in gen:
                    pass

        def step(gen):
            if gen is None:
                return None
            try:
                next(gen)
                return gen
            except StopIteration:
                return None
        drain(prepA(0))
        for m in range(HP):
            prepB(m)
            nxt_prep = prepA(m + 1) if m + 1 < HP else None
            for _ in groups(m):
                nxt_prep = step(nxt_prep)
            for _ in seqpass(m):
                nxt_prep = step(nxt_prep)
            drain(nxt_prep)
        self.zeroed = True
        if DBG == 2 or DBG >= 20:
            return
        hq = HP // self.NQ
        for q in range(self.NQ):
            Rs1, Rs2 = self.R("seg_in%d" % q), self.R("seg_out%d" % q)
            self.DMA("sp", self.t_sgin[q].ap(), self.SEGB[:, q * hq * 128:(q + 1) * hq * 128], [self.R("segb")], [Rs1], "segst")
            self.allgather8(self.t_sgin[q], self.t_sgmid[q], self.t_sgout[q], Rs1, Rs2)
        if DBG == 3:
            return
        self.barrier()
        self.rwkv_finish(j)

    def rwkv_finish(self, j):
        c = self.cfg
        KC, HP, T, PT, NS, NU, NCH, D = c.KC, c.HP, c.T, c.PT, c.NS, c.NU, c.NCH, c.D
        arF, RF = self.arF, self.RF
        F = lambda i: arF[:, i * T:(i + 1) * T]
        slotsets = [(10, 13, 11, 7), (0, 1, 2, 3)]
        A2 = self.AR2
        A2f = A2[:, 0:10240].bitcast(F32)
        SG = A2f[:, 0:1024]
        SG3 = SG.rearrange("p (r n) -> p r n", n=128)
        Tbd8 = A2[:, 2048:3072]
        Pl = A2f[:, 1536:2112]
        Pb = [A2[:, 4224:4288], A2[:, 4288:4352]]
        Sst, Sbd = A2[:, 4352:4416], A2[:, 4416:4544]
        Ssel = A2f[:, 2304:2432]
        Ceffb = A2[:, 4864:4864 + PT]
        RSG, RT8, RPl, RSst, RSbd, RSsel, RCb = (self.R(n) for n in ("c_sg", "c_t8", "c_pl", "c_sst", "c_sbd", "c_ssel", "c_cb"))
        RPb = [self.R("c_pb0"), self.R("c_pb1")]
        bones = self.cbv(CB_BONES, 128)
        Rc = self.R_const
        SM = self.SMALL
        owk = SM[:, 640 + NS * 64:640 + NS * 64 + (1 + NS) * 64]
        Rowk = self.R("owk")
        hq = HP // self.NQ
        self.add("dve", lambda e: e.memset(Tbd8, 0.0), [], [RT8])
        self.add("dve", lambda e: e.memset(Sbd, 0.0), [], [RSbd])
        self.add("dve", lambda e: e.memset(Pb[0], 0.0), [], [RPb[0]])
        self.add("dve", lambda e: e.memset(Pl[:, 0:64], 0.0), [], [RPl])
        T84 = Tbd8.rearrange("p (r h c) -> p r h c", h=2, c=64)
        seg3 = [t.ap().rearrange("(r p) f -> p r f", p=128) for t in self.t_sgout]
        yb, ysq = self.SQ[0], self.SQ[1]
        Ryb, Rysq = self.R_sq[0], self.R_sq[1]
        bq = [0]

        def nb():
            bq[0] += 1
            return (bq[0] - 1) % 8
        def stage1(m):
            sY, sB, sC, sT = slotsets[m % 2]
            Y_, bon, cfst, t1 = F(sY), F(sB), F(sC), F(sT)
            RY, Rbon, Rcf, Rt1 = RF[sY], RF[sB], RF[sC], RF[sT]
            lw2, Rlw2 = self.LW[m % 2], self.R("lw2_%d" % (m % 2))
            lw2, Rlw2 = self.LW[m % 2], self.R("lw2_%d" % (m % 2))
            self.DMA("pool", lw2, self.d_l2[j * HP + m], [], [Rlw2], "lw2_%d" % (m % 2))
            self.DMA("sp", SG3, seg3[m // hq][:, :, (m % hq) * 128:(m % hq + 1) * 128], [self.R("seg_out%d" % (m // hq))], [RSG], "c_sg")
            self.DMA("sp", Y_, self.d_ycb[m], [self.R("ycb0_%d" % m)], [RY], "c_y%d" % (m % 2))
            self.DMA("sp", cfst[:, 0:PT], self.d_ycb[HP + m, :, 0:PT], [self.R("ycb1_%d" % m)], [Rcf], "c_cf%d" % (m % 2))
            self.DMA("sp", bon, self.d_ycb[2 * HP + m], [self.R("ycb2_%d" % m)], [Rbon], "c_bon%d" % (m % 2))
            for h in range(2):
                ps = slice(64 * h, 64 * h + 64)
                self.CP("act" if h else "dve", T84[ps, :, h, :], SG3[ps, :, 64:128], [RSG], [RT8])
            for r in range(NCORES):
                b = nb()
                cur, nxt = r % 2, (r + 1) % 2
                self.MM(self.banks[b][:, 0:64], Tbd8[:, r * 128:(r + 1) * 128], Pb[cur], True, True, [RT8, RPb[cur]], [self.R_bank[b]])
                self.TT("dve", Pb[nxt], self.banks[b][:, 0:64], SG3[:, r, 0:64], ALU.add, [self.R_bank[b], RSG], [RPb[nxt]])
                self.TT("dve", Pl[:, (r + 1) * 64:(r + 2) * 64], self.banks[b][:, 0:64], SG3[:, r, 0:64], ALU.add, [self.R_bank[b], RSG], [RPl])
                yield
            for k_, base in ((0, 0), (1, 1)):
                dst = Ssel[:, k_ * 64:(k_ + 1) * 64]
                for r in range(NCORES):
                    src = Pl[:, (r + base) * 64:(r + base + 1) * 64]
                    sc = self.SEL[:, 8 + r:9 + r]
                    if r == 0:
                        self.TS("dve", dst, src, sc, None, ALU.mult, None, [RPl, Rc], [RSsel])
                    else:
                        self.STT("dve", dst, src, sc, dst, ALU.mult, ALU.add, [RPl, Rc], [RSsel])
            self.CP("act", owk[:, 0:64], Ssel[:, 64:128], [RSsel], [Rowk])
            oo = ((j * HP + m) * (1 + NS)) * 64
            self.DMA("sp", self.d_owkv[:, oo:oo + 64], owk[:, 0:64], [Rowk], [self.R("d_owkv")], "stowk")
            self.CP("act", Sst, Ssel[:, 0:64], [RSsel], [RSst])
            for h in range(2):
                ps = slice(64 * h, 64 * h + 64)
                self.CP("dve", Sbd[ps, 64 * h:64 * h + 64], Sst[ps, :], [RSst], [RSbd])
            self.CP("act", Ceffb, cfst[:, 0:PT], [Rcf], [RCb])
            for (o, n) in [(o, min(512, PT - o)) for o in range(0, PT, 512)]:
                b = nb()
                self.MM(self.banks[b][:, 0:n], Sbd, Ceffb[:, o:o + n], True, True, [RSbd, RCb], [self.R_bank[b]])
                self.TT("dve", Y_[:, o:o + n], Y_[:, o:o + n], self.banks[b][:, 0:n], ALU.add, [self.R_bank[b]], [RY])
            yield

        def stage2(m):
            sY, sB, sC, sT = slotsets[m % 2]
            Y_, bon, cfst, t1 = F(sY), F(sB), F(sC), F(sT)
            RY, Rbon, Rcf, Rt1 = RF[sY], RF[sB], RF[sC], RF[sT]
            lw2, Rlw2 = self.LW[m % 2], self.R("lw2_%d" % (m % 2))
            self.CP("act", yb[:, :], Y_, [RY], [Ryb])
            self.ACT(ysq[:, :], Y_, AF.Square, [RY], [Rysq])
            mean, rstd = self.mean, self.rstd
            for (o, n) in self.ptiles:
                b1, b2 = nb(), nb()
                self.MM(self.banks[b1][:, 0:n], bones, yb[:, o:o + n], True, True, [Ryb, Rc], [self.R_bank[b1]])
                self.MM(self.banks[b2][:, 0:n], bones, ysq[:, o:o + n], True, True, [Rysq, Rc], [self.R_bank[b2]])
                mn, rs = mean[:, o:o + n], rstd[:, o:o + n]
                self.TS("dve", mn, self.banks[b1][:, 0:n], 1.0 / 64, None, ALU.mult, None, [self.R_bank[b1]], [self.R_stat])
                self.TT("dve", rs, mn, mn, ALU.mult, [], [self.R_stat])
                self.STT("dve", rs, self.banks[b2][:, 0:n], 1.0 / 64, rs, ALU.mult, ALU.subtract, [self.R_bank[b2]], [self.R_stat])
                self.ACT(rs, rs, AF.Sqrt, [Rc], [self.R_stat], bias=self.epsc(GN_EPS))
                self.add("dve", lambda e, rs=rs: e.reciprocal(out=rs, in_=rs), [], [self.R_stat])
                yield
            self.TT("dve", t1, Y_, mean, ALU.subtract, [RY, self.R_stat], [Rt1])
            self.TT("dve", t1, t1, rstd, ALU.mult, [self.R_stat], [Rt1])
            yield
            self.TS("dve", t1, t1, self.par(("rlng", j), m), self.par(("rlnb", j), m), ALU.mult, ALU.add, [Rc], [Rt1])
            self.TT("dve", t1, t1, bon, ALU.add, [Rbon], [Rt1])
            yield
            for (o, n) in self.ptiles:
                b = nb()
                self.MM(self.banks[b][:, 0:n], lw2[:, 384:512], self.SGL[:, o:o + n], True, False, [Rlw2, self.R("sgl")], [self.R_bank[b]])
                self.MM(self.banks[b][:, 0:n], lw2[:, 512:640], self.SGL[:, T + o:T + o + n], False, True, [Rlw2, self.R("sgl")], [self.R_bank[b]])
                self.TT("dve", self.ht3[:, m, o:o + n], t1[:, o:o + n], self.banks[b][:, 0:n], ALU.mult, [self.R_bank[b], Rt1],
                        [self.R_ht[m], self.R_hthalo])

            yield

        def step(gen):
            if gen is None:
                return None
            try:
                next(gen)
                return gen
            except StopIteration:
                return None
        g1 = stage1(0)
        while g1 is not None:
            g1 = step(g1)
        for m in range(HP):
            g2 = stage2(m)
            g1 = stage1(m + 1) if m + 1 < HP else None
            while g1 is not None or g2 is not None:
                g1 = step(g1)
                g2 = step(g2)

    def rwkv_pre(self, j):
        self.halo_recv()
        c = self.cfg
        KC, PT, NS = c.KC, c.PT, c.NS
        stg = self.SMALL[:, 0:KC * NS]
        Rst = self.R("shst")
        self.DMA("sp", stg, self.d_stshift[:, j * KC * NS:(j + 1) * KC * NS], [], [Rst], "shld")
        b0 = 30 + PT
        for kc in range(KC):
            dst = self.ht3[:, kc, b0:b0 + NS * 65].rearrange("p (s c) -> p s c", c=65)[:, :, 0:1]
            self.CP("act", dst, stg[:, kc * NS:(kc + 1) * NS].rearrange("p (s o) -> p s o", o=1), [Rst], [self.R_ht[kc]])
        n = KC * (1 + NS)
        self.DMA("sp", self.d_oshift[:, j * n:(j + 1) * n], self.OSH[:, 0:n], [self.R("osh")], [self.R("d_oshift")], "stosh")

    def hlast(self, kc, src, g):
        c = self.cfg
        PT, NS, T = c.PT, c.NS, c.T
        o = kc * (1 + NS)
        self.STT("dve", self.OSH[:, o:o + 1], src[:, PT - 1:PT], g, self.rstd[:, PT - 1:PT], ALU.mult, ALU.mult,
                 [self.R_big[kc], self.R_stat], [self.R("osh")])
        s3 = src[:, PT:T].rearrange("p (s c) -> p s c", c=64)[:, :, 63:64]
        r3 = self.rstd[:, PT:T].rearrange("p (s c) -> p s c", c=64)[:, :, 63:64]
        self.STT("dve", self.OSH[:, o + 1:o + 1 + NS].rearrange("p (s o) -> p s o", o=1), s3, g, r3, ALU.mult, ALU.mult,
                 [self.R_big[kc], self.R_stat], [self.R("osh")])

    def build(self, sublayers):
        c = self.cfg
        KC, T = c.KC, c.T
        self.DMA("sp", self.PAR[:, :], self.d_par[:, :], [], [self.R_const], "cst")
        self.DMA("sp", self.CB[:, :], self.d_cb[:, :], [], [self.R_const], "cst")
        self.DMA("sp", self.SEL[:, :], self.d_sel[:, :], [], [self.R_const], "cst")
        for i_, v_ in enumerate((RMS_EPS, LN_EPS, GN_EPS, 1e-24)):
            self.add("dve", lambda e, i_=i_, v_=v_: e.memset(self.EPS[:, i_:i_ + 1], v_), [], [self.R_const])
        for kc in range(KC):
            self.DMA("sp", self.big3[:, kc, :], self.d_x[:, kc * T:(kc + 1) * T], [], [self.R_big[kc]], "xin")
        if sublayers is None:
            sublayers = []
            for i in range(c.depth):
                sublayers.append(("mix", i))
                sublayers.append(("ffn", i))
        for (kind, i) in sublayers:
            self.barrier()
            if kind == "mix":
                if i % 2 == 1:
                    self.rwkv_alloc()
                    self.norm_in(("ng", i, 0), hlast=self.hlast)
                    self.rwkv_pre(i // 2)
                    self.barrier()
                    self.rwkv(i // 2)
                else:
                    self.norm_in(("ng", i, 0))
                    self.conformer(i // 2)
                self.barrier()
                self.resid(("ng", i, 1))
            else:
                self.norm_in(("ng", i, 2))
                self.ffn(i)
                self.barrier()
                self.resid(("ng", i, 3))
        for kc in range(KC):
            self.DMA("sp", self.d_y[:, kc * T:(kc + 1) * T], self.big3[:, kc, :], [self.R_big[kc]], [self.R("d_y")], "yout")
        self.p.emit()
        self.stack.close()


def fm_tokens(x):
    x = np.asarray(x, np.float32)
    t, d = x.shape
    return np.ascontiguousarray(x.reshape(t, d // 128, 128).transpose(2, 1, 0))


def prep_shared(cfg, inp):
    c = cfg
    sh = {}
    sh["par"] = pack_params(cfg, inp)
    sh["cb"] = const_bf16(cfg)
    sh["w_pw1"] = np.concatenate([wl(inp["conv_pw1_w"][j]) for j in range(c.NCL)], 0)
    sh["w_pw2"] = np.concatenate([wl(inp["conv_pw2_w"][j]) for j in range(c.NCL)], 0)
    sh["w_in"] = np.concatenate([wl(inp["ffn_w_in"][i]) for i in range(c.depth)], 0)
    wo = []
    for i in range(c.depth):
        w = np.asarray(inp["ffn_w_out"][i], np.float32)
        wo.append(np.ascontiguousarray(w.reshape(c.FC // c.G, c.G, 128, c.D).transpose(0, 2, 1, 3)).reshape(c.FC // c.G, 128, c.G * c.D))
    sh["w_out"] = np.concatenate(wo, 0)
    return sh


def prep_core(cfg, inp, core):
    c = cfg
    m = {}
    xp = np.asarray(inp["x_prompt"], np.float32)[0, core * c.PT:(core + 1) * c.PT]
    xs = np.asarray(inp["x_sample"], np.float32)[core * c.NS:(core + 1) * c.NS].reshape(c.NS * c.SL, c.D)
    m["xT"] = fm_tokens(np.concatenate([xp, xs], 0)).reshape(128, -1)
    sel = np.zeros((128, 17), np.float32)
    if core > 0:
        sel[:, core - 1] = 1.0
        sel[:, 16] = 1.0
    sel[:, 8 + core] = 1.0
    m["sel"] = sel
    sq = slice(core * c.NS, (core + 1) * c.NS)
    sc = np.asarray(inp["state_conv_mix"], np.float32)[:, sq]
    m["st_conv"] = np.ascontiguousarray(sc.reshape(c.NCL, c.NS, 30, c.KC, 128).transpose(4, 0, 3, 1, 2)).reshape(128, -1)
    sf = np.asarray(inp["state_ffn_conv"], np.float32)[:, sq]
    m["st_ffn"] = np.ascontiguousarray(sf.reshape(c.depth, c.NS, 2, c.FC, 128).transpose(4, 0, 3, 1, 2)).reshape(128, -1)
    return m


def assemble(cfg, results):
    c = cfg
    KC, FC, T, PT, NS, SL, D = c.KC, c.FC, c.T, c.PT, c.NS, c.SL, c.D
    yp, ys = [], []
    for r in results:
        y = np.asarray(r["yT"]).reshape(128, KC, T).transpose(2, 1, 0).reshape(T, D)
        yp.append(y[:PT])
        ys.append(y[PT:].reshape(NS, SL, D))
    y_prompt = np.concatenate(yp, 0)[None]
    y_sample = np.concatenate(ys, 0)

    def st(name, nl, nch, w):
        arrs = [np.asarray(r[name]).reshape(128, nl, nch, 1 + NS, w) for r in results]
        p = arrs[-1][:, :, :, 0, :].transpose(1, 3, 2, 0).reshape(nl, 1, w, nch * 128)
        s = np.concatenate([a[:, :, :, 1:, :].transpose(1, 3, 4, 2, 0).reshape(nl, NS, w, nch * 128) for a in arrs], 1)
        return np.ascontiguousarray(p), np.ascontiguousarray(s)
    conv_p, conv_s = st("o_conv", c.NCL, KC, 30)
    ffn_p, ffn_s = st("o_ffn", c.depth, FC, 2)
    outs = [y_prompt, y_sample, conv_p, conv_s, None, None, None, None, ffn_p, ffn_s]
    if c.NRL > 0 and "o_shift" in results[0]:
        outs[4:8] = assemble_rwkv(cfg, results)
    return tuple(outs)


def prep_shared_rwkv(cfg, inp):
    c = cfg
    if c.NRL == 0:
        return {}
    sh = {}
    rk = []
    for j in range(c.NRL):
        for nm in ("rwkv_w_r", "rwkv_w_k", "rwkv_w_v", "rwkv_w_o"):
            rk.append(wl(inp[nm][j]))
    sh["w_rkvo"] = np.concatenate(rk, 0)
    W = c.KC * 128
    l1 = np.zeros((c.NRL * 5, 128, W), np.float32)
    l2 = np.zeros((c.NRL * c.HP, 128, 5 * 128), np.float32)
    for j in range(c.NRL):
        l1[j * 5 + 0, :, :c.KC * 96] = wl(inp["rwkv_w1"][j], 96)[0]
        l1[j * 5 + 1, :, :c.KC * 96] = wl(inp["rwkv_a1"][j], 96)[0]
        g1 = wl(inp["rwkv_g1"][j], 128)
        l1[j * 5 + 2] = g1[0]
        l1[j * 5 + 3] = g1[1]
        if j > 0:
            l1[j * 5 + 4, :, :c.KC * 64] = wl(inp["rwkv_v1"][j - 1], 64)[0]
        w2 = np.asarray(inp["rwkv_w2"][j], np.float32)
        a2 = np.asarray(inp["rwkv_a2"][j], np.float32)
        g2 = np.asarray(inp["rwkv_g2"][j], np.float32)
        for m in range(c.HP):
            cs = slice(m * 128, (m + 1) * 128)
            l2[j * c.HP + m, 0:96, 0:128] = w2[:, cs]
            l2[j * c.HP + m, 0:96, 128:256] = a2[:, cs]
            if j > 0:
                l2[j * c.HP + m, 0:64, 256:384] = np.asarray(inp["rwkv_v2"][j - 1], np.float32)[:, cs]
            l2[j * c.HP + m, :, 384:512] = g2[0:128, cs]
            l2[j * c.HP + m, :, 512:640] = g2[128:256, cs]
    sh["w_l1"] = l1
    sh["w_l2"] = l2
    return sh


def prep_core_rwkv(cfg, inp, core):
    c = cfg
    if c.NRL == 0:
        return {}
    m = {}
    sq = slice(core * c.NS, (core + 1) * c.NS)
    ss = np.asarray(inp["state_rwkv_shift"], np.float32)[:, sq]
    m["st_shift"] = np.ascontiguousarray(ss.reshape(c.NRL, c.NS, c.KC, 128).transpose(3, 0, 2, 1)).reshape(128, -1)
    sw = np.asarray(inp["state_rwkv_wkv"], np.float32)[:, sq]
    sw = sw.reshape(c.NRL, c.NS, c.HP, 2, 64, 64)
    m["st_wkv"] = np.ascontiguousarray(sw.transpose(3, 5, 0, 2, 1, 4)).reshape(128, -1)
    return m


def assemble_rwkv(cfg, results):
    c = cfg
    KC, HP, NS, D = c.KC, c.HP, c.NS, c.D
    sh = [np.asarray(r["o_shift"]).reshape(128, c.NRL, KC, 1 + NS) for r in results]
    shift_p = sh[-1][:, :, :, 0].transpose(1, 2, 0).reshape(c.NRL, 1, D)
    shift_s = np.concatenate([a[:, :, :, 1:].transpose(1, 3, 2, 0).reshape(c.NRL, NS, D) for a in sh], 1)
    wk = [np.asarray(r["o_wkv"]).reshape(2, 64, c.NRL, HP, 1 + NS, 64) for r in results]
    tr = lambda a: a.transpose(2, 4, 3, 0, 5, 1).reshape(c.NRL, a.shape[4], HP * 2, 64, 64)
    wkv_p = tr(wk[-1][:, :, :, :, 0:1, :])
    wkv_s = np.concatenate([tr(a[:, :, :, :, 1:, :]) for a in wk], 1)
    return [np.ascontiguousarray(x) for x in (shift_p, shift_s, wkv_p, wkv_s)]


_CACHE = {}


def kernel(**inputs):
    cfg = Cfg()
    if "b" not in _CACHE:
        _CACHE["b"] = B(cfg)
    b = _CACHE["b"]
    inp = {k: np.asarray(v) for k, v in inputs.items()}
    sh = prep_shared(cfg, inp)
    sh.update(prep_shared_rwkv(cfg, inp))
    maps = []
    for c in range(NCORES):
        m = dict(sh)
        m.update(prep_core(cfg, inp, c))
        m.update(prep_core_rwkv(cfg, inp, c))
        maps.append(m)
    res = run_bass_kernel_spmd(b.nc, maps, core_ids=list(range(NCORES)))
    outs = assemble(cfg, [r for r in res.results])
    return tuple(np.ascontiguousarray(o, dtype=np.float32) for o in outs)
```
